# Optimizing a Trainium2 kernel written in Bass

```python
import jax
import jax.numpy as jnp
from jax import lax
import numpy as np

D_MODEL = 2048
BATCH = 4
SEQ = 4096
DEPTH = 2

GRID_W = 64
CTX_LEN = 256
MIX_WIDTH = D_MODEL

RET_WIDTH = D_MODEL // 4
RET_HEAD_DIM = 128
RET_HEADS = RET_WIDTH // RET_HEAD_DIM
RET_CHUNK = 128
RET_GN_EPS = 1e-5

MLA_WIDTH = D_MODEL // 2
MLA_V = 128
MLA_HEADS = MLA_WIDTH // MLA_V
MLA_NOPE = 128
MLA_ROPE = 64
Q_LORA = D_MODEL // 4
KV_LORA = D_MODEL // 8
ATTN_BLOCK = 128
ROPE_BASE = 10000.0
MLA_SCALE = (MLA_NOPE + MLA_ROPE) ** -0.5

RWKV_WIDTH = D_MODEL // 4
RWKV_HEAD = 64
RWKV_HEADS = RWKV_WIDTH // RWKV_HEAD
RWKV_LORA = 64
RWKV_FEAT = 3 * RWKV_WIDTH + 4 * RWKV_LORA
RWKV_GN_EPS = 64e-5

IN_WIDTHS = (RET_WIDTH, RET_WIDTH, RET_WIDTH, RET_WIDTH,
             Q_LORA, KV_LORA, MLA_ROPE, MLA_WIDTH,
             RWKV_FEAT, RWKV_WIDTH)
N_IN = sum(IN_WIDTHS)

ALPHA = (2 * DEPTH) ** 0.25
OUT_INIT = (8 * DEPTH) ** -0.25
LN_EPS = 1e-5
RMS_EPS = 1e-6
F32 = jnp.float32

kernel_name = 'hybrid_ret_mla_rwkv7_diffusion_block'


def split_cols(p, widths):
    out, start = [], 0
    for w in widths:
        out.append(p[..., start:start + w])
        start += w
    return out


def layer_norm(x, g, b, eps=LN_EPS):
    xf = x.astype(F32)
    mu = jnp.mean(xf, -1, keepdims=True)
    var = jnp.mean(jnp.square(xf - mu), -1, keepdims=True)
    return ((xf - mu) * lax.rsqrt(var + eps)).astype(x.dtype) * g + b


def rms_norm(x, g, eps=RMS_EPS):
    xf = x.astype(F32)
    return (xf * lax.rsqrt(jnp.mean(jnp.square(xf), -1, keepdims=True) + eps)).astype(x.dtype) * g


def head_group_norm(y, g, n_heads, eps):
    shp = y.shape
    yh = y.reshape(shp[:-1] + (n_heads, shp[-1] // n_heads)).astype(F32)
    mu = jnp.mean(yh, -1, keepdims=True)
    var = jnp.mean(jnp.square(yh - mu), -1, keepdims=True)
    return ((yh - mu) * lax.rsqrt(var + eps)).reshape(shp).astype(y.dtype) * g


def axial_rope(x, row, col):
    nf = MLA_ROPE // 4
    inv = ROPE_BASE ** (-jnp.arange(nf, dtype=F32) / nf)
    extra = (1,) * (x.ndim - 3)

    def rot(xh, pos):
        ang = pos.astype(F32)[:, None] * inv[None, :]
        ang = ang.reshape((ang.shape[0],) + extra + (nf,))
        cos, sin = jnp.cos(ang).astype(x.dtype), jnp.sin(ang).astype(x.dtype)
        x1, x2 = xh[..., :nf], xh[..., nf:]
        return jnp.concatenate([x1 * cos - x2 * sin, x2 * cos + x1 * sin], -1)

    half = MLA_ROPE // 2
    return jnp.concatenate([rot(x[..., :half], row), rot(x[..., half:], col)], -1)


def bidir_concat(a_ctx, a_lat):
    fwd = jnp.concatenate([a_ctx, a_lat], axis=1)
    bwd = jnp.concatenate([jnp.flip(a_ctx, 1), jnp.flip(a_lat, 1)], axis=1)
    return jnp.stack([fwd, bwd], 0)


def bidir_merge(y, n_ctx):
    y_ctx = y[0, :, :n_ctx] + jnp.flip(y[1, :, :n_ctx], 1)
    y_lat = y[0, :, n_ctx:] + jnp.flip(y[1, :, n_ctx:], 1)
    return y_ctx, y_lat


def centred_shift(p, mu):
    prev = jnp.pad(p[:, :-1], ((0, 0), (1, 0), (0, 0)))
    nxt = jnp.pad(p[:, 1:], ((0, 0), (0, 1), (0, 0)))
    return p + mu[0] * (prev - p) + mu[1] * (nxt - p)


def retention_bidir(q, k, v, decay_logit):
    out_dtype = q.dtype
    q, k, v = (t.astype(F32) for t in (q, k, v))
    n_dir, bsz, length, n_h, dh = q.shape
    C = RET_CHUNK
    nc = length // C
    k = k * dh ** -0.5
    q, k, v = (t.reshape(n_dir, bsz, nc, C, n_h, dh) for t in (q, k, v))
    log_g = jax.nn.log_sigmoid(decay_logit.astype(F32))
    i = jnp.arange(C, dtype=F32)
    rel = i[:, None] - i[None, :]
    dmask = jnp.where(rel >= 0, jnp.exp(jnp.maximum(rel, 0.0)[None, None] * log_g[:, :, None, None]), 0.0)
    s = jnp.einsum('dbnihe,dbnjhe->dbnhij', q, k) * dmask[:, None, None]
    intra = jnp.einsum('dbnhij,dbnjhe->dbnihe', s, v)
    zeta = jnp.exp((C - 1 - i)[None, :] * log_g[:, :, None])
    xi = jnp.exp((i + 1)[None, :] * log_g[:, :, None])
    u = jnp.einsum('dbnjhk,dhj,dbnjhv->ndbhkv', k, zeta, v)
    g_chunk = jnp.exp(C * log_g)[:, None, :, None, None]

    def step(state, u_n):
        return state * g_chunk + u_n, state

    _, s_prev = lax.scan(step, jnp.zeros(u.shape[1:], F32), u)
    cross = jnp.einsum('dbnihk,dhi,ndbhkv->dbnihv', q, xi, s_prev)
    return (intra + cross).reshape(n_dir, bsz, length, n_h * dh).astype(out_dtype)


def mla_attend(q_nope, q_rope, k_nope, k_rope, v):
    s = jnp.einsum('bqhd,bkhd->bhqk', q_nope, k_nope) + jnp.einsum('bqhr,bkr->bhqk', q_rope, k_rope)
    p = jax.nn.softmax(s.astype(F32) * MLA_SCALE, axis=-1).astype(v.dtype)
    return jnp.einsum('bhqk,bkhd->bqhd', p, v)


def rwkv7_bidir(r, k, v, w_lora, a_lora, w0, w2, a0, a2, k_k, k_a):
    out_dtype = r.dtype
    r, k, v, w_lora, a_lora = (t.astype(F32) for t in (r, k, v, w_lora, a_lora))
    n_dir, bsz, length, width = r.shape
    w_in_d = jnp.stack([w_lora[0, ..., :RWKV_LORA], w_lora[1, ..., RWKV_LORA:]])
    a_in_d = jnp.stack([a_lora[0, ..., :RWKV_LORA], a_lora[1, ..., RWKV_LORA:]])
    w_log = -jax.nn.softplus(-(w0[:, None, None] + jnp.einsum('dblr,drc->dblc', jnp.tanh(w_in_d), w2))) - 0.5
    decay = jnp.exp(-jnp.exp(w_log))
    a_rate = jax.nn.sigmoid(a0[:, None, None] + jnp.einsum('dblr,drc->dblc', a_in_d, a2))
    heads = lambda t: t.reshape(n_dir, bsz, length, RWKV_HEADS, RWKV_HEAD)
    kk = heads(k * k_k)
    kk = kk / jnp.maximum(jnp.sqrt(jnp.sum(jnp.square(kk), -1, keepdims=True)), 1e-12)
    k_eff = k * (1.0 + (a_rate - 1.0) * k_a)
    xs = tuple(jnp.moveaxis(t, 2, 0) for t in
               (heads(r), heads(decay), heads(k_eff), heads(v), -kk, kk * heads(a_rate)))

    def step(state, inp):
        r_t, w_t, k_t, v_t, a_t, b_t = inp
        sa = jnp.einsum('dbhvk,dbhk->dbhv', state, a_t)
        state = state * w_t[..., None, :] + sa[..., :, None] * b_t[..., None, :] + v_t[..., :, None] * k_t[..., None, :]
        return state, jnp.einsum('dbhvk,dbhk->dbhv', state, r_t)

    state0 = jnp.zeros((n_dir, bsz, RWKV_HEADS, RWKV_HEAD, RWKV_HEAD), F32)
    _, y = lax.scan(step, state0, xs)
    return jnp.moveaxis(y, 0, 2).reshape(n_dir, bsz, length, width).astype(out_dtype)


def mix_layer(x_lat, x_ctx, c, c_ctx, row, col, need_ctx, w_ada, b_ada, w_in, ret_decay_logit, ret_gn_g,
              mla_q_norm_g, mla_w_uq, mla_kv_norm_g, mla_w_ukv, rwkv_shift_mu, rwkv_w0, rwkv_w2, rwkv_a0,
              rwkv_a2, rwkv_k_k, rwkv_k_a, rwkv_r_k, rwkv_gn_g, w_out, ln_g, ln_b):
    bsz, n_lat = x_lat.shape[:2]
    n_ctx = x_ctx.shape[1]
    shift, scale, gate = jnp.split(jax.nn.silu(c) @ w_ada + b_ada, 3, axis=-1)
    shift_c, scale_c, gate_c = jnp.split(jax.nn.silu(c_ctx) @ w_ada + b_ada, 3, axis=-1)
    h_lat = x_lat * (1.0 + scale[:, None]) + shift[:, None]
    h_ctx = x_ctx * (1.0 + scale_c) + shift_c
    p_lat = split_cols(h_lat @ w_in, IN_WIDTHS)
    p_ctx = split_cols(h_ctx @ w_in, IN_WIDTHS)

    ret_seq = [bidir_concat(p_ctx[j], p_lat[j]).reshape(2, bsz, n_ctx + n_lat, RET_HEADS, RET_HEAD_DIM)
               for j in range(3)]
    ret_c, ret_l = bidir_merge(retention_bidir(ret_seq[0], ret_seq[1], ret_seq[2], ret_decay_logit), n_ctx)

    def ret_out(y, g):
        return head_group_norm(y, ret_gn_g, RET_HEADS, RET_GN_EPS) * jax.nn.silu(g)

    def q_heads(cq):
        q = (rms_norm(cq, mla_q_norm_g) @ mla_w_uq).reshape(cq.shape[:2] + (MLA_HEADS, MLA_NOPE + MLA_ROPE))
        return q[..., :MLA_NOPE], q[..., MLA_NOPE:]

    def kv_heads(ckv):
        kv = (rms_norm(ckv, mla_kv_norm_g) @ mla_w_ukv).reshape(ckv.shape[:2] + (MLA_HEADS, MLA_NOPE + MLA_V))
        return kv[..., :MLA_NOPE], kv[..., MLA_NOPE:]

    kn_c, v_c = kv_heads(p_ctx[5])
    kr_c = p_ctx[6]
    kn_l, v_l = kv_heads(p_lat[5])
    kr_l = axial_rope(p_lat[6], row, col)
    kn_all = jnp.concatenate([kn_c, kn_l], 1)
    kr_all = jnp.concatenate([kr_c, kr_l], 1)
    v_all = jnp.concatenate([v_c, v_l], 1)
    qn_l, qr_l = q_heads(p_lat[4])
    qr_l = axial_rope(qr_l, row, col)
    n_blk = n_lat // ATTN_BLOCK

    def to_blocks(t):
        return jnp.moveaxis(t.reshape((bsz, n_blk, ATTN_BLOCK) + t.shape[2:]), 1, 0)

    mla_l = lax.map(lambda qb: mla_attend(qb[0], qb[1], kn_all, kr_all, v_all), (to_blocks(qn_l), to_blocks(qr_l)))
    mla_l = jnp.moveaxis(mla_l, 0, 1).reshape(bsz, n_lat, MLA_WIDTH)

    feat_c = centred_shift(p_ctx[8], rwkv_shift_mu)
    feat_l = centred_shift(p_lat[8], rwkv_shift_mu)
    rw_seq = split_cols(bidir_concat(feat_c, feat_l), (RWKV_WIDTH,) * 3 + (2 * RWKV_LORA,) * 2)
    rw_c, rw_l = bidir_merge(rwkv7_bidir(rw_seq[0], rw_seq[1], rw_seq[2], rw_seq[3], rw_seq[4], rwkv_w0, rwkv_w2,
                                         rwkv_a0, rwkv_a2, rwkv_k_k, rwkv_k_a), n_ctx)

    def rwkv_out(y, feat, g):
        r_, k_, v_ = split_cols(feat, (RWKV_WIDTH,) * 3)
        shp = r_.shape[:-1] + (RWKV_HEADS, RWKV_HEAD)
        bonus = jnp.sum((r_ * k_).reshape(shp) * rwkv_r_k, -1, keepdims=True) * v_.reshape(shp)
        return (head_group_norm(y, rwkv_gn_g, RWKV_HEADS, RWKV_GN_EPS) + bonus.reshape(r_.shape)) * jax.nn.silu(g)

    def merge(ret_y, mla_y, rw_y, p, feat):
        return jnp.concatenate([ret_out(ret_y, p[3]), mla_y * jax.nn.silu(p[7]), rwkv_out(rw_y, feat, p[9])], -1) @ w_out

    x_lat_new = layer_norm(ALPHA * x_lat + gate[:, None] * merge(ret_l, mla_l, rw_l, p_lat, feat_l), ln_g, ln_b)
    if not need_ctx:
        return x_lat_new, x_ctx
    qn_c, qr_c = q_heads(p_ctx[4])
    mla_c = mla_attend(qn_c, qr_c, kn_c, kr_c, v_c).reshape(bsz, n_ctx, MLA_WIDTH)
    x_ctx_new = layer_norm(ALPHA * x_ctx + gate_c * merge(ret_c, mla_c, rw_c, p_ctx, feat_c), ln_g, ln_b)
    return x_lat_new, x_ctx_new


def setup_inputs(seed: int = 0) -> dict:
    key = jax.random.key(seed)
    ks = jax.random.split(key, 25)
    nrm = lambda k, shape: jax.random.normal(k, shape, F32)
    L = DEPTH
    gam = 1.0 - 2.0 ** (-5.0 - np.arange(RET_HEADS))
    ret_logit0 = jnp.asarray(np.log(gam / (1.0 - gam)), F32)
    return {
        'x': nrm(ks[0], (BATCH, SEQ, D_MODEL)),
        'c': nrm(ks[1], (BATCH, D_MODEL)),
        'ctx': nrm(ks[2], (BATCH, CTX_LEN, D_MODEL)),
        'c_ctx': nrm(ks[3], (D_MODEL,)),
        'w_ada': nrm(ks[4], (L, D_MODEL, 3 * D_MODEL)) * D_MODEL ** -0.5,
        'b_ada': 0.02 * nrm(ks[5], (L, 3 * D_MODEL)),
        'w_in': nrm(ks[6], (L, D_MODEL, N_IN)) * D_MODEL ** -0.5,
        'ret_decay_logit': ret_logit0 + 0.1 * nrm(ks[7], (L, 2, RET_HEADS)),
        'ret_gn_g': 1.0 + 0.02 * nrm(ks[8], (L, RET_WIDTH)),
        'mla_q_norm_g': 1.0 + 0.02 * nrm(ks[9], (L, Q_LORA)),
        'mla_w_uq': nrm(ks[10], (L, Q_LORA, MLA_HEADS * (MLA_NOPE + MLA_ROPE))) * Q_LORA ** -0.5,
        'mla_kv_norm_g': 1.0 + 0.02 * nrm(ks[11], (L, KV_LORA)),
        'mla_w_ukv': nrm(ks[12], (L, KV_LORA, MLA_HEADS * (MLA_NOPE + MLA_V))) * KV_LORA ** -0.5,
        'rwkv_shift_mu': jax.random.uniform(ks[13], (L, 2, RWKV_FEAT), F32, 0.0, 0.5),
        'rwkv_w0': jnp.linspace(-6.0, -1.0, RWKV_WIDTH, dtype=F32) + 0.1 * nrm(ks[14], (L, 2, RWKV_WIDTH)),
        'rwkv_w2': 0.5 * nrm(ks[15], (L, 2, RWKV_LORA, RWKV_WIDTH)) * RWKV_LORA ** -0.5,
        'rwkv_a0': 0.1 * nrm(ks[16], (L, 2, RWKV_WIDTH)),
        'rwkv_a2': 0.5 * nrm(ks[17], (L, 2, RWKV_LORA, RWKV_WIDTH)) * RWKV_LORA ** -0.5,
        'rwkv_k_k': 0.85 + 0.05 * nrm(ks[18], (L, RWKV_WIDTH)),
        'rwkv_k_a': 1.0 + 0.05 * nrm(ks[19], (L, RWKV_WIDTH)),
        'rwkv_r_k': 0.1 * nrm(ks[20], (L, RWKV_HEADS, RWKV_HEAD)),
        'rwkv_gn_g': 1.0 + 0.02 * nrm(ks[21], (L, RWKV_WIDTH)),
        'w_out': nrm(ks[22], (L, MIX_WIDTH, D_MODEL)) * MIX_WIDTH ** -0.5 * OUT_INIT,
        'ln_g': 1.0 + 0.02 * nrm(ks[23], (L, D_MODEL)),
        'ln_b': 0.02 * nrm(ks[24], (L, D_MODEL)),
    }


def reference(x, c, ctx, c_ctx, w_ada, b_ada, w_in, ret_decay_logit, ret_gn_g, mla_q_norm_g, mla_w_uq,
              mla_kv_norm_g, mla_w_ukv, rwkv_shift_mu, rwkv_w0, rwkv_w2, rwkv_a0, rwkv_a2, rwkv_k_k, rwkv_k_a,
              rwkv_r_k, rwkv_gn_g, w_out, ln_g, ln_b):
    n_lat = x.shape[1]
    rows = n_lat // GRID_W
    row = jnp.repeat(jnp.arange(rows, dtype=jnp.int32), GRID_W, total_repeat_length=rows * GRID_W)
    col = jnp.arange(rows * GRID_W, dtype=jnp.int32) % GRID_W
    x_lat, x_ctx = x, ctx
    for l in range(DEPTH):
        x_lat, x_ctx = mix_layer(
            x_lat, x_ctx, c, c_ctx, row, col, l < DEPTH - 1,
            w_ada[l], b_ada[l], w_in[l], ret_decay_logit[l], ret_gn_g[l], mla_q_norm_g[l], mla_w_uq[l],
            mla_kv_norm_g[l], mla_w_ukv[l], rwkv_shift_mu[l], rwkv_w0[l], rwkv_w2[l], rwkv_a0[l], rwkv_a2[l],
            rwkv_k_k[l], rwkv_k_a[l], rwkv_r_k[l], rwkv_gn_g[l], w_out[l], ln_g[l], ln_b[l])
    return x_lat
```

```python
import numpy as np
import ml_dtypes
from contextlib import ExitStack
import concourse.bass as bass
import concourse.mybir as mybir
from concourse.bass_utils import run_bass_kernel_spmd

F32 = mybir.dt.float32
BF16 = mybir.dt.bfloat16
AF = mybir.ActivationFunctionType
ALU = mybir.AluOpType

D = 2048
NCTX = 256
NLAT = 4096
T = NCTX + NLAT
NCH = T // 128
NCOLS = 6400
NCC = NCOLS // 128
ALPHA = 4 ** 0.25
MLA_SCALE = 192 ** -0.5
TBLK = [(0, 256)] + [(256 + i * 512, 512) for i in range(8)]


class Res:
    __slots__ = ("lw", "rd", "dsem", "dval", "name", "psum")

    def __init__(self, name="", psum=False):
        self.psum = psum
        self.lw = None
        self.rd = {}
        self.dsem = None
        self.dval = 0
        self.name = name


class Tile:
    def __init__(self, t, name):
        self.t = t
        self.r = Res(name)

    def __getitem__(self, idx):
        return self.t[idx]


class KB:
    ENG = ("pe", "act", "dve", "pool", "sp")

    def __init__(self, nc):
        self.nc = nc
        self.eng = dict(pe=nc.tensor, act=nc.scalar, dve=nc.vector, pool=nc.gpsimd, sp=nc.sync)
        self.root = ExitStack()
        self.stacks = [self.root]
        self.sem = {e: self.root.enter_context(nc.semaphore("s_" + e)) for e in self.ENG}
        self.cnt = {e: 0 for e in self.ENG}
        self.waited = {e: {} for e in self.ENG}
        self.dma_owners = [[]]
        self.nid = 0
        self.sem_pool = []
        self.okey = 0

    def push(self):
        es = ExitStack()
        self.stacks.append(es)
        self.dma_owners.append([])

    def pop(self):
        self.barrier()
        self.dma_owners.pop()
        self.stacks.pop().close()

    def _name(self, n):
        self.nid += 1
        return f"{n}_{self.nid}"

    def sb(self, name, shape, dtype):
        t = self.stacks[-1].enter_context(self.nc.sbuf_tensor(self._name(name), list(shape), dtype))
        return Tile(t, name)

    def ps(self, name, shape=(128, 512), dtype=F32):
        t = self.stacks[-1].enter_context(self.nc.psum_tensor(self._name(name), list(shape), dtype))
        tl = Tile(t, name)
        tl.r.psum = True
        return tl

    def _wait(self, e, ev):
        if ev is None:
            return
        key, sem, val = ev
        if key == "pe" and e == "pe":
            return
        if self.waited[e].get(key, 0) >= val:
            return
        self.eng[e].wait_ge(sem, val)
        self.waited[e][key] = val

    def _deps(self, e, reads, writes):
        for r in reads:
            self._wait(e, r.lw)
            if r.psum:
                for key, ev in r.rd.items():
                    if key != e:
                        self._wait(e, ev)
        for r in writes:
            self._wait(e, r.lw)
            for ev in r.rd.values():
                self._wait(e, ev)

    def _mark(self, ev, reads, writes):
        for r in reads:
            r.rd[ev[0]] = ev
        for r in writes:
            r.lw = ev
            r.rd = {}

    def op(self, e, fn, reads=(), writes=()):
        reads = [x.r if isinstance(x, Tile) else x for x in reads]
        writes = [x.r if isinstance(x, Tile) else x for x in writes]
        self._deps(e, reads, writes)
        inst = fn(self.eng[e])
        self.cnt[e] += 1
        inst.then_inc(self.sem[e], 1)
        self._mark((e, self.sem[e], self.cnt[e]), reads, writes)

    def dma(self, q, out, in_, reads=(), writes=(), **kw):
        reads = [x.r if isinstance(x, Tile) else x for x in reads]
        writes = [x.r if isinstance(x, Tile) else x for x in writes]
        owner = writes[0] if writes else reads[0]
        self._deps(q, reads, writes)
        if owner.dsem is None:
            if self.sem_pool:
                owner.dsem, owner.dval = self.sem_pool.pop()
            else:
                owner.dsem = self.root.enter_context(self.nc.semaphore(self._name("d")))
                owner.dval = 0
            self.okey += 1
            owner.name = ("dma", self.okey)
            self.dma_owners[-1].append(owner)
        inst = self.eng[q].dma_start(out=out, in_=in_, **kw)
        owner.dval += 16
        inst.then_inc(owner.dsem, 16)
        self._mark((owner.name, owner.dsem, owner.dval), reads, writes)

    def barrier(self):
        evs = [(e, self.sem[e], self.cnt[e]) for e in self.ENG if self.cnt[e] > 0]
        for lst in self.dma_owners:
            for o in lst:
                evs.append((o.name, o.dsem, o.dval))
        for e in self.ENG:
            for ev in evs:
                if ev[0] == e:
                    continue
                if self.waited[e].get(ev[0], 0) >= ev[2]:
                    continue
                self.eng[e].wait_ge(ev[1], ev[2])
                self.waited[e][ev[0]] = ev[2]
        for o in self.dma_owners[-1]:
            for e in self.ENG:
                self.waited[e].pop(o.name, None)
            self.sem_pool.append((o.dsem, o.dval))
            o.dsem = None

    def mm(self, out, lhsT, rhs, start, stop, reads, writes):
        self.op("pe", lambda g: g.matmul(out, lhsT=lhsT, rhs=rhs, start=start, stop=stop), reads, writes)

    def tr(self, out, in_, ident, reads, writes):
        self.op("pe", lambda g: g.transpose(out, in_, ident), reads, writes)

    def act(self, out, in_, func, reads, writes, scale=1.0, bias=0.0, e="act"):
        self.op(e, lambda g: g.activation(out=out, in_=in_, func=func, bias=bias, scale=scale), reads, writes)

    def ts(self, e, out, in0, s1, s2, op0, op1, reads, writes):
        self.op(e, lambda g: g.tensor_scalar(out=out, in0=in0, scalar1=s1, scalar2=s2, op0=op0, op1=op1), reads, writes)

    def tt(self, e, out, in0, in1, op, reads, writes):
        self.op(e, lambda g: g.tensor_tensor(out=out, in0=in0, in1=in1, op=op), reads, writes)

    def stt(self, out, in0, scalar, in1, op0, op1, reads, writes):
        self.op("dve", lambda g: g.scalar_tensor_tensor(out=out, in0=in0, scalar=scalar, in1=in1, op0=op0, op1=op1),
                reads, writes)

    def cp(self, e, out, in_, reads, writes):
        if e == "act":
            self.op(e, lambda g: g.copy(out=out, in_=in_), reads, writes)
        else:
            self.op(e, lambda g: g.tensor_copy(out=out, in_=in_), reads, writes)


PP = {}
_o = 0
for _n, _w in [("gq", 4), ("gkv", 2), ("retg", 4), ("lng", 16), ("lnb", 16), ("mu0", 14), ("mu1", 14),
               ("w0", 8), ("a0", 8), ("kk", 4), ("ka", 4), ("rk", 4), ("rwg", 4)]:
    PP[_n] = (_o, _w)
    _o += _w
NPP = _o

CS = {}
_o = 0
for _n, _w in [("ident", 128), ("ones", 128), ("onesm", 128), ("relf", 128), ("relb", 128), ("mf", 128), ("mb", 128),
               ("ip1", 128), ("cmi", 128), ("blk64", 128), ("blk64m", 128), ("ones2k", 128),
               ("sf", 128), ("mf2", 128), ("sb", 128), ("mb2", 128), ("rm", 512), ("LM", 7 * 128), ("LMT", 7 * 128),
               ("colj", 1), ("colcj", 1)]:
    CS[_n] = (_o, _w)
    _o += _w
NCS = _o


def build_program(n_layers=2, dbg=()):
    nc = bass.Bass("TRN2", target_bir_lowering=False)
    k = KB(nc)

    def din(name, shape, dt=F32):
        return nc.dram_tensor(name, list(shape), dt, kind="ExternalInput").ap()

    xT = din("xT", [D, T])
    cc = din("cc", [128, 32])
    wada = din("wada", [2, D, 6144])
    bada = din("bada", [128, 96])
    win = din("win", [2, D, NCOLS])
    wuq = din("wuq", [2, 512, 2048])
    wukv = din("wukv", [2, 256, 2048])
    wout = din("wout", [2, D, D])
    ppd = din("pp", [128, 2 * NPP])
    rdl = din("rdl", [128, 16])
    w2p = din("w2p", [2, 2, 128, 512])
    a2p = din("a2p", [2, 2, 128, 512])
    tabQ = din("tabQ", [128, T])
    tabKc = din("tabKc", [128, T])
    tabKs = din("tabKs", [128, T])
    cstd = din("cst", [128, NCS])
    yT = nc.dram_tensor("yT", [D, NLAT], F32, kind="ExternalOutput").ap()
    PT = nc.dram_tensor("PT", [NCOLS, T], F32, kind="Internal").ap()
    MIXT = nc.dram_tensor("MIXT", [D, T], BF16, kind="Internal").ap()
    XN = nc.dram_tensor("XN", [D, T], F32, kind="Internal").ap()
    dbg_out = {}
    if "mod" in dbg:
        dbg_out["mod"] = nc.dram_tensor("dbg_mod", [128, 192], F32, kind="ExternalOutput").ap()
    if "PT" in dbg:
        dbg_out["PT"] = nc.dram_tensor("dbg_PT", [NCOLS, T], F32, kind="ExternalOutput").ap()
        PT = dbg_out["PT"]
    if "PTin" in dbg:
        PT = nc.dram_tensor("PTin", [NCOLS, T], F32, kind="ExternalInput").ap()
    if "MIXT" in dbg:
        dbg_out["MIXT"] = nc.dram_tensor("dbg_MIXT", [D, T], F32, kind="ExternalOutput").ap()
        MIXT = dbg_out["MIXT"]
    if "XN" in dbg:
        dbg_out["XN"] = nc.dram_tensor("dbg_XN", [D, T], F32, kind="ExternalOutput").ap()
        XN = dbg_out["XN"]

    mod = k.sb("mod", [128, 192], F32)
    pp = k.sb("pp", [128, 2 * NPP], F32)
    cst = k.sb("cst", [128, NCS], F32)
    k.dma("sp", pp[:], ppd, writes=[pp])
    k.dma("sp", cst[:], cstd, writes=[cst])

    def C(name):
        o, w = CS[name]
        return cst[:, o:o + w]

    def P(l, name, j=0):
        o, w = PP[name]
        return pp[:, l * NPP + o + j: l * NPP + o + j + 1]

    def modc(l, j, r):
        return mod[:, l * 96 + j * 2 + r: l * 96 + j * 2 + r + 1]

    k.push()
    if "PTin" in dbg:
        n_layers_mod = 0
    else:
        n_layers_mod = n_layers
    cct = k.sb("cct", [128, 32], F32)
    sc = k.sb("sc", [128, 32], F32)
    bad = k.sb("bad", [128, 96], F32)
    k.dma("sp", cct[:], cc, writes=[cct])
    k.dma("sp", bad[:], bada, writes=[bad])
    k.act(sc[:], cct[:], AF.Silu, [cct], [sc])
    wa = [k.sb("wa", [128, 16, 512], F32) for _ in range(2)]
    pm = [k.ps("pm") for _ in range(2)]
    it = 0
    for l in range(n_layers_mod):
        for mc in range(12):
            w = wa[it % 2]
            src = wada[l, :, mc * 512:(mc + 1) * 512].rearrange("(kc p) c -> p kc c", p=128)
            k.dma("sp", w[:, 0:8, :], src[:, 0:8, :], writes=[w])
            k.dma("sp", w[:, 8:16, :], src[:, 8:16, :], writes=[w])
            for j in range(4):
                p_ = pm[(it * 4 + j) % 2]
                for kc in range(16):
                    k.mm(p_[:, 0:2], w[:, kc, j * 128:(j + 1) * 128], sc[:, kc * 2:kc * 2 + 2], kc == 0, kc == 15,
                         [w, sc], [p_])
                jj = mc * 4 + j
                k.ts("dve", mod[:, l * 96 + jj * 2: l * 96 + jj * 2 + 2], p_[:, 0:2],
                     bad[:, l * 48 + jj: l * 48 + jj + 1], None, ALU.add, ALU.bypass, [p_, bad], [mod])
            it += 1
        k.ts("dve", mod[:, l * 96 + 32: l * 96 + 64], mod[:, l * 96 + 32: l * 96 + 64], 1.0, None, ALU.add, ALU.bypass,
             [mod], [mod])
    if "mod" in dbg:
        k.dma("sp", dbg_out["mod"], mod[:], reads=[mod])
    k.pop()

    for l in range(n_layers):
        xsrc = xT if l == 0 else XN
        last = (l == n_layers - 1)
        if "PTin" not in dbg:
            phase_inproj(k, l, xsrc, win, PT, modc)
        if "PT" in dbg:
            break
        env = dict(C=C, P=P, cst=cst, pp=pp, mod=mod, modc=modc)
        if "nomla" not in dbg:
            phase_mla(k, l, PT, MIXT, wuq, wukv, tabQ, tabKc, tabKs, env, not last)
        if "noret" not in dbg:
            phase_ret(k, l, PT, MIXT, rdl, env, not last)
        if "norw" not in dbg:
            phase_rwkv(k, l, PT, MIXT, w2p, a2p, env, not last)
        if "MIXT" in dbg:
            break
        phase_out(k, l, xsrc, MIXT, wout, yT if last else XN, env, last)

    k.barrier()
    k.root.close()
    return nc


def phase_inproj(k, l, xsrc, win, PT, modc):
    k.push()
    HT = k.sb("HT", [128, 16, T], BF16)
    k.push()
    xs = [k.sb("xs", [128, 2176], F32) for _ in range(2)]
    it = 0
    for kc in range(16):
        for half in range(2):
            x_ = xs[it % 2]
            it += 1
            t0 = half * 2176
            k.dma("sp", x_[:], xsrc[kc * 128:(kc + 1) * 128, t0:t0 + 2176], writes=[x_])
            def affine(out_, in_, r_):
                if it % 2:
                    k.ts("dve", out_, in_, modc(l, 16 + kc, r_), modc(l, kc, r_), ALU.mult, ALU.add, [x_], [HT])
                else:
                    k.act(out_, in_, AF.Identity, [x_], [HT], scale=modc(l, 16 + kc, r_), bias=modc(l, kc, r_))
            if half == 0:
                affine(HT[:, kc, 0:NCTX], x_[:, 0:NCTX], 1)
                affine(HT[:, kc, NCTX:2176], x_[:, NCTX:2176], 0)
            else:
                affine(HT[:, kc, 2176:T], x_[:], 0)
    k.pop()
    wf = [k.sb("wf", [128, 16, 128], F32) for _ in range(2)]
    wb = [k.sb("wb", [128, 16, 128], BF16) for _ in range(2)]
    og = [k.sb("og", [128, 2304], F32) for _ in range(2)]
    pp_ = [k.ps("pi") for _ in range(4)]
    GATE = set(range(12, 16)) | set(range(24, 32)) | set(range(46, 50))
    pi = 0
    oi = 0
    for c in range(NCC):
        w_f, w_b = wf[c % 2], wb[c % 2]
        k.dma("sp", w_f[:], win[l, :, c * 128:(c + 1) * 128].rearrange("(kc p) c -> p kc c", p=128), writes=[w_f])
        k.cp("pool", w_b[:], w_f[:], [w_f], [w_b])
        for half in range(2):
            o_ = og[oi % 2]
            oi += 1
            blocks = [b for b in TBLK if (b[0] < 2176) == (half == 0)]
            for (t0, tl) in blocks:
                p_ = pp_[pi % 4]
                pi += 1
                for kc in range(16):
                    k.mm(p_[:, 0:tl], w_b[:, kc, :], HT[:, kc, t0:t0 + tl], kc == 0, kc == 15, [w_b, HT], [p_])
                so = t0 - half_start(half)
                fn = AF.Silu if c in GATE else AF.Copy
                if fn == AF.Copy and (pi % 2):
                    k.cp("dve", o_[:, so:so + tl], p_[:, 0:tl], [p_], [o_])
                else:
                    k.act(o_[:, so:so + tl], p_[:, 0:tl], fn, [p_], [o_])
            hs = half_start(half)
            hl = half_len(half)
            k.dma("pool", PT[c * 128:(c + 1) * 128, hs:hs + hl], o_[:, 0:hl], reads=[o_])
    k.pop()


def _v3(ap, c=128):
    return ap.rearrange("p (n c) -> p n c", c=c)


def group_norm_fm(k, env, pO, tl, lhs_mean, eps, gcol, tmp, e2="pool"):
    C, cst = env["C"], env["cst"]
    y, ysq, d, m2, pm, pq = tmp
    k.cp("act", y[:, 0:tl], pO[:, 0:tl], [pO], [y])
    k.act(ysq[:, 0:tl], pO[:, 0:tl], AF.Square, [pO], [ysq])
    k.mm(pm[:, 0:tl], lhs_mean, y[:, 0:tl], True, True, [cst, y], [pm])
    k.mm(pq[:, 0:tl], lhs_mean, ysq[:, 0:tl], True, True, [cst, ysq], [pq])
    k.tt("dve", d[:, 0:tl], y[:, 0:tl], pm[:, 0:tl], ALU.subtract, [y, pm], [d])
    k.act(m2[:, 0:tl], pm[:, 0:tl], AF.Square, [pm], [m2])
    k.stt(m2[:, 0:tl], m2[:, 0:tl], -1.0, pq[:, 0:tl], ALU.mult, ALU.add, [m2, pq], [m2])
    k.ts("dve", m2[:, 0:tl], m2[:, 0:tl], 0.0, eps, ALU.max, ALU.add, [m2], [m2])
    k.act(m2[:, 0:tl], m2[:, 0:tl], AF.Sqrt, [m2], [m2])
    k.op("dve", lambda g: g.reciprocal(out=m2[:, 0:tl], in_=m2[:, 0:tl]), [m2], [m2])
    k.stt(d[:, 0:tl], d[:, 0:tl], gcol, m2[:, 0:tl], ALU.mult, ALU.mult, [d, m2, env["pp"]], [d])
    return d


def phase_mla(k, l, PT, MIXT, wuq, wukv, tabQ, tabKc, tabKs, env, need_ctx):
    C, P, cst, pp = env["C"], env["P"], env["cst"], env["pp"]
    k.push()
    cqn = k.sb("cqn", [128, 4, T], BF16)
    ckvn = k.sb("ckvn", [128, 2, T], BF16)
    KR = k.sb("KR", [128, T], BF16)
    tq = k.sb("tq", [128, T], F32)
    k.dma("sp", tq[:], tabQ, writes=[tq])
    wq_b = k.sb("wq_b", [128, 4, 2048], BF16)
    wkv_b = k.sb("wkv_b", [128, 2, 2048], BF16)
    onesb = k.sb("onesb", [128, 128], BF16)
    k.cp("dve", onesb[:], C("ones"), [cst], [onesb])
    k.push()
    wst = [k.sb("wst", [128, 2048], F32) for _ in range(2)]
    for i in range(6):
        w = wst[i % 2]
        src = wuq[l, i * 128:(i + 1) * 128, :] if i < 4 else wukv[l, (i - 4) * 128:(i - 3) * 128, :]
        k.dma("sp", w[:], src, writes=[w])
        if i < 4:
            k.cp("pool", wq_b[:, i, :], w[:], [w], [wq_b])
        else:
            k.cp("pool", wkv_b[:, i - 4, :], w[:], [w], [wkv_b])
    xin = [k.sb("xin", [128, 8, 512], F32) for _ in range(2)]
    tk = [k.sb("tk", [128, 2, 512], F32) for _ in range(2)]
    sq = k.sb("sq", [128, 6, 512], F32)
    rs = k.sb("rs", [128, 2, 512], F32)
    t1 = k.sb("t1", [128, 512], F32)
    t2 = k.sb("t2", [128, 512], F32)
    psq = [k.ps("psq") for _ in range(2)]
    for bi, (t0, tl) in enumerate(TBLK):
        x_, tk_ = xin[bi % 2], tk[bi % 2]
        k.dma("sp", x_[:, :, 0:tl], PT[16 * 128:24 * 128, t0:t0 + tl].rearrange("(c p) t -> p c t", p=128), writes=[x_])
        k.dma("sp", tk_[:, 0, 0:tl], tabKc[:, t0:t0 + tl], writes=[tk_])
        k.dma("sp", tk_[:, 1, 0:tl], tabKs[:, t0:t0 + tl], writes=[tk_])
        k.act(sq[:, :, 0:tl], x_[:, 0:6, 0:tl], AF.Square, [x_], [sq])
        for g, (c0, n) in enumerate([(0, 4), (4, 2)]):
            for j in range(n):
                k.mm(psq[g][:, 0:tl], C("ones"), sq[:, c0 + j, 0:tl], j == 0, j == n - 1, [cst, sq], [psq[g]])
            k.ts("dve", rs[:, g, 0:tl], psq[g][:, 0:tl], 1.0 / (n * 128), 1e-6, ALU.mult, ALU.add, [psq[g]], [rs])
            k.act(rs[:, g, 0:tl], rs[:, g, 0:tl], AF.Sqrt, [rs], [rs])
            k.op("dve", lambda e, g=g: e.reciprocal(out=rs[:, g, 0:tl], in_=rs[:, g, 0:tl]), [rs], [rs])
            for j in range(n):
                if g == 0:
                    k.stt(cqn[:, j, t0:t0 + tl], x_[:, j, 0:tl], P(l, "gq", j), rs[:, 0, 0:tl], ALU.mult, ALU.mult,
                          [x_, rs, pp], [cqn])
                else:
                    k.stt(ckvn[:, j, t0:t0 + tl], x_[:, 4 + j, 0:tl], P(l, "gkv", j), rs[:, 1, 0:tl], ALU.mult, ALU.mult,
                          [x_, rs, pp], [ckvn])
        k.tt("pool", t1[:, 0:tl], x_[:, 6, 0:tl], tk_[:, 0, 0:tl], ALU.mult, [x_, tk_], [t1])
        k.tt("dve", t2[:, 0:tl], x_[:, 7, 0:tl], tk_[:, 1, 0:tl], ALU.mult, [x_, tk_], [t2])
        k.tt("dve", KR[:, t0:t0 + tl], t1[:, 0:tl], t2[:, 0:tl], ALU.add, [t1, t2], [KR])
    k.pop()
    KN = k.sb("KN", [128, T], BF16)
    V = k.sb("V", [128, NCH, 128], BF16)
    QN = k.sb("QN", [128, T], BF16)
    QR = k.sb("QR", [128, T], BF16)
    pS = [k.ps("pS") for _ in range(3)]
    pO = k.ps("pO")
    pD = k.ps("pD")
    pA = [k.ps("pA") for _ in range(2)]
    Pt = [k.sb("Pt", [128, 512], BF16) for _ in range(4)]
    gt = [k.sb("gt", [128, 512], F32) for _ in range(2)]
    rd = k.sb("rd", [128, 512], F32)
    ot = k.sb("ot", [128, 512], F32)
    mo = [k.sb("mo", [128, 512], MIXT.dtype) for _ in range(2)]
    ai = 0
    for h in range(8):
        for (t0, tl) in TBLK:
            p_ = pA[ai % 2]
            ai += 1
            for kc in range(2):
                k.mm(p_[:, 0:tl], wkv_b[:, kc, h * 256:h * 256 + 128], ckvn[:, kc, t0:t0 + tl], kc == 0, kc == 1,
                     [wkv_b, ckvn], [p_])
            k.cp("dve", KN[:, t0:t0 + tl], p_[:, 0:tl], [p_], [KN])
            p_ = pA[ai % 2]
            ai += 1
            for kc in range(4):
                k.mm(p_[:, 0:tl], wq_b[:, kc, h * 256:h * 256 + 128], cqn[:, kc, t0:t0 + tl], kc == 0, kc == 3,
                     [wq_b, cqn], [p_])
            k.cp("act", QN[:, t0:t0 + tl], p_[:, 0:tl], [p_], [QN])
            p_ = pA[ai % 2]
            ai += 1
            for kc in range(4):
                k.mm(p_[:, 0:tl], wq_b[:, kc, h * 256 + 128:h * 256 + 256], cqn[:, kc, t0:t0 + tl], kc == 0, kc == 3,
                     [wq_b, cqn], [p_])
            k.tt("dve", QR[:, t0:t0 + tl], p_[:, 0:tl], tq[:, t0:t0 + tl], ALU.mult, [p_, tq], [QR])
        for n4 in range(0, NCH, 4):
            p_ = pA[ai % 2]
            ai += 1
            nn = min(4, NCH - n4)
            for j in range(nn):
                n = n4 + j
                for kc in range(2):
                    k.mm(p_[:, j * 128:(j + 1) * 128], ckvn[:, kc, n * 128:(n + 1) * 128],
                         wkv_b[:, kc, h * 256 + 128:h * 256 + 256], kc == 0, kc == 1, [ckvn, wkv_b], [p_])
            k.cp("act", V[:, n4:n4 + nn, :], _v3(p_[:, 0:nn * 128]), [p_], [V])
        for bi, (t0, tl) in enumerate(TBLK):
            if bi == 0 and not need_ctx:
                continue
            kbs = list(range(0, 2)) if bi == 0 else list(range(NCH))
            g_ = gt[bi % 2]
            k.dma("sp", g_[:, 0:tl], PT[(24 + h) * 128:(25 + h) * 128, t0:t0 + tl], writes=[g_])
            nk = len(kbs)

            def s_step(ii):
                kb = kbs[ii]
                s_ = pS[ii % 3]
                k.mm(s_[:, 0:tl], KN[:, kb * 128:(kb + 1) * 128], QN[:, t0:t0 + tl], True, False, [KN, QN], [s_])
                k.mm(s_[:, 0:tl], KR[:, kb * 128:(kb + 1) * 128], QR[:, t0:t0 + tl], False, True, [KR, QR], [s_])
                p_ = Pt[ii % 4]
                k.act(p_[:, 0:tl], s_[:, 0:tl], AF.Exp, [s_], [p_], scale=MLA_SCALE)

            def od_step(ii):
                kb = kbs[ii]
                p_ = Pt[ii % 4]
                k.mm(pO[:, 0:tl], V[:, kb, :], p_[:, 0:tl], ii == 0, ii == nk - 1, [V, p_], [pO])
                k.mm(pD[:, 0:tl], onesb[:], p_[:, 0:tl], ii == 0, ii == nk - 1, [onesb, p_], [pD])

            for ii in range(min(2, nk)):
                s_step(ii)
            for ii in range(nk):
                if ii + 2 < nk:
                    s_step(ii + 2)
                od_step(ii)
            k.op("dve", lambda e, tl=tl: e.reciprocal(out=rd[:, 0:tl], in_=pD[:, 0:tl]), [pD], [rd])
            k.tt("dve", ot[:, 0:tl], pO[:, 0:tl], rd[:, 0:tl], ALU.mult, [pO, rd], [ot])
            m_ = mo[bi % 2]
            k.tt("pool", m_[:, 0:tl], ot[:, 0:tl], g_[:, 0:tl], ALU.mult, [ot, g_], [m_])
            k.dma("pool", MIXT[512 + h * 128:512 + (h + 1) * 128, t0:t0 + tl], m_[:, 0:tl], reads=[m_])
    k.pop()


def phase_ret(k, l, PT, MIXT, rdl, env, need_ctx):
    C, P, cst, pp = env["C"], env["P"], env["cst"], env["pp"]
    k.push()
    lg = k.sb("lg", [128, 8], F32)
    rdt = k.sb("rdt", [128, 16], F32)
    k.dma("sp", rdt[:], rdl, writes=[rdt])
    k.act(lg[:], rdt[:, l * 8:(l + 1) * 8], AF.Exp, [rdt], [lg], scale=-1.0)
    k.ts("dve", lg[:], lg[:], 1.0, None, ALU.add, ALU.bypass, [lg], [lg])
    k.act(lg[:], lg[:], AF.Ln, [lg], [lg])
    k.ts("dve", lg[:], lg[:], -1.0, None, ALU.mult, ALU.bypass, [lg], [lg])
    mask = k.sb("mask", [128, 128], F32)
    mt = k.sb("mt", [128, 128], F32)
    xi = [k.sb("xi", [128, 128], F32) for _ in range(2)]
    zeta = k.sb("zeta", [128, 2], F32)
    gC = k.sb("gC", [128, 2], F32)
    qT, kT, vT, sg = [k.sb(n, [128, T], F32) for n in ("qT", "kT", "vT", "sg")]
    qb, kb_, qxf, qxb = [k.sb(n, [128, T], BF16) for n in ("qb", "kb", "qxf", "qxb")]
    knf, knb, vn, SPf, SPb = [k.sb(n, [128, NCH, 128], BF16) for n in ("knf", "knb", "vn", "SPf", "SPb")]
    st = [k.sb("st", [128, 128], F32) for _ in range(2)]
    SM = k.sb("SM", [128, 4, 128], BF16)
    tmp = [k.sb("gn", [128, 512], F32) for _ in range(4)]
    mo = [k.sb("mo", [128, 512], MIXT.dtype) for _ in range(2)]
    pT = [k.ps("pT") for _ in range(2)]
    pU = [k.ps("pU") for _ in range(2)]
    pS = k.ps("pS")
    pO = k.ps("pO")
    pm, pq = k.ps("pm"), k.ps("pq")
    sc = 128 ** -0.5
    ui = 0
    import os
    STOP = int(os.environ.get("RET_STOP", "99"))
    for h in range(4):
        lgf, lgb = lg[:, h:h + 1], lg[:, 4 + h:5 + h]
        k.act(mask[:], C("relf"), AF.Exp, [cst, lg], [mask], scale=lgf)
        k.tt("dve", mask[:], mask[:], C("mf"), ALU.mult, [mask, cst], [mask])
        k.act(mt[:], C("relb"), AF.Exp, [cst, lg], [mt], scale=lgb)
        k.tt("dve", mt[:], mt[:], C("mb"), ALU.mult, [mt, cst], [mt])
        k.tt("dve", mask[:], mask[:], mt[:], ALU.add, [mask, mt], [mask])
        k.ts("dve", mask[:], mask[:], sc, None, ALU.mult, ALU.bypass, [mask], [mask])
        k.act(xi[0][:], C("ip1"), AF.Exp, [cst, lg], [xi[0]], scale=lgf)
        k.act(xi[1][:], C("cmi"), AF.Exp, [cst, lg], [xi[1]], scale=lgb)
        k.act(zeta[:, 0:1], C("colcj"), AF.Exp, [cst, lg], [zeta], scale=lgf)
        k.act(zeta[:, 1:2], C("colj"), AF.Exp, [cst, lg], [zeta], scale=lgb)
        k.ts("dve", zeta[:], zeta[:], sc, None, ALU.mult, ALU.bypass, [zeta], [zeta])
        k.act(gC[:, 0:1], lgf, AF.Exp, [lg], [gC], scale=128.0)
        k.act(gC[:, 1:2], lgb, AF.Exp, [lg], [gC], scale=128.0)
        if STOP <= 1:
            continue
        for j, tile in enumerate([qT, kT, vT, sg]):
            r0 = (j * 4 + h) * 128
            k.dma("sp", tile[:, 0:2176], PT[r0:r0 + 128, 0:2176], writes=[tile])
            k.dma("sp", tile[:, 2176:T], PT[r0:r0 + 128, 2176:T], writes=[tile])
        k.cp("act", qb[:], qT[:], [qT], [qb])
        k.cp("dve", kb_[:], kT[:], [kT], [kb_])
        k.tt("dve", _v3(qxf[:]), _v3(qT[:]), xi[0][:].unsqueeze(1).broadcast_to([128, NCH, 128]), ALU.mult,
             [qT, xi[0]], [qxf])
        k.tt("dve", _v3(qxb[:]), _v3(qT[:]), xi[1][:].unsqueeze(1).broadcast_to([128, NCH, 128]), ALU.mult,
             [qT, xi[1]], [qxb])
        if STOP <= 2:
            continue
        for n4 in range(0, NCH, 4):
            nn = min(4, NCH - n4)
            pk, pv = pT
            for j in range(nn):
                n = n4 + j
                k.tr(pk[:, j * 128:(j + 1) * 128], kT[:, n * 128:(n + 1) * 128], C("ident"), [kT, cst], [pk])
                k.tr(pv[:, j * 128:(j + 1) * 128], vT[:, n * 128:(n + 1) * 128], C("ident"), [vT, cst], [pv])
            k.ts("dve", knf[:, n4:n4 + nn, :], _v3(pk[:, 0:nn * 128]), zeta[:, 0:1], None, ALU.mult, ALU.bypass,
                 [pk, zeta], [knf])
            k.ts("dve", knb[:, n4:n4 + nn, :], _v3(pk[:, 0:nn * 128]), zeta[:, 1:2], None, ALU.mult, ALU.bypass,
                 [pk, zeta], [knb])
            k.cp("act", vn[:, n4:n4 + nn, :], _v3(pv[:, 0:nn * 128]), [pv], [vn])
        if STOP <= 3:
            continue
        for d, order in enumerate([list(range(NCH)), [1, 0] + list(range(NCH - 1, 1, -1))]):
            kn = knf if d == 0 else knb
            SP = SPf if d == 0 else SPb
            s_ = st[d]
            k.op("dve", lambda e, s_=s_: e.memset(s_[:], 0.0), [], [s_])
            for n in order:
                k.cp("act", SP[:, n, :], s_[:], [s_], [SP])
                pu = pU[ui % 2]
                ui += 1
                k.mm(pu[:, 0:128], kn[:, n, :], vn[:, n, :], True, True, [kn, vn], [pu])
                k.stt(s_[:], s_[:], gC[:, d:d + 1], pu[:, 0:128], ALU.mult, ALU.add, [s_, gC, pu], [s_])
        if STOP <= 4:
            continue
        for bi, (t0, tl) in enumerate(TBLK):
            if bi == 0 and not need_ctx:
                continue
            nn, n0 = tl // 128, t0 // 128
            for j in range(nn):
                n = n0 + j
                k.mm(pS[:, j * 128:(j + 1) * 128], kb_[:, n * 128:(n + 1) * 128], qb[:, n * 128:(n + 1) * 128], True, True,
                     [kb_, qb], [pS])
            k.tt("dve", SM[:, 0:nn, :], _v3(pS[:, 0:tl]), mask[:].unsqueeze(1).broadcast_to([128, nn, 128]), ALU.mult,
                 [pS, mask], [SM])
            for j in range(nn):
                n = n0 + j
                cs = slice(j * 128, (j + 1) * 128)
                k.mm(pO[:, cs], vn[:, n, :], SM[:, j, :], True, False, [vn, SM], [pO])
                k.mm(pO[:, cs], SPf[:, n, :], qxf[:, n * 128:(n + 1) * 128], False, False, [SPf, qxf], [pO])
                k.mm(pO[:, cs], SPb[:, n, :], qxb[:, n * 128:(n + 1) * 128], False, True, [SPb, qxb], [pO])
            d_ = group_norm_fm(k, env, pO, tl, C("onesm"), 1e-5, P(l, "retg", h), tmp + [pm, pq])
            m_ = mo[bi % 2]
            k.tt("pool", m_[:, 0:tl], d_[:, 0:tl], sg[:, t0:t0 + tl], ALU.mult, [d_, sg], [m_])
            k.dma("pool", MIXT[h * 128:(h + 1) * 128, t0:t0 + tl], m_[:, 0:tl], reads=[m_])
    k.pop()


def phase_out(k, l, xsrc, MIXT, wout, dst, env, last):
    C, P, cst, pp, mod, modc = env["C"], env["P"], env["cst"], env["pp"], env["mod"], env["modc"]
    k.push()
    wo = k.sb("wo", [128, 16, 2048], BF16)
    k.push()
    wst = [k.sb("wst", [128, 2048], F32) for _ in range(2)]
    for kc in range(16):
        w = wst[kc % 2]
        k.dma("sp", w[:], wout[l, kc * 128:(kc + 1) * 128, :], writes=[w])
        k.cp("pool" if kc % 2 else "act", wo[:, kc, :], w[:], [w], [wo])
    k.pop()
    mx = [k.sb("mx", [128, 16, 512], BF16) for _ in range(2)]
    xz = [k.sb("xz", [128, 16, 512], F32) for _ in range(2)]
    tt_ = [k.sb("tt", [128, 512], F32) for _ in range(2)]
    sq_ = [k.sb("sq", [128, 512], BF16) for _ in range(2)]
    zb_ = [k.sb("zb", [128, 512], BF16) for _ in range(2)]
    o2kb = k.sb("o2kb", [128, 128], BF16)
    k.cp("dve", o2kb[:], C("ones2k"), [cst], [o2kb])
    mean, rs, dd = [k.sb(n, [128, 512], F32) for n in ("mean", "rs", "dd")]
    pz = [k.ps("pz") for _ in range(3)]
    pm_, pq_ = k.ps("pm"), k.ps("pq")
    blocks = TBLK[1:] if last else TBLK
    for bi, (t0, tl) in enumerate(blocks):
        m_, x_ = mx[bi % 2], xz[bi % 2]
        k.dma("sp", m_[:, :, 0:tl], MIXT[:, t0:t0 + tl].rearrange("(kc p) t -> p kc t", p=128), writes=[m_])
        k.dma("sp", x_[:, :, 0:tl], xsrc[:, t0:t0 + tl].rearrange("(kc p) t -> p kc t", p=128), writes=[x_])
        r = 1 if t0 < NCTX else 0
        for dc in range(16):
            p_ = pz[dc % 3]
            for kc in range(16):
                k.mm(p_[:, 0:tl], wo[:, kc, dc * 128:(dc + 1) * 128], m_[:, kc, 0:tl], kc == 0, kc == 15, [wo, m_], [p_])
            t_ = tt_[dc % 2]
            s_ = sq_[dc % 2]
            k.act(t_[:, 0:tl], p_[:, 0:tl], AF.Identity, [p_, mod], [t_], scale=modc(l, 32 + dc, r))
            k.stt(x_[:, dc, 0:tl], x_[:, dc, 0:tl], ALPHA, t_[:, 0:tl], ALU.mult, ALU.add, [x_, t_], [x_])
            k.act(s_[:, 0:tl], x_[:, dc, 0:tl], AF.Square, [x_], [s_])
            zb = zb_[dc % 2]
            k.cp("pool", zb[:, 0:tl], x_[:, dc, 0:tl], [x_], [zb])
            k.mm(pm_[:, 0:tl], o2kb[:], zb[:, 0:tl], dc == 0, dc == 15, [o2kb, zb], [pm_])
            k.mm(pq_[:, 0:tl], o2kb[:], s_[:, 0:tl], dc == 0, dc == 15, [o2kb, s_], [pq_])
        k.cp("act", mean[:, 0:tl], pm_[:, 0:tl], [pm_], [mean])
        k.act(rs[:, 0:tl], pm_[:, 0:tl], AF.Square, [pm_], [rs])
        k.stt(rs[:, 0:tl], rs[:, 0:tl], -1.0, pq_[:, 0:tl], ALU.mult, ALU.add, [rs, pq_], [rs])
        k.ts("dve", rs[:, 0:tl], rs[:, 0:tl], 0.0, 1e-5, ALU.max, ALU.add, [rs], [rs])
        k.act(rs[:, 0:tl], rs[:, 0:tl], AF.Sqrt, [rs], [rs])
        k.op("dve", lambda e, tl=tl: e.reciprocal(out=rs[:, 0:tl], in_=rs[:, 0:tl]), [rs], [rs])
        for dc in range(16):
            k.tt("dve", dd[:, 0:tl], x_[:, dc, 0:tl], mean[:, 0:tl], ALU.subtract, [x_, mean], [dd])
            k.tt("pool", dd[:, 0:tl], dd[:, 0:tl], rs[:, 0:tl], ALU.mult, [dd, rs], [dd])
            k.act(x_[:, dc, 0:tl], dd[:, 0:tl], AF.Identity, [dd, pp], [x_], scale=P(l, "lng", dc), bias=P(l, "lnb", dc))
        o0 = t0 - NCTX if last else t0
        k.dma("pool", dst[:, o0:o0 + tl].rearrange("(kc p) t -> p kc t", p=128), x_[:, :, 0:tl], reads=[x_])
    k.pop()


RBLK = [(i * 256, 256) for i in range(17)]


def phase_rwkv(k, l, PT, MIXT, w2p, a2p, env, need_ctx):
    C, P, cst, pp = env["C"], env["P"], env["cst"], env["pp"]
    k.push()
    wlT, alT, rT, kT, vT, kkT, YT = [k.sb(n, [128, T], F32) for n in ("wlT", "alT", "rT", "kT", "vT", "kkT", "YT")]
    w2t = k.sb("w2t", [128, 128], F32)
    a2t = k.sb("a2t", [128, 128], F32)
    muc = k.sb("muc", [128, 14], F32)
    o0, o1 = PP["mu0"][0] + l * NPP, PP["mu1"][0] + l * NPP
    k.tt("dve", muc[:], pp[:, o0:o0 + 14], pp[:, o1:o1 + 14], ALU.add, [pp], [muc])
    k.ts("dve", muc[:], muc[:], -1.0, 1.0, ALU.mult, ALU.add, [muc], [muc])
    M4 = [k.sb("M4", [128, 512], F32) for _ in range(2)]
    MS = [None, None]
    for d, (a, b) in enumerate((("sf", "mf2"), ("sb", "mb2"))):
        for q in range(2):
            k.cp("dve", M4[d][:, q * 256:q * 256 + 128], C(a), [cst], [M4[d]])
            k.cp("dve", M4[d][:, q * 256 + 128:q * 256 + 256], C(b), [cst], [M4[d]])
    MS[0], MS[1] = C("sb"), C("sf")
    raw = YT

    def load_shift(c, dst):
        r0 = (32 + c) * 128
        k.dma("sp", raw[:, 0:2176], PT[r0:r0 + 128, 0:2176], writes=[raw])
        k.dma("sp", raw[:, 2176:T], PT[r0:r0 + 128, 2176:T], writes=[raw])
        k.act(dst[:], raw[:], AF.Identity, [raw, muc], [dst], scale=muc[:, c:c + 1])
        for (a, b) in ((0, NCTX), (NCTX, T)):
            k.stt(dst[:, a + 1:b], raw[:, a:b - 1], P(l, "mu0", c), dst[:, a + 1:b], ALU.mult, ALU.add, [raw, dst, pp], [dst])
            k.stt(dst[:, a:b - 1], raw[:, a + 1:b], P(l, "mu1", c), dst[:, a:b - 1], ALU.mult, ALU.add, [raw, dst, pp], [dst])

    load_shift(12, wlT)
    k.act(wlT[:], wlT[:], AF.Tanh, [wlT], [wlT])
    load_shift(13, alT)
    G = {n: k.sb(n, [128, 256], F32) for n in ("sgm", "ar", "ld", "cI", "cX", "E1", "E2", "E3", "E4", "dl", "ke", "b",
                                               "Bt", "Bh", "Kt", "Kh", "g1", "g2", "g3", "g4")}
    AR = k.sb("AR", [128, 512], F32)
    WC = k.sb("WC", [128, 4], F32)
    NJ = 4
    J = []
    for _ in range(NJ):
        J.append(dict(
            XBK=k.sb("XBK", [128, 512], F32), A_=k.sb("A_", [128, 128], BF16), TM=k.sb("TM", [128, 192], F32),
            ATb=k.sb("ATb", [128, 128], BF16), TTf=k.sb("TTf", [128, 128], F32),
            Y0=k.sb("Y0", [128, 128], F32), T_=[k.sb("T_", [128, 128], BF16) for _ in range(2)],
            TT_=[k.sb("TT_", [128, 128], BF16) for _ in range(2)], Wm=k.sb("Wm", [128, 128], BF16),
            Wn=k.sb("Wn", [128, 128], BF16), X_=k.sb("X_", [128, 128], F32), Q1T=k.sb("Q1T", [128, 128], F32),
            GmT=k.sb("GmT", [128, 64], F32), b1=k.ps("b1"), b2=k.ps("b2")))
    Hss = [[k.sb("Hs", [128, 64], F32) for _ in range(2)] for _ in range(2)]
    YTr = [Res("yt0"), Res("yt1")]
    mo = [k.sb("mo", [128, 256], MIXT.dtype) for _ in range(2)]
    pz, pa = J[0]["b1"], J[1]["b1"]
    ident = C("ident")
    oLM, oLMT = CS["LM"][0], CS["LMT"][0]

    def chunk_head(job, hh, j, t0, cur, delay):
        XBK, A_, TM, Y0, T_, TT_, Wm, Wn, X_, Q1T, GmT, b1, b2 = (job[n] for n in (
            "XBK", "A_", "TM", "Y0", "T_", "TT_", "Wm", "Wn", "X_", "Q1T", "GmT", "b1", "b2"))
        ytr = YTr[hh]
        Hs = Hss[hh]
        ATb, TTf = job["ATb"], job["TTf"]
        d = cur_d[0]
        mT = (lambda q: cst[:, oLM + q * 128:oLM + (q + 1) * 128]) if d == 0 else \
             (lambda q: cst[:, oLMT + q * 128:oLMT + (q + 1) * 128])
        mTT = (lambda q: cst[:, oLMT + q * 128:oLMT + (q + 1) * 128]) if d == 0 else \
              (lambda q: cst[:, oLM + q * 128:oLM + (q + 1) * 128])
        cs = slice(j * 128, (j + 1) * 128)
        tk_ = slice(t0 + j * 128, t0 + (j + 1) * 128)
        pb = hh * 64
        ps_ = slice(pb, pb + 64)
        At = AR[ps_, j * 256:j * 256 + 128]
        Rt = AR[ps_, j * 256 + 128:j * 256 + 256]
        ARj = AR[ps_, j * 256:(j + 1) * 256]
        idn = ident[ps_, pb:pb + 64]
        k.mm(b1[:, 0:256], G["Bt"][ps_, cs], ARj, True, True, [G["Bt"], AR], [b1])
        k.mm(b1[:, 256:512], G["Kt"][ps_, cs], ARj, True, True, [G["Kt"], AR], [b1])
        k.mm(b2[:, 0:128], At, G["Bt"][ps_, cs], True, True, [AR, G["Bt"]], [b2])
        k.tr(b2[:, 128:192], G["Bh"][ps_, cs], idn, [G["Bh"], cst], [b2])
        k.tr(b2[:, 192:256], G["Kh"][ps_, cs], idn, [G["Kh"], cst], [b2])
        k.tr(b2[:, 256:320], vT[ps_, tk_], idn, [vT, cst], [b2])
        k.tr(b2[:, 320:384], At, idn, [AR, cst], [b2])
        yield
        k.tt("dve", XBK[:], b1[:], M4[d][:], ALU.mult, [b1, M4[d]], [XBK])
        k.cp("act", TM[:], b2[:, 128:320], [b2], [TM])
        yield
        k.tt("dve", A_[:], b2[:, 0:128], MS[d], ALU.mult, [b2, cst], [A_])
        k.mm(b2[:, 384:448], XBK[:, 256:384], TM[:, 128:192], True, True, [XBK, TM], [b2])
        yield
        k.tt("pool", Wm[:], A_[:], mT(0), ALU.mult, [A_, cst], [Wm])
        k.tt("pool", Wn[:], XBK[:, 0:128], mTT(0), ALU.mult, [XBK, cst], [Wn])
        k.cp("pool", ATb[:], XBK[:, 0:128], [XBK], [ATb])
        k.cp("act", Y0[:], b2[:, 320:448], [b2], [Y0])
        yield
        k.tt("pool", T_[0][:], Wm[:], ident, ALU.add, [Wm, cst], [T_[0]])
        k.tt("pool", TT_[0][:], Wn[:], ident, ALU.add, [Wn, cst], [TT_[0]])
        yield
        c_ = 0
        for q in range(1, 7):
            lastq = (q == 6)
            if not lastq:
                k.mm(b1[:, 0:128], ATb[:], T_[c_][:], True, True, [ATb, T_[c_]], [b1])
            k.mm(b1[:, 128:256], A_[:], TT_[c_][:], True, True, [A_, TT_[c_]], [b1])
            yield
            if not lastq:
                k.tt("dve", Wm[:], b1[:, 0:128], mT(q), ALU.mult, [b1, cst], [Wm])
            k.tt("dve", Wn[:], b1[:, 128:256], mTT(q), ALU.mult, [b1, cst], [Wn])
            yield
            if not lastq:
                k.mm(b1[:, 256:384], TT_[c_][:], Wm[:], True, True, [TT_[c_], Wm], [b1])
            k.mm(b1[:, 384:512], T_[c_][:], Wn[:], True, True, [T_[c_], Wn], [b1])
            yield
            if not lastq:
                k.tt("dve", T_[1 - c_][:], T_[c_][:], b1[:, 256:384], ALU.add, [T_[c_], b1], [T_[1 - c_]])
            if not lastq:
                k.tt("dve", TT_[1 - c_][:], TT_[c_][:], b1[:, 384:512], ALU.add, [TT_[c_], b1], [TT_[1 - c_]])
            else:
                k.tt("dve", TTf[:], TT_[c_][:], b1[:, 384:512], ALU.add, [TT_[c_], b1], [TTf])
            yield
            c_ = 1 - c_
        k.mm(b1[:, 0:128], TTf[:], Y0[:], True, True, [TTf, Y0], [b1])
        yield
        k.cp("act", X_[:], b1[:, 0:128], [b1], [X_])
        yield
        P1, P2 = X_[:, 0:64], X_[:, 64:128]
        Bh_, Kh_, V_ = TM[:, 0:64], TM[:, 64:128], TM[:, 128:192]
        k.mm(b2[ps_, 0:128], P1, XBK[:, 128:256], True, True, [X_, XBK], [b2])
        k.mm(b2[ps_, 128:192], P1, Bh_, True, True, [X_, TM], [b2])
        yield
        k.tt("dve", Q1T[ps_, :], b2[ps_, 0:128], Rt, ALU.add, [b2, AR], [Q1T])
        k.stt(GmT[ps_, :], ident[ps_, pb:pb + 64], WC[ps_, j:j + 1], b2[ps_, 128:192], ALU.mult, ALU.add,
              [cst, WC, b2], [GmT])
        yield
        Ho, Hn = Hs[cur], Hs[1 - cur]
        for _ in range(delay):
            yield
        k.mm(b2[ps_, 192:256], Bh_, P2, True, False, [TM, X_], [b2])
        k.mm(b2[ps_, 192:256], Kh_, V_, False, False, [TM], [b2])
        k.mm(b2[ps_, 192:256], GmT[ps_, :], Ho[ps_, :], False, True, [GmT, Ho], [b2])
        k.mm(b2[ps_, 256:384], P2, XBK[:, 128:256], True, False, [X_, XBK], [b2])
        k.mm(b2[ps_, 256:384], V_, XBK[:, 384:512], False, False, [TM, XBK], [b2])
        k.mm(b2[ps_, 256:384], Ho[ps_, :], Q1T[ps_, :], False, True, [Ho, Q1T], [b2])
        yield
        k.cp("act", Hn[ps_, :], b2[ps_, 192:256], [b2], [Hn])
        if d == 0:
            k.cp("dve", YT[ps_, tk_], b2[ps_, 256:384], [b2], [ytr])
        else:
            k.tt("dve", YT[ps_, tk_], YT[ps_, tk_], b2[ps_, 256:384], ALU.add, [ytr, b2], [ytr])
        yield

    cur_d = [0]
    import os
    STAG = int(os.environ.get("RW_STAG", "1"))
    for hp in range(4):
        load_shift(hp, rT)
        load_shift(4 + hp, kT)
        load_shift(8 + hp, vT)
        k.op("dve", lambda e: e.memset(YT[:, 0:1], 0.0), [], [YT] + YTr)
        for (t0, tl) in RBLK:
            g1, g2 = G["g1"], G["g2"]
            k.ts("dve", kkT[:, t0:t0 + tl], kT[:, t0:t0 + tl], P(l, "kk", hp), None, ALU.mult, ALU.bypass, [kT, pp], [kkT])
            k.act(g1[:, 0:tl], kkT[:, t0:t0 + tl], AF.Square, [kkT], [g1])
            k.mm(pz[:, 0:tl], C("blk64"), g1[:, 0:tl], True, True, [cst, g1], [pz])
            k.act(g2[:, 0:tl], pz[:, 0:tl], AF.Sqrt, [pz], [g2])
            k.ts("dve", g2[:, 0:tl], g2[:, 0:tl], 1e-12, None, ALU.max, ALU.bypass, [g2], [g2])
            k.op("dve", lambda e, tl=tl: e.reciprocal(out=g2[:, 0:tl], in_=g2[:, 0:tl]), [g2], [g2])
            k.tt("dve", kkT[:, t0:t0 + tl], kkT[:, t0:t0 + tl], g2[:, 0:tl], ALU.mult, [kkT, g2], [kkT])
        for d in range(2):
            blocks = list(RBLK) if d == 0 else [RBLK[0]] + list(reversed(RBLK[1:]))
            cur = 0
            cur_d[0] = d
            k.dma("sp", w2t[:], w2p[l, d, :, hp * 128:(hp + 1) * 128], writes=[w2t])
            k.dma("sp", a2t[:], a2p[l, d, :, hp * 128:(hp + 1) * 128], writes=[a2t])
            for hh_ in range(2):
                k.op("dve", lambda e, hh_=hh_: e.memset(Hss[hh_][0][:], 0.0), [], [Hss[hh_][0]])
            for (t0, tl) in blocks:
                nn = tl // 128
                sl = slice(t0, t0 + tl)
                w = slice(0, tl)
                k.mm(pz[:, w], w2t[:], wlT[:, sl], True, True, [w2t, wlT], [pz])
                k.mm(pa[:, w], a2t[:], alT[:, sl], True, True, [a2t, alT], [pa])
                k.act(G["sgm"][:, w], pz[:, w], AF.Sigmoid, [pz, pp], [G["sgm"]], bias=P(l, "w0", d * 4 + hp))
                k.act(G["ar"][:, w], pa[:, w], AF.Sigmoid, [pa, pp], [G["ar"]], bias=P(l, "a0", d * 4 + hp))
                k.ts("dve", G["ld"][:, w], G["sgm"][:, w], -0.6065306597126334, None, ALU.mult, ALU.bypass,
                     [G["sgm"]], [G["ld"]])
                o, c0_, rm = CS["rm"][0], None, None
                k.op("dve", lambda e, w=w: e.tensor_tensor_scan(out=G["cI"][:, w], data0=env["cst"][:, o:o + w.stop],
                                                               data1=G["ld"][:, w], initial=0.0, op0=ALU.mult,
                                                               op1=ALU.add), [cst, G["ld"]], [G["cI"]])
                cI3 = _v3(G["cI"][:, w])
                if d == 1:
                    k.tt("dve", G["g1"][:, w], G["ld"][:, w], G["cI"][:, w], ALU.subtract, [G["ld"], G["cI"]], [G["g1"]])
                    k.tt("dve", _v3(G["g2"][:, w]), _v3(G["g1"][:, w]), cI3[:, :, 127:128].broadcast_to([128, nn, 128]),
                         ALU.add, [G["g1"], G["cI"]], [G["g2"]])
                    k.cp("dve", G["cI"][:, w], G["g2"][:, w], [G["g2"]], [G["cI"]])
                    tot = cI3[:, :, 0:1]
                else:
                    tot = cI3[:, :, 127:128]
                k.tt("dve", G["cX"][:, w], G["cI"][:, w], G["ld"][:, w], ALU.subtract, [G["cI"], G["ld"]], [G["cX"]])
                k.act(WC[:, 0:nn].unsqueeze(2), tot, AF.Exp, [G["cI"]], [WC])
                k.act(G["E1"][:, w], G["cI"][:, w], AF.Exp, [G["cI"]], [G["E1"]])
                k.act(G["E2"][:, w], G["cI"][:, w], AF.Exp, [G["cI"]], [G["E2"]], scale=-1.0)
                k.act(G["E3"][:, w], G["cX"][:, w], AF.Exp, [G["cX"]], [G["E3"]])
                k.tt("dve", _v3(G["dl"][:, w]), tot.broadcast_to([128, nn, 128]), cI3, ALU.subtract, [G["cI"]], [G["dl"]])
                k.act(G["E4"][:, w], G["dl"][:, w], AF.Exp, [G["dl"]], [G["E4"]])
                k.ts("dve", G["g3"][:, w], G["ar"][:, w], -1.0, P(l, "ka", hp), ALU.add, ALU.mult, [G["ar"], pp], [G["g3"]])
                k.stt(G["ke"][:, w], G["g3"][:, w], 1.0, kT[:, sl], ALU.add, ALU.mult, [G["g3"], kT], [G["ke"]])
                ARv = AR[:, 0:nn * 256].rearrange("p (n a c) -> p n a c", a=2, c=128)
                k.stt(ARv[:, :, 0, :], _v3(kkT[:, sl]), -1.0, _v3(G["E3"][:, w]), ALU.mult, ALU.mult, [kkT, G["E3"]], [AR])
                k.tt("dve", ARv[:, :, 1, :], _v3(rT[:, sl]), _v3(G["E1"][:, w]), ALU.mult, [rT, G["E1"]], [AR])
                k.tt("pool", G["b"][:, w], kkT[:, sl], G["ar"][:, w], ALU.mult, [kkT, G["ar"]], [G["b"]])
                k.tt("dve", G["Bt"][:, w], G["b"][:, w], G["E2"][:, w], ALU.mult, [G["b"], G["E2"]], [G["Bt"]])
                k.tt("pool", G["Bh"][:, w], G["b"][:, w], G["E4"][:, w], ALU.mult, [G["b"], G["E4"]], [G["Bh"]])
                k.tt("dve", G["Kt"][:, w], G["ke"][:, w], G["E2"][:, w], ALU.mult, [G["ke"], G["E2"]], [G["Kt"]])
                k.tt("pool", G["Kh"][:, w], G["ke"][:, w], G["E4"][:, w], ALU.mult, [G["ke"], G["E4"]], [G["Kh"]])
                jorder = list(range(nn)) if d == 0 else list(reversed(range(nn)))
                gens = []
                for ji, j in enumerate(jorder):
                    for hh in range(2):
                        gens.append(chunk_head(J[(ji * 2 + hh) % NJ], hh, j, t0, cur, 2 * ji))
                    cur = 1 - cur
                rnd = 0
                done = [False] * len(gens)
                while not all(done):
                    for gi, g_ in enumerate(gens):
                        if done[gi] or rnd < gi * STAG:
                            continue
                        try:
                            next(g_)
                        except StopIteration:
                            done[gi] = True
                    rnd += 1
        for bi, (t0, tl) in enumerate(RBLK):
            if bi == 0 and not need_ctx:
                continue
            sl = slice(t0, t0 + tl)
            w = slice(0, tl)
            y = YT
            g_ = G["g4"]
            k.dma("sp", g_[:, w], PT[(46 + hp) * 128:(47 + hp) * 128, sl], writes=[g_])
            k.act(G["g1"][:, w], YT[:, sl], AF.Square, [YT] + YTr, [G["g1"]])
            k.mm(pz[:, w], C("blk64m"), YT[:, sl], True, True, [cst, YT] + YTr, [pz])
            k.mm(pa[:, w], C("blk64m"), G["g1"][:, w], True, True, [cst, G["g1"]], [pa])
            k.tt("dve", G["g2"][:, w], YT[:, sl], pz[:, w], ALU.subtract, [YT, pz] + YTr, [G["g2"]])
            k.act(G["g3"][:, w], pz[:, w], AF.Square, [pz], [G["g3"]])
            k.stt(G["g3"][:, w], G["g3"][:, w], -1.0, pa[:, w], ALU.mult, ALU.add, [G["g3"], pa], [G["g3"]])
            k.ts("dve", G["g3"][:, w], G["g3"][:, w], 0.0, 64e-5, ALU.max, ALU.add, [G["g3"]], [G["g3"]])
            k.act(G["g3"][:, w], G["g3"][:, w], AF.Sqrt, [G["g3"]], [G["g3"]])
            k.op("dve", lambda e, w=w: e.reciprocal(out=G["g3"][:, w], in_=G["g3"][:, w]), [G["g3"]], [G["g3"]])
            k.stt(G["g2"][:, w], G["g2"][:, w], P(l, "rwg", hp), G["g3"][:, w], ALU.mult, ALU.mult,
                  [G["g2"], G["g3"], pp], [G["g2"]])
            k.stt(G["g1"][:, w], rT[:, sl], P(l, "rk", hp), kT[:, sl], ALU.mult, ALU.mult, [rT, kT, pp], [G["g1"]])
            k.mm(pz[:, w], C("blk64"), G["g1"][:, w], True, True, [cst, G["g1"]], [pz])
            k.tt("dve", G["g1"][:, w], pz[:, w], vT[:, sl], ALU.mult, [pz, vT], [G["g1"]])
            k.tt("pool", G["g2"][:, w], G["g2"][:, w], G["g1"][:, w], ALU.add, [G["g2"], G["g1"]], [G["g2"]])
            m_ = mo[bi % 2]
            k.tt("pool", m_[:, w], G["g2"][:, w], g_[:, w], ALU.mult, [G["g2"], g_], [m_])
            k.dma("pool", MIXT[1536 + hp * 128:1536 + (hp + 1) * 128, sl], m_[:, w], reads=[m_])
    k.pop()


def half_start(half):
    return 0 if half == 0 else 2304


def half_len(half):
    return 2304 if half == 0 else T - 2304


def _in_cols():
    idx = list(range(0, 2048))
    idx += list(range(2048, 2560))
    idx += list(range(2560, 2816))
    kr = list(range(2816, 2880))
    idx += kr + kr
    krp = [2816 + (i ^ 16) for i in range(64)]
    idx += krp + krp
    idx += list(range(2880, 3904))
    idx += list(range(3904, 5696))
    idx += list(range(5696, 6208))
    assert len(idx) == NCOLS
    return np.asarray(idx)


def _uq_cols():
    idx = []
    for h in range(8):
        b = h * 192
        idx += list(range(b, b + 128))
        idx += list(range(b + 128, b + 192))
        idx += [b + 128 + (i ^ 16) for i in range(64)]
    return np.asarray(idx)


def _rope_tables():
    inv = (10000.0 ** (-np.arange(16, dtype=np.float32) / 16)).astype(np.float32)
    t = np.arange(NLAT)
    row = (t // 64).astype(np.float32)
    col = (t % 64).astype(np.float32)
    cos = np.ones((64, T), np.float32)
    sin = np.zeros((64, T), np.float32)
    for i in range(64):
        pos = row if i < 32 else col
        ang = (pos * inv[i % 16]).astype(np.float32)
        sgn = -1.0 if (i % 32) < 16 else 1.0
        cos[i, NCTX:] = np.cos(ang)
        sin[i, NCTX:] = sgn * np.sin(ang)
    tq = np.concatenate([cos, sin], 0)
    return tq, np.concatenate([cos, cos], 0), np.concatenate([sin, sin], 0)


def _consts():
    c = np.zeros((128, NCS), np.float32)
    i = np.arange(128)
    jj, ii = np.meshgrid(i, i, indexing="ij")

    def put(n, a):
        o, w = CS[n]
        c[:, o:o + w] = a
    put("ident", np.eye(128))
    put("ones", np.ones((128, 128)))
    put("onesm", np.full((128, 128), 1.0 / 128))
    put("relf", np.maximum(ii - jj, 0))
    put("relb", np.maximum(jj - ii, 0))
    put("mf", (ii >= jj).astype(np.float32))
    put("mb", (jj >= ii).astype(np.float32))
    put("ip1", np.broadcast_to(i[None, :] + 1.0, (128, 128)))
    put("cmi", np.broadcast_to(128.0 - i[None, :], (128, 128)))
    put("blk64", np.kron(np.eye(2), np.ones((64, 64))))
    put("blk64m", np.kron(np.eye(2), np.ones((64, 64))) / 64.0)
    put("ones2k", np.full((128, 128), 1.0 / 2048))
    put("sf", (ii > jj).astype(np.float32))
    put("mf2", (ii >= jj).astype(np.float32))
    put("sb", (jj > ii).astype(np.float32))
    put("mb2", (jj >= ii).astype(np.float32))
    rm = np.ones((128, 512), np.float32)
    rm[:, ::128] = 0.0
    put("rm", rm)
    lm = np.zeros((128, 7 * 128), np.float32)
    lmt = np.zeros((128, 7 * 128), np.float32)
    for kk in range(7):
        mk = (((jj >> (kk + 1)) == (ii >> (kk + 1))) & (((jj >> kk) & 1) == 1) & (((ii >> kk) & 1) == 0)).astype(np.float32)
        lm[:, kk * 128:(kk + 1) * 128] = mk
        lmt[:, kk * 128:(kk + 1) * 128] = mk.T
    put("LM", lm)
    put("LMT", lmt)
    put("colj", i[:, None].astype(np.float32))
    put("colcj", (127.0 - i)[:, None])
    return c


def _chunks(v, n):
    return np.ascontiguousarray(np.asarray(v, np.float32).reshape(n, 128).T)


def prep_inputs(inp, b):
    f = lambda a: np.ascontiguousarray(a, dtype=np.float32)
    m = {}
    xt = np.concatenate([inp["ctx"][b], inp["x"][b]], 0)
    m["xT"] = f(xt.T)
    m["cc"] = f(np.stack([_chunks(inp["c"][b], 16), _chunks(inp["c_ctx"], 16)], -1).reshape(128, 32))
    m["wada"] = f(inp["w_ada"])
    m["bada"] = f(np.concatenate([_chunks(inp["b_ada"][l], 48) for l in range(2)], 1))
    m["win"] = f(inp["w_in"][:, :, _in_cols()])
    m["wuq"] = f(inp["mla_w_uq"][:, :, _uq_cols()])
    m["wukv"] = f(inp["mla_w_ukv"])
    m["wout"] = f(inp["w_out"])
    pp = np.zeros((128, 2 * NPP), np.float32)
    for l in range(2):
        def put(n, a):
            o, w = PP[n]
            pp[:, l * NPP + o: l * NPP + o + w] = a
        put("gq", _chunks(inp["mla_q_norm_g"][l], 4))
        put("gkv", _chunks(inp["mla_kv_norm_g"][l], 2))
        put("retg", _chunks(inp["ret_gn_g"][l], 4))
        put("lng", _chunks(inp["ln_g"][l], 16))
        put("lnb", _chunks(inp["ln_b"][l], 16))
        put("mu0", _chunks(inp["rwkv_shift_mu"][l, 0], 14))
        put("mu1", _chunks(inp["rwkv_shift_mu"][l, 1], 14))
        put("w0", np.concatenate([_chunks(inp["rwkv_w0"][l, d], 4) for d in range(2)], 1))
        put("a0", np.concatenate([_chunks(inp["rwkv_a0"][l, d], 4) for d in range(2)], 1))
        put("kk", _chunks(inp["rwkv_k_k"][l], 4))
        put("ka", _chunks(inp["rwkv_k_a"][l], 4))
        put("rk", _chunks(inp["rwkv_r_k"][l].reshape(-1), 4))
        put("rwg", _chunks(inp["rwkv_gn_g"][l], 4))
    m["pp"] = pp
    w2p = np.zeros((2, 2, 128, 512), np.float32)
    a2p = np.zeros((2, 2, 128, 512), np.float32)
    for l in range(2):
        for d in range(2):
            w2p[l, d, d * 64:(d + 1) * 64] = inp["rwkv_w2"][l, d]
            a2p[l, d, d * 64:(d + 1) * 64] = inp["rwkv_a2"][l, d]
    m["w2p"], m["a2p"] = w2p, a2p
    m["rdl"] = f(np.broadcast_to(inp["ret_decay_logit"].reshape(1, 16), (128, 16)))
    tq, tkc, tks = _rope_tables()
    m["tabQ"], m["tabKc"], m["tabKs"] = f(tq), f(tkc), f(tks)
    m["cst"] = _consts()
    return m


def kernel(**inputs):
    inp = {k_: np.asarray(v) for k_, v in inputs.items()}
    nc = build_program()
    in_maps = [prep_inputs(inp, c % 4) for c in range(4)]
    in_maps = in_maps + in_maps
    res = run_bass_kernel_spmd(nc, in_maps, core_ids=list(range(8)))
    out = np.stack([np.ascontiguousarray(res.results[b]["yT"].T) for b in range(4)], 0)
    return out.astype(np.float32)
```

```python
import numpy as np
import ml_dtypes
from contextlib import ExitStack
import concourse.bass as bass
import concourse.mybir as mybir
from concourse.bass_utils import run_bass_kernel_spmd

F32 = mybir.dt.float32
BF16 = mybir.dt.bfloat16
AF = mybir.ActivationFunctionType
ALU = mybir.AluOpType

D = 2048
NCTX = 256
NLAT = 4096
T = NCTX + NLAT
NCH = T // 128
NCOLS = 6400
NCC = NCOLS // 128
ALPHA = 4 ** 0.25
MLA_SCALE = 192 ** -0.5
TBLK = [(0, 256)] + [(256 + i * 512, 512) for i in range(8)]


class Res:
    __slots__ = ("lw", "rd", "dsem", "dval", "name", "psum")

    def __init__(self, name="", psum=False):
        self.psum = psum
        self.lw = None
        self.rd = {}
        self.dsem = None
        self.dval = 0
        self.name = name


class Tile:
    def __init__(self, t, name):
        self.t = t
        self.r = Res(name)

    def __getitem__(self, idx):
        return self.t[idx]


class KB:
    ENG = ("pe", "act", "dve", "pool", "sp")

    def __init__(self, nc):
        self.nc = nc
        self.eng = dict(pe=nc.tensor, act=nc.scalar, dve=nc.vector, pool=nc.gpsimd, sp=nc.sync)
        self.root = ExitStack()
        self.stacks = [self.root]
        self.sem = {e: self.root.enter_context(nc.semaphore("s_" + e)) for e in self.ENG}
        self.cnt = {e: 0 for e in self.ENG}
        self.waited = {e: {} for e in self.ENG}
        self.dma_owners = [[]]
        self.nid = 0
        self.sem_pool = []
        self.okey = 0

    def push(self):
        es = ExitStack()
        self.stacks.append(es)
        self.dma_owners.append([])

    def pop(self):
        self.barrier()
        self.dma_owners.pop()
        self.stacks.pop().close()

    def _name(self, n):
        self.nid += 1
        return f"{n}_{self.nid}"

    def sb(self, name, shape, dtype):
        t = self.stacks[-1].enter_context(self.nc.sbuf_tensor(self._name(name), list(shape), dtype))
        return Tile(t, name)

    def ps(self, name, shape=(128, 512), dtype=F32):
        t = self.stacks[-1].enter_context(self.nc.psum_tensor(self._name(name), list(shape), dtype))
        tl = Tile(t, name)
        tl.r.psum = True
        return tl

    def _wait(self, e, ev):
        if ev is None:
            return
        key, sem, val = ev
        if key == "pe" and e == "pe":
            return
        if self.waited[e].get(key, 0) >= val:
            return
        self.eng[e].wait_ge(sem, val)
        self.waited[e][key] = val

    def _deps(self, e, reads, writes):
        for r in reads:
            self._wait(e, r.lw)
            if r.psum:
                for key, ev in r.rd.items():
                    if key != e:
                        self._wait(e, ev)
        for r in writes:
            self._wait(e, r.lw)
            for ev in r.rd.values():
                self._wait(e, ev)

    def _mark(self, ev, reads, writes):
        for r in reads:
            r.rd[ev[0]] = ev
        for r in writes:
            r.lw = ev
            r.rd = {}

    def op(self, e, fn, reads=(), writes=()):
        reads = [x.r if isinstance(x, Tile) else x for x in reads]
        writes = [x.r if isinstance(x, Tile) else x for x in writes]
        self._deps(e, reads, writes)
        inst = fn(self.eng[e])
        self.cnt[e] += 1
        inst.then_inc(self.sem[e], 1)
        self._mark((e, self.sem[e], self.cnt[e]), reads, writes)

    def dma(self, q, out, in_, reads=(), writes=(), **kw):
        reads = [x.r if isinstance(x, Tile) else x for x in reads]
        writes = [x.r if isinstance(x, Tile) else x for x in writes]
        owner = writes[0] if writes else reads[0]
        self._deps(q, reads, writes)
        if owner.dsem is None:
            if self.sem_pool:
                owner.dsem, owner.dval = self.sem_pool.pop()
            else:
                owner.dsem = self.root.enter_context(self.nc.semaphore(self._name("d")))
                owner.dval = 0
            self.okey += 1
            owner.name = ("dma", self.okey)
            self.dma_owners[-1].append(owner)
        inst = self.eng[q].dma_start(out=out, in_=in_, **kw)
        owner.dval += 16
        inst.then_inc(owner.dsem, 16)
        self._mark((owner.name, owner.dsem, owner.dval), reads, writes)

    def barrier(self):
        evs = [(e, self.sem[e], self.cnt[e]) for e in self.ENG if self.cnt[e] > 0]
        for lst in self.dma_owners:
            for o in lst:
                evs.append((o.name, o.dsem, o.dval))
        for e in self.ENG:
            for ev in evs:
                if ev[0] == e:
                    continue
                if self.waited[e].get(ev[0], 0) >= ev[2]:
                    continue
                self.eng[e].wait_ge(ev[1], ev[2])
                self.waited[e][ev[0]] = ev[2]
        for o in self.dma_owners[-1]:
            for e in self.ENG:
                self.waited[e].pop(o.name, None)
            self.sem_pool.append((o.dsem, o.dval))
            o.dsem = None

    def mm(self, out, lhsT, rhs, start, stop, reads, writes):
        self.op("pe", lambda g: g.matmul(out, lhsT=lhsT, rhs=rhs, start=start, stop=stop), reads, writes)

    def tr(self, out, in_, ident, reads, writes):
        self.op("pe", lambda g: g.transpose(out, in_, ident), reads, writes)

    def act(self, out, in_, func, reads, writes, scale=1.0, bias=0.0, e="act"):
        self.op(e, lambda g: g.activation(out=out, in_=in_, func=func, bias=bias, scale=scale), reads, writes)

    def ts(self, e, out, in0, s1, s2, op0, op1, reads, writes):
        self.op(e, lambda g: g.tensor_scalar(out=out, in0=in0, scalar1=s1, scalar2=s2, op0=op0, op1=op1), reads, writes)

    def tt(self, e, out, in0, in1, op, reads, writes):
        self.op(e, lambda g: g.tensor_tensor(out=out, in0=in0, in1=in1, op=op), reads, writes)

    def stt(self, out, in0, scalar, in1, op0, op1, reads, writes):
        self.op("dve", lambda g: g.scalar_tensor_tensor(out=out, in0=in0, scalar=scalar, in1=in1, op0=op0, op1=op1),
                reads, writes)

    def cp(self, e, out, in_, reads, writes):
        if e == "act":
            self.op(e, lambda g: g.copy(out=out, in_=in_), reads, writes)
        else:
            self.op(e, lambda g: g.tensor_copy(out=out, in_=in_), reads, writes)


PP = {}
_o = 0
for _n, _w in [("gq", 4), ("gkv", 2), ("retg", 4), ("lng", 16), ("lnb", 16), ("mu0", 14), ("mu1", 14),
               ("w0", 8), ("a0", 8), ("kk", 4), ("ka", 4), ("rk", 4), ("rwg", 4)]:
    PP[_n] = (_o, _w)
    _o += _w
NPP = _o

CS = {}
_o = 0
for _n, _w in [("ident", 128), ("ones", 128), ("onesm", 128), ("relf", 128), ("relb", 128), ("mf", 128), ("mb", 128),
               ("ip1", 128), ("cmi", 128), ("blk64", 128), ("blk64m", 128), ("ones2k", 128),
               ("sf", 128), ("mf2", 128), ("sb", 128), ("mb2", 128), ("rm", 512), ("LM", 7 * 128), ("LMT", 7 * 128),
               ("colj", 1), ("colcj", 1)]:
    CS[_n] = (_o, _w)
    _o += _w
NCS = _o


def build_program(n_layers=2, dbg=()):
    nc = bass.Bass("TRN2", target_bir_lowering=False)
    k = KB(nc)

    def din(name, shape, dt=F32):
        return nc.dram_tensor(name, list(shape), dt, kind="ExternalInput").ap()

    xT = din("xT", [D, T])
    cc = din("cc", [128, 32])
    wada = din("wada", [2, D, 6144])
    bada = din("bada", [128, 96])
    win = din("win", [2, D, NCOLS])
    wuq = din("wuq", [2, 512, 2048])
    wukv = din("wukv", [2, 256, 2048])
    wout = din("wout", [2, D, D])
    ppd = din("pp", [128, 2 * NPP])
    rdl = din("rdl", [128, 16])
    w2p = din("w2p", [2, 2, 128, 512])
    a2p = din("a2p", [2, 2, 128, 512])
    tabQ = din("tabQ", [128, T])
    tabKc = din("tabKc", [128, T])
    tabKs = din("tabKs", [128, T])
    cstd = din("cst", [128, NCS])
    yT = nc.dram_tensor("yT", [D, NLAT], F32, kind="ExternalOutput").ap()
    PT = nc.dram_tensor("PT", [NCOLS, T], F32, kind="Internal").ap()
    MIXT = nc.dram_tensor("MIXT", [D, T], BF16, kind="Internal").ap()
    XN = nc.dram_tensor("XN", [D, T], F32, kind="Internal").ap()
    dbg_out = {}
    if "mod" in dbg:
        dbg_out["mod"] = nc.dram_tensor("dbg_mod", [128, 192], F32, kind="ExternalOutput").ap()
    if "PT" in dbg:
        dbg_out["PT"] = nc.dram_tensor("dbg_PT", [NCOLS, T], F32, kind="ExternalOutput").ap()
        PT = dbg_out["PT"]
    if "PTin" in dbg:
        PT = nc.dram_tensor("PTin", [NCOLS, T], F32, kind="ExternalInput").ap()
    if "MIXT" in dbg:
        dbg_out["MIXT"] = nc.dram_tensor("dbg_MIXT", [D, T], F32, kind="ExternalOutput").ap()
        MIXT = dbg_out["MIXT"]
    if "XN" in dbg:
        dbg_out["XN"] = nc.dram_tensor("dbg_XN", [D, T], F32, kind="ExternalOutput").ap()
        XN = dbg_out["XN"]

    mod = k.sb("mod", [128, 192], F32)
    pp = k.sb("pp", [128, 2 * NPP], F32)
    cst = k.sb("cst", [128, NCS], F32)
    k.dma("sp", pp[:], ppd, writes=[pp])
    k.dma("sp", cst[:], cstd, writes=[cst])

    def C(name):
        o, w = CS[name]
        return cst[:, o:o + w]

    def P(l, name, j=0):
        o, w = PP[name]
        return pp[:, l * NPP + o + j: l * NPP + o + j + 1]

    def modc(l, j, r):
        return mod[:, l * 96 + j * 2 + r: l * 96 + j * 2 + r + 1]

    k.push()
    if "PTin" in dbg:
        n_layers_mod = 0
    else:
        n_layers_mod = n_layers
    cct = k.sb("cct", [128, 32], F32)
    sc = k.sb("sc", [128, 32], F32)
    bad = k.sb("bad", [128, 96], F32)
    k.dma("sp", cct[:], cc, writes=[cct])
    k.dma("sp", bad[:], bada, writes=[bad])
    k.act(sc[:], cct[:], AF.Silu, [cct], [sc])
    wa = [k.sb("wa", [128, 16, 512], F32) for _ in range(2)]
    pm = [k.ps("pm") for _ in range(2)]
    it = 0
    for l in range(n_layers_mod):
        for mc in range(12):
            w = wa[it % 2]
            src = wada[l, :, mc * 512:(mc + 1) * 512].rearrange("(kc p) c -> p kc c", p=128)
            k.dma("sp", w[:, 0:8, :], src[:, 0:8, :], writes=[w])
            k.dma("sp", w[:, 8:16, :], src[:, 8:16, :], writes=[w])
            for j in range(4):
                p_ = pm[(it * 4 + j) % 2]
                for kc in range(16):
                    k.mm(p_[:, 0:2], w[:, kc, j * 128:(j + 1) * 128], sc[:, kc * 2:kc * 2 + 2], kc == 0, kc == 15,
                         [w, sc], [p_])
                jj = mc * 4 + j
                k.ts("dve", mod[:, l * 96 + jj * 2: l * 96 + jj * 2 + 2], p_[:, 0:2],
                     bad[:, l * 48 + jj: l * 48 + jj + 1], None, ALU.add, ALU.bypass, [p_, bad], [mod])
            it += 1
        k.ts("dve", mod[:, l * 96 + 32: l * 96 + 64], mod[:, l * 96 + 32: l * 96 + 64], 1.0, None, ALU.add, ALU.bypass,
             [mod], [mod])
    if "mod" in dbg:
        k.dma("sp", dbg_out["mod"], mod[:], reads=[mod])
    k.pop()

    for l in range(n_layers):
        xsrc = xT if l == 0 else XN
        last = (l == n_layers - 1)
        if "PTin" not in dbg:
            phase_inproj(k, l, xsrc, win, PT, modc)
        if "PT" in dbg:
            break
        env = dict(C=C, P=P, cst=cst, pp=pp, mod=mod, modc=modc)
        if "nomla" not in dbg:
            phase_mla(k, l, PT, MIXT, wuq, wukv, tabQ, tabKc, tabKs, env, not last)
        if "noret" not in dbg:
            phase_ret(k, l, PT, MIXT, rdl, env, not last)
        if "norw" not in dbg:
            phase_rwkv(k, l, PT, MIXT, w2p, a2p, env, not last)
        if "MIXT" in dbg:
            break
        phase_out(k, l, xsrc, MIXT, wout, yT if last else XN, env, last)

    k.barrier()
    k.root.close()
    return nc


def phase_inproj(k, l, xsrc, win, PT, modc):
    k.push()
    HT = k.sb("HT", [128, 16, T], BF16)
    k.push()
    xs = [k.sb("xs", [128, 2176], F32) for _ in range(2)]
    it = 0
    for kc in range(16):
        for half in range(2):
            x_ = xs[it % 2]
            it += 1
            t0 = half * 2176
            k.dma("sp", x_[:], xsrc[kc * 128:(kc + 1) * 128, t0:t0 + 2176], writes=[x_])
            def affine(out_, in_, r_):
                if it % 2:
                    k.ts("dve", out_, in_, modc(l, 16 + kc, r_), modc(l, kc, r_), ALU.mult, ALU.add, [x_], [HT])
                else:
                    k.act(out_, in_, AF.Identity, [x_], [HT], scale=modc(l, 16 + kc, r_), bias=modc(l, kc, r_))
            if half == 0:
                affine(HT[:, kc, 0:NCTX], x_[:, 0:NCTX], 1)
                affine(HT[:, kc, NCTX:2176], x_[:, NCTX:2176], 0)
            else:
                affine(HT[:, kc, 2176:T], x_[:], 0)
    k.pop()
    wf = [k.sb("wf", [128, 16, 128], F32) for _ in range(2)]
    wb = [k.sb("wb", [128, 16, 128], BF16) for _ in range(2)]
    og = [k.sb("og", [128, 2304], F32) for _ in range(2)]
    pp_ = [k.ps("pi") for _ in range(4)]
    GATE = set(range(12, 16)) | set(range(24, 32)) | set(range(46, 50))
    pi = 0
    oi = 0
    for c in range(NCC):
        w_f, w_b = wf[c % 2], wb[c % 2]
        k.dma("sp", w_f[:], win[l, :, c * 128:(c + 1) * 128].rearrange("(kc p) c -> p kc c", p=128), writes=[w_f])
        k.cp("pool", w_b[:], w_f[:], [w_f], [w_b])
        for half in range(2):
            o_ = og[oi % 2]
            oi += 1
            blocks = [b for b in TBLK if (b[0] < 2176) == (half == 0)]
            for (t0, tl) in blocks:
                p_ = pp_[pi % 4]
                pi += 1
                for kc in range(16):
                    k.mm(p_[:, 0:tl], w_b[:, kc, :], HT[:, kc, t0:t0 + tl], kc == 0, kc == 15, [w_b, HT], [p_])
                so = t0 - half_start(half)
                fn = AF.Silu if c in GATE else AF.Copy
                if fn == AF.Copy and (pi % 2):
                    k.cp("dve", o_[:, so:so + tl], p_[:, 0:tl], [p_], [o_])
                else:
                    k.act(o_[:, so:so + tl], p_[:, 0:tl], fn, [p_], [o_])
            hs = half_start(half)
            hl = half_len(half)
            k.dma("pool", PT[c * 128:(c + 1) * 128, hs:hs + hl], o_[:, 0:hl], reads=[o_])
    k.pop()


def _v3(ap, c=128):
    return ap.rearrange("p (n c) -> p n c", c=c)


def group_norm_fm(k, env, pO, tl, lhs_mean, eps, gcol, tmp, e2="pool"):
    C, cst = env["C"], env["cst"]
    y, ysq, d, m2, pm, pq = tmp
    k.cp("act", y[:, 0:tl], pO[:, 0:tl], [pO], [y])
    k.act(ysq[:, 0:tl], pO[:, 0:tl], AF.Square, [pO], [ysq])
    k.mm(pm[:, 0:tl], lhs_mean, y[:, 0:tl], True, True, [cst, y], [pm])
    k.mm(pq[:, 0:tl], lhs_mean, ysq[:, 0:tl], True, True, [cst, ysq], [pq])
    k.tt("dve", d[:, 0:tl], y[:, 0:tl], pm[:, 0:tl], ALU.subtract, [y, pm], [d])
    k.act(m2[:, 0:tl], pm[:, 0:tl], AF.Square, [pm], [m2])
    k.stt(m2[:, 0:tl], m2[:, 0:tl], -1.0, pq[:, 0:tl], ALU.mult, ALU.add, [m2, pq], [m2])
    k.ts("dve", m2[:, 0:tl], m2[:, 0:tl], 0.0, eps, ALU.max, ALU.add, [m2], [m2])
    k.act(m2[:, 0:tl], m2[:, 0:tl], AF.Sqrt, [m2], [m2])
    k.op("dve", lambda g: g.reciprocal(out=m2[:, 0:tl], in_=m2[:, 0:tl]), [m2], [m2])
    k.stt(d[:, 0:tl], d[:, 0:tl], gcol, m2[:, 0:tl], ALU.mult, ALU.mult, [d, m2, env["pp"]], [d])
    return d


def phase_mla(k, l, PT, MIXT, wuq, wukv, tabQ, tabKc, tabKs, env, need_ctx):
    C, P, cst, pp = env["C"], env["P"], env["cst"], env["pp"]
    k.push()
    cqn = k.sb("cqn", [128, 4, T], BF16)
    ckvn = k.sb("ckvn", [128, 2, T], BF16)
    KR = k.sb("KR", [128, T], BF16)
    tq = k.sb("tq", [128, T], F32)
    k.dma("sp", tq[:], tabQ, writes=[tq])
    wq_b = k.sb("wq_b", [128, 4, 2048], BF16)
    wkv_b = k.sb("wkv_b", [128, 2, 2048], BF16)
    onesb = k.sb("onesb", [128, 128], BF16)
    k.cp("dve", onesb[:], C("ones"), [cst], [onesb])
    k.push()
    wst = [k.sb("wst", [128, 2048], F32) for _ in range(2)]
    for i in range(6):
        w = wst[i % 2]
        src = wuq[l, i * 128:(i + 1) * 128, :] if i < 4 else wukv[l, (i - 4) * 128:(i - 3) * 128, :]
        k.dma("sp", w[:], src, writes=[w])
        if i < 4:
            k.cp("pool", wq_b[:, i, :], w[:], [w], [wq_b])
        else:
            k.cp("pool", wkv_b[:, i - 4, :], w[:], [w], [wkv_b])
    xin = [k.sb("xin", [128, 8, 512], F32) for _ in range(2)]
    tk = [k.sb("tk", [128, 2, 512], F32) for _ in range(2)]
    sq = k.sb("sq", [128, 6, 512], F32)
    rs = k.sb("rs", [128, 2, 512], F32)
    t1 = k.sb("t1", [128, 512], F32)
    t2 = k.sb("t2", [128, 512], F32)
    psq = [k.ps("psq") for _ in range(2)]
    for bi, (t0, tl) in enumerate(TBLK):
        x_, tk_ = xin[bi % 2], tk[bi % 2]
        k.dma("sp", x_[:, :, 0:tl], PT[16 * 128:24 * 128, t0:t0 + tl].rearrange("(c p) t -> p c t", p=128), writes=[x_])
        k.dma("sp", tk_[:, 0, 0:tl], tabKc[:, t0:t0 + tl], writes=[tk_])
        k.dma("sp", tk_[:, 1, 0:tl], tabKs[:, t0:t0 + tl], writes=[tk_])
        k.act(sq[:, :, 0:tl], x_[:, 0:6, 0:tl], AF.Square, [x_], [sq])
        for g, (c0, n) in enumerate([(0, 4), (4, 2)]):
            for j in range(n):
                k.mm(psq[g][:, 0:tl], C("ones"), sq[:, c0 + j, 0:tl], j == 0, j == n - 1, [cst, sq], [psq[g]])
            k.ts("dve", rs[:, g, 0:tl], psq[g][:, 0:tl], 1.0 / (n * 128), 1e-6, ALU.mult, ALU.add, [psq[g]], [rs])
            k.act(rs[:, g, 0:tl], rs[:, g, 0:tl], AF.Sqrt, [rs], [rs])
            k.op("dve", lambda e, g=g: e.reciprocal(out=rs[:, g, 0:tl], in_=rs[:, g, 0:tl]), [rs], [rs])
            for j in range(n):
                if g == 0:
                    k.stt(cqn[:, j, t0:t0 + tl], x_[:, j, 0:tl], P(l, "gq", j), rs[:, 0, 0:tl], ALU.mult, ALU.mult,
                          [x_, rs, pp], [cqn])
                else:
                    k.stt(ckvn[:, j, t0:t0 + tl], x_[:, 4 + j, 0:tl], P(l, "gkv", j), rs[:, 1, 0:tl], ALU.mult, ALU.mult,
                          [x_, rs, pp], [ckvn])
        k.tt("pool", t1[:, 0:tl], x_[:, 6, 0:tl], tk_[:, 0, 0:tl], ALU.mult, [x_, tk_], [t1])
        k.tt("dve", t2[:, 0:tl], x_[:, 7, 0:tl], tk_[:, 1, 0:tl], ALU.mult, [x_, tk_], [t2])
        k.tt("dve", KR[:, t0:t0 + tl], t1[:, 0:tl], t2[:, 0:tl], ALU.add, [t1, t2], [KR])
    k.pop()
    KN = k.sb("KN", [128, T], BF16)
    V = k.sb("V", [128, NCH, 128], BF16)
    QN = k.sb("QN", [128, T], BF16)
    QR = k.sb("QR", [128, T], BF16)
    pS = [k.ps("pS") for _ in range(3)]
    pO = k.ps("pO")
    pD = k.ps("pD")
    pA = [k.ps("pA") for _ in range(2)]
    Pt = [k.sb("Pt", [128, 512], BF16) for _ in range(4)]
    gt = [k.sb("gt", [128, 512], F32) for _ in range(2)]
    rd = k.sb("rd", [128, 512], F32)
    ot = k.sb("ot", [128, 512], F32)
    mo = [k.sb("mo", [128, 512], MIXT.dtype) for _ in range(2)]
    ai = 0
    for h in range(8):
        for (t0, tl) in TBLK:
            p_ = pA[ai % 2]
            ai += 1
            for kc in range(2):
                k.mm(p_[:, 0:tl], wkv_b[:, kc, h * 256:h * 256 + 128], ckvn[:, kc, t0:t0 + tl], kc == 0, kc == 1,
                     [wkv_b, ckvn], [p_])
            k.cp("dve", KN[:, t0:t0 + tl], p_[:, 0:tl], [p_], [KN])
            p_ = pA[ai % 2]
            ai += 1
            for kc in range(4):
                k.mm(p_[:, 0:tl], wq_b[:, kc, h * 256:h * 256 + 128], cqn[:, kc, t0:t0 + tl], kc == 0, kc == 3,
                     [wq_b, cqn], [p_])
            k.cp("act", QN[:, t0:t0 + tl], p_[:, 0:tl], [p_], [QN])
            p_ = pA[ai % 2]
            ai += 1
            for kc in range(4):
                k.mm(p_[:, 0:tl], wq_b[:, kc, h * 256 + 128:h * 256 + 256], cqn[:, kc, t0:t0 + tl], kc == 0, kc == 3,
                     [wq_b, cqn], [p_])
            k.tt("dve", QR[:, t0:t0 + tl], p_[:, 0:tl], tq[:, t0:t0 + tl], ALU.mult, [p_, tq], [QR])
        for n4 in range(0, NCH, 4):
            p_ = pA[ai % 2]
            ai += 1
            nn = min(4, NCH - n4)
            for j in range(nn):
                n = n4 + j
                for kc in range(2):
                    k.mm(p_[:, j * 128:(j + 1) * 128], ckvn[:, kc, n * 128:(n + 1) * 128],
                         wkv_b[:, kc, h * 256 + 128:h * 256 + 256], kc == 0, kc == 1, [ckvn, wkv_b], [p_])
            k.cp("act", V[:, n4:n4 + nn, :], _v3(p_[:, 0:nn * 128]), [p_], [V])
        for bi, (t0, tl) in enumerate(TBLK):
            if bi == 0 and not need_ctx:
                continue
            kbs = list(range(0, 2)) if bi == 0 else list(range(NCH))
            g_ = gt[bi % 2]
            k.dma("sp", g_[:, 0:tl], PT[(24 + h) * 128:(25 + h) * 128, t0:t0 + tl], writes=[g_])
            nk = len(kbs)

            def s_step(ii):
                kb = kbs[ii]
                s_ = pS[ii % 3]
                k.mm(s_[:, 0:tl], KN[:, kb * 128:(kb + 1) * 128], QN[:, t0:t0 + tl], True, False, [KN, QN], [s_])
                k.mm(s_[:, 0:tl], KR[:, kb * 128:(kb + 1) * 128], QR[:, t0:t0 + tl], False, True, [KR, QR], [s_])
                p_ = Pt[ii % 4]
                k.act(p_[:, 0:tl], s_[:, 0:tl], AF.Exp, [s_], [p_], scale=MLA_SCALE)

            def od_step(ii):
                kb = kbs[ii]
                p_ = Pt[ii % 4]
                k.mm(pO[:, 0:tl], V[:, kb, :], p_[:, 0:tl], ii == 0, ii == nk - 1, [V, p_], [pO])
                k.mm(pD[:, 0:tl], onesb[:], p_[:, 0:tl], ii == 0, ii == nk - 1, [onesb, p_], [pD])

            for ii in range(min(2, nk)):
                s_step(ii)
            for ii in range(nk):
                if ii + 2 < nk:
                    s_step(ii + 2)
                od_step(ii)
            k.op("dve", lambda e, tl=tl: e.reciprocal(out=rd[:, 0:tl], in_=pD[:, 0:tl]), [pD], [rd])
            k.tt("dve", ot[:, 0:tl], pO[:, 0:tl], rd[:, 0:tl], ALU.mult, [pO, rd], [ot])
            m_ = mo[bi % 2]
            k.tt("pool", m_[:, 0:tl], ot[:, 0:tl], g_[:, 0:tl], ALU.mult, [ot, g_], [m_])
            k.dma("pool", MIXT[512 + h * 128:512 + (h + 1) * 128, t0:t0 + tl], m_[:, 0:tl], reads=[m_])
    k.pop()


def phase_ret(k, l, PT, MIXT, rdl, env, need_ctx):
    C, P, cst, pp = env["C"], env["P"], env["cst"], env["pp"]
    k.push()
    lg = k.sb("lg", [128, 8], F32)
    rdt = k.sb("rdt", [128, 16], F32)
    k.dma("sp", rdt[:], rdl, writes=[rdt])
    k.act(lg[:], rdt[:, l * 8:(l + 1) * 8], AF.Exp, [rdt], [lg], scale=-1.0)
    k.ts("dve", lg[:], lg[:], 1.0, None, ALU.add, ALU.bypass, [lg], [lg])
    k.act(lg[:], lg[:], AF.Ln, [lg], [lg])
    k.ts("dve", lg[:], lg[:], -1.0, None, ALU.mult, ALU.bypass, [lg], [lg])
    mask = k.sb("mask", [128, 128], F32)
    mt = k.sb("mt", [128, 128], F32)
    xi = [k.sb("xi", [128, 128], F32) for _ in range(2)]
    zeta = k.sb("zeta", [128, 2], F32)
    gC = k.sb("gC", [128, 2], F32)
    qT, kT, vT, sg = [k.sb(n, [128, T], F32) for n in ("qT", "kT", "vT", "sg")]
    qb, kb_, qxf, qxb = [k.sb(n, [128, T], BF16) for n in ("qb", "kb", "qxf", "qxb")]
    knf, knb, vn, SPf, SPb = [k.sb(n, [128, NCH, 128], BF16) for n in ("knf", "knb", "vn", "SPf", "SPb")]
    st = [k.sb("st", [128, 128], F32) for _ in range(2)]
    SM = k.sb("SM", [128, 4, 128], BF16)
    tmp = [k.sb("gn", [128, 512], F32) for _ in range(4)]
    mo = [k.sb("mo", [128, 512], MIXT.dtype) for _ in range(2)]
    pT = [k.ps("pT") for _ in range(2)]
    pU = [k.ps("pU") for _ in range(2)]
    pS = k.ps("pS")
    pO = k.ps("pO")
    pm, pq = k.ps("pm"), k.ps("pq")
    sc = 128 ** -0.5
    ui = 0
    import os
    STOP = int(os.environ.get("RET_STOP", "99"))
    for h in range(4):
        lgf, lgb = lg[:, h:h + 1], lg[:, 4 + h:5 + h]
        k.act(mask[:], C("relf"), AF.Exp, [cst, lg], [mask], scale=lgf)
        k.tt("dve", mask[:], mask[:], C("mf"), ALU.mult, [mask, cst], [mask])
        k.act(mt[:], C("relb"), AF.Exp, [cst, lg], [mt], scale=lgb)
        k.tt("dve", mt[:], mt[:], C("mb"), ALU.mult, [mt, cst], [mt])
        k.tt("dve", mask[:], mask[:], mt[:], ALU.add, [mask, mt], [mask])
        k.ts("dve", mask[:], mask[:], sc, None, ALU.mult, ALU.bypass, [mask], [mask])
        k.act(xi[0][:], C("ip1"), AF.Exp, [cst, lg], [xi[0]], scale=lgf)
        k.act(xi[1][:], C("cmi"), AF.Exp, [cst, lg], [xi[1]], scale=lgb)
        k.act(zeta[:, 0:1], C("colcj"), AF.Exp, [cst, lg], [zeta], scale=lgf)
        k.act(zeta[:, 1:2], C("colj"), AF.Exp, [cst, lg], [zeta], scale=lgb)
        k.ts("dve", zeta[:], zeta[:], sc, None, ALU.mult, ALU.bypass, [zeta], [zeta])
        k.act(gC[:, 0:1], lgf, AF.Exp, [lg], [gC], scale=128.0)
        k.act(gC[:, 1:2], lgb, AF.Exp, [lg], [gC], scale=128.0)
        if STOP <= 1:
            continue
        for j, tile in enumerate([qT, kT, vT, sg]):
            r0 = (j * 4 + h) * 128
            k.dma("sp", tile[:, 0:2176], PT[r0:r0 + 128, 0:2176], writes=[tile])
            k.dma("sp", tile[:, 2176:T], PT[r0:r0 + 128, 2176:T], writes=[tile])
        k.cp("act", qb[:], qT[:], [qT], [qb])
        k.cp("dve", kb_[:], kT[:], [kT], [kb_])
        k.tt("dve", _v3(qxf[:]), _v3(qT[:]), xi[0][:].unsqueeze(1).broadcast_to([128, NCH, 128]), ALU.mult,
             [qT, xi[0]], [qxf])
        k.tt("dve", _v3(qxb[:]), _v3(qT[:]), xi[1][:].unsqueeze(1).broadcast_to([128, NCH, 128]), ALU.mult,
             [qT, xi[1]], [qxb])
        if STOP <= 2:
            continue
        for n4 in range(0, NCH, 4):
            nn = min(4, NCH - n4)
            pk, pv = pT
            for j in range(nn):
                n = n4 + j
                k.tr(pk[:, j * 128:(j + 1) * 128], kT[:, n * 128:(n + 1) * 128], C("ident"), [kT, cst], [pk])
                k.tr(pv[:, j * 128:(j + 1) * 128], vT[:, n * 128:(n + 1) * 128], C("ident"), [vT, cst], [pv])
            k.ts("dve", knf[:, n4:n4 + nn, :], _v3(pk[:, 0:nn * 128]), zeta[:, 0:1], None, ALU.mult, ALU.bypass,
                 [pk, zeta], [knf])
            k.ts("dve", knb[:, n4:n4 + nn, :], _v3(pk[:, 0:nn * 128]), zeta[:, 1:2], None, ALU.mult, ALU.bypass,
                 [pk, zeta], [knb])
            k.cp("act", vn[:, n4:n4 + nn, :], _v3(pv[:, 0:nn * 128]), [pv], [vn])
        if STOP <= 3:
            continue
        for d, order in enumerate([list(range(NCH)), [1, 0] + list(range(NCH - 1, 1, -1))]):
            kn = knf if d == 0 else knb
            SP = SPf if d == 0 else SPb
            s_ = st[d]
            k.op("dve", lambda e, s_=s_: e.memset(s_[:], 0.0), [], [s_])
            for n in order:
                k.cp("act", SP[:, n, :], s_[:], [s_], [SP])
                pu = pU[ui % 2]
                ui += 1
                k.mm(pu[:, 0:128], kn[:, n, :], vn[:, n, :], True, True, [kn, vn], [pu])
                k.stt(s_[:], s_[:], gC[:, d:d + 1], pu[:, 0:128], ALU.mult, ALU.add, [s_, gC, pu], [s_])
        if STOP <= 4:
            continue
        for bi, (t0, tl) in enumerate(TBLK):
            if bi == 0 and not need_ctx:
                continue
            nn, n0 = tl // 128, t0 // 128
            for j in range(nn):
                n = n0 + j
                k.mm(pS[:, j * 128:(j + 1) * 128], kb_[:, n * 128:(n + 1) * 128], qb[:, n * 128:(n + 1) * 128], True, True,
                     [kb_, qb], [pS])
            k.tt("dve", SM[:, 0:nn, :], _v3(pS[:, 0:tl]), mask[:].unsqueeze(1).broadcast_to([128, nn, 128]), ALU.mult,
                 [pS, mask], [SM])
            for j in range(nn):
                n = n0 + j
                cs = slice(j * 128, (j + 1) * 128)
                k.mm(pO[:, cs], vn[:, n, :], SM[:, j, :], True, False, [vn, SM], [pO])
                k.mm(pO[:, cs], SPf[:, n, :], qxf[:, n * 128:(n + 1) * 128], False, False, [SPf, qxf], [pO])
                k.mm(pO[:, cs], SPb[:, n, :], qxb[:, n * 128:(n + 1) * 128], False, True, [SPb, qxb], [pO])
            d_ = group_norm_fm(k, env, pO, tl, C("onesm"), 1e-5, P(l, "retg", h), tmp + [pm, pq])
            m_ = mo[bi % 2]
            k.tt("pool", m_[:, 0:tl], d_[:, 0:tl], sg[:, t0:t0 + tl], ALU.mult, [d_, sg], [m_])
            k.dma("pool", MIXT[h * 128:(h + 1) * 128, t0:t0 + tl], m_[:, 0:tl], reads=[m_])
    k.pop()


def phase_out(k, l, xsrc, MIXT, wout, dst, env, last):
    C, P, cst, pp, mod, modc = env["C"], env["P"], env["cst"], env["pp"], env["mod"], env["modc"]
    k.push()
    wo = k.sb("wo", [128, 16, 2048], BF16)
    k.push()
    wst = [k.sb("wst", [128, 2048], F32) for _ in range(2)]
    for kc in range(16):
        w = wst[kc % 2]
        k.dma("sp", w[:], wout[l, kc * 128:(kc + 1) * 128, :], writes=[w])
        k.cp("pool" if kc % 2 else "act", wo[:, kc, :], w[:], [w], [wo])
    k.pop()
    mx = [k.sb("mx", [128, 16, 512], BF16) for _ in range(2)]
    xz = [k.sb("xz", [128, 16, 512], F32) for _ in range(2)]
    tt_ = [k.sb("tt", [128, 512], F32) for _ in range(2)]
    sq_ = [k.sb("sq", [128, 512], F32) for _ in range(2)]
    mean, rs, dd = [k.sb(n, [128, 512], F32) for n in ("mean", "rs", "dd")]
    pz = [k.ps("pz") for _ in range(3)]
    pm_, pq_ = k.ps("pm"), k.ps("pq")
    blocks = TBLK[1:] if last else TBLK
    for bi, (t0, tl) in enumerate(blocks):
        m_, x_ = mx[bi % 2], xz[bi % 2]
        k.dma("sp", m_[:, :, 0:tl], MIXT[:, t0:t0 + tl].rearrange("(kc p) t -> p kc t", p=128), writes=[m_])
        k.dma("sp", x_[:, :, 0:tl], xsrc[:, t0:t0 + tl].rearrange("(kc p) t -> p kc t", p=128), writes=[x_])
        r = 1 if t0 < NCTX else 0
        for dc in range(16):
            p_ = pz[dc % 3]
            for kc in range(16):
                k.mm(p_[:, 0:tl], wo[:, kc, dc * 128:(dc + 1) * 128], m_[:, kc, 0:tl], kc == 0, kc == 15, [wo, m_], [p_])
            t_ = tt_[dc % 2]
            s_ = sq_[dc % 2]
            k.act(t_[:, 0:tl], p_[:, 0:tl], AF.Identity, [p_, mod], [t_], scale=modc(l, 32 + dc, r))
            k.stt(x_[:, dc, 0:tl], x_[:, dc, 0:tl], ALPHA, t_[:, 0:tl], ALU.mult, ALU.add, [x_, t_], [x_])
            k.act(s_[:, 0:tl], x_[:, dc, 0:tl], AF.Square, [x_], [s_])
            k.mm(pm_[:, 0:tl], C("ones2k"), x_[:, dc, 0:tl], dc == 0, dc == 15, [cst, x_], [pm_])
            k.mm(pq_[:, 0:tl], C("ones2k"), s_[:, 0:tl], dc == 0, dc == 15, [cst, s_], [pq_])
        k.cp("act", mean[:, 0:tl], pm_[:, 0:tl], [pm_], [mean])
        k.act(rs[:, 0:tl], pm_[:, 0:tl], AF.Square, [pm_], [rs])
        k.stt(rs[:, 0:tl], rs[:, 0:tl], -1.0, pq_[:, 0:tl], ALU.mult, ALU.add, [rs, pq_], [rs])
        k.ts("dve", rs[:, 0:tl], rs[:, 0:tl], 0.0, 1e-5, ALU.max, ALU.add, [rs], [rs])
        k.act(rs[:, 0:tl], rs[:, 0:tl], AF.Sqrt, [rs], [rs])
        k.op("dve", lambda e, tl=tl: e.reciprocal(out=rs[:, 0:tl], in_=rs[:, 0:tl]), [rs], [rs])
        for dc in range(16):
            k.tt("dve", dd[:, 0:tl], x_[:, dc, 0:tl], mean[:, 0:tl], ALU.subtract, [x_, mean], [dd])
            k.tt("pool", dd[:, 0:tl], dd[:, 0:tl], rs[:, 0:tl], ALU.mult, [dd, rs], [dd])
            k.act(x_[:, dc, 0:tl], dd[:, 0:tl], AF.Identity, [dd, pp], [x_], scale=P(l, "lng", dc), bias=P(l, "lnb", dc))
        o0 = t0 - NCTX if last else t0
        k.dma("pool", dst[:, o0:o0 + tl].rearrange("(kc p) t -> p kc t", p=128), x_[:, :, 0:tl], reads=[x_])
    k.pop()


RBLK = list(TBLK)


def phase_rwkv(k, l, PT, MIXT, w2p, a2p, env, need_ctx):
    C, P, cst, pp = env["C"], env["P"], env["cst"], env["pp"]
    k.push()
    rT, kT, vT, kkT, YT = [k.sb(n, [128, T], F32) for n in ("rT", "kT", "vT", "kkT", "YT")]
    wlT, alT = [k.sb(n, [128, T], BF16) for n in ("wlT", "alT")]
    w2t = k.sb("w2t", [128, 128], BF16)
    a2t = k.sb("a2t", [128, 128], BF16)
    w2f = k.sb("w2f", [128, 128], F32)
    a2f = k.sb("a2f", [128, 128], F32)
    muc = k.sb("muc", [128, 14], F32)
    o0, o1 = PP["mu0"][0] + l * NPP, PP["mu1"][0] + l * NPP
    k.tt("dve", muc[:], pp[:, o0:o0 + 14], pp[:, o1:o1 + 14], ALU.add, [pp], [muc])
    k.ts("dve", muc[:], muc[:], -1.0, 1.0, ALU.mult, ALU.add, [muc], [muc])
    M4 = [k.sb("M4", [128, 512], F32) for _ in range(2)]
    MS = [None, None]
    for d, (a, b) in enumerate((("sf", "mf2"), ("sb", "mb2"))):
        for q in range(2):
            k.cp("dve", M4[d][:, q * 256:q * 256 + 128], C(a), [cst], [M4[d]])
            k.cp("dve", M4[d][:, q * 256 + 128:q * 256 + 256], C(b), [cst], [M4[d]])
    MS[0], MS[1] = C("sb"), C("sf")
    raw = YT

    def load_shift(c, dst):
        r0 = (32 + c) * 128
        k.dma("sp", raw[:, 0:2176], PT[r0:r0 + 128, 0:2176], writes=[raw])
        k.dma("sp", raw[:, 2176:T], PT[r0:r0 + 128, 2176:T], writes=[raw])
        k.act(dst[:], raw[:], AF.Identity, [raw, muc], [dst], scale=muc[:, c:c + 1])
        for (a, b) in ((0, NCTX), (NCTX, T)):
            k.stt(dst[:, a + 1:b], raw[:, a:b - 1], P(l, "mu0", c), dst[:, a + 1:b], ALU.mult, ALU.add, [raw, dst, pp], [dst])
            k.stt(dst[:, a:b - 1], raw[:, a + 1:b], P(l, "mu1", c), dst[:, a:b - 1], ALU.mult, ALU.add, [raw, dst, pp], [dst])

    load_shift(12, wlT)
    k.act(wlT[:], wlT[:], AF.Tanh, [wlT], [wlT])
    load_shift(13, alT)
    G = {n: k.sb(n, [128, 512], F32) for n in ("sgm", "ar", "ld", "cI", "cX", "E1", "E2", "E3", "E4", "dl", "ke", "b",
                                               "Bt", "Bh", "Kt", "Kh", "g1", "g2", "g3", "g4")}
    AR = k.sb("AR", [128, 1024], F32)
    WC = k.sb("WC", [128, 4], F32)
    NJ = 4
    J = []
    for _ in range(NJ):
        J.append(dict(
            XBK=k.sb("XBK", [128, 512], F32), A_=k.sb("A_", [128, 128], BF16), TM=k.sb("TM", [128, 192], F32),
            ATb=k.sb("ATb", [128, 128], BF16), TTf=k.sb("TTf", [128, 128], F32),
            Y0=k.sb("Y0", [128, 128], F32), T_=[k.sb("T_", [128, 128], BF16) for _ in range(2)],
            TT_=[k.sb("TT_", [128, 128], BF16) for _ in range(2)], Wm=k.sb("Wm", [128, 128], BF16),
            Wn=k.sb("Wn", [128, 128], BF16), X_=k.sb("X_", [128, 128], F32), Q1T=k.sb("Q1T", [128, 128], F32),
            GmT=k.sb("GmT", [128, 64], F32), b1=k.ps("b1"), b2=k.ps("b2")))
    Hss = [[k.sb("Hs", [128, 64], F32) for _ in range(2)] for _ in range(2)]
    YTr = [Res("yt0"), Res("yt1")]
    mo = [k.sb("mo", [128, 512], MIXT.dtype) for _ in range(2)]
    pz, pa = J[0]["b1"], J[1]["b1"]
    ident = C("ident")
    oLM, oLMT = CS["LM"][0], CS["LMT"][0]

    def chunk_head(job, hh, j, t0, cur, delay):
        XBK, A_, TM, Y0, T_, TT_, Wm, Wn, X_, Q1T, GmT, b1, b2 = (job[n] for n in (
            "XBK", "A_", "TM", "Y0", "T_", "TT_", "Wm", "Wn", "X_", "Q1T", "GmT", "b1", "b2"))
        ytr = YTr[hh]
        Hs = Hss[hh]
        ATb, TTf = job["ATb"], job["TTf"]
        d = cur_d[0]
        mT = (lambda q: cst[:, oLM + q * 128:oLM + (q + 1) * 128]) if d == 0 else \
             (lambda q: cst[:, oLMT + q * 128:oLMT + (q + 1) * 128])
        mTT = (lambda q: cst[:, oLMT + q * 128:oLMT + (q + 1) * 128]) if d == 0 else \
              (lambda q: cst[:, oLM + q * 128:oLM + (q + 1) * 128])
        cs = slice(j * 128, (j + 1) * 128)
        tk_ = slice(t0 + j * 128, t0 + (j + 1) * 128)
        pb = hh * 64
        ps_ = slice(pb, pb + 64)
        At = AR[ps_, j * 256:j * 256 + 128]
        Rt = AR[ps_, j * 256 + 128:j * 256 + 256]
        ARj = AR[ps_, j * 256:(j + 1) * 256]
        idn = ident[ps_, pb:pb + 64]
        k.mm(b1[:, 0:256], G["Bt"][ps_, cs], ARj, True, True, [G["Bt"], AR], [b1])
        k.mm(b1[:, 256:512], G["Kt"][ps_, cs], ARj, True, True, [G["Kt"], AR], [b1])
        k.mm(b2[:, 0:128], At, G["Bt"][ps_, cs], True, True, [AR, G["Bt"]], [b2])
        k.tr(b2[:, 128:192], G["Bh"][ps_, cs], idn, [G["Bh"], cst], [b2])
        k.tr(b2[:, 192:256], G["Kh"][ps_, cs], idn, [G["Kh"], cst], [b2])
        k.tr(b2[:, 256:320], vT[ps_, tk_], idn, [vT, cst], [b2])
        k.tr(b2[:, 320:384], At, idn, [AR, cst], [b2])
        yield
        k.tt("dve", XBK[:], b1[:], M4[d][:], ALU.mult, [b1, M4[d]], [XBK])
        k.cp("act", TM[:], b2[:, 128:320], [b2], [TM])
        yield
        k.tt("dve", A_[:], b2[:, 0:128], MS[d], ALU.mult, [b2, cst], [A_])
        k.mm(b2[:, 384:448], XBK[:, 256:384], TM[:, 128:192], True, True, [XBK, TM], [b2])
        yield
        k.tt("pool", Wm[:], A_[:], mT(0), ALU.mult, [A_, cst], [Wm])
        k.tt("pool", Wn[:], XBK[:, 0:128], mTT(0), ALU.mult, [XBK, cst], [Wn])
        k.cp("pool", ATb[:], XBK[:, 0:128], [XBK], [ATb])
        k.cp("act", Y0[:], b2[:, 320:448], [b2], [Y0])
        yield
        k.tt("pool", T_[0][:], Wm[:], ident, ALU.add, [Wm, cst], [T_[0]])
        k.tt("pool", TT_[0][:], Wn[:], ident, ALU.add, [Wn, cst], [TT_[0]])
        yield
        c_ = 0
        for q in range(1, 7):
            lastq = (q == 6)
            if not lastq:
                k.mm(b1[:, 0:128], ATb[:], T_[c_][:], True, True, [ATb, T_[c_]], [b1])
            k.mm(b1[:, 128:256], A_[:], TT_[c_][:], True, True, [A_, TT_[c_]], [b1])
            yield
            if not lastq:
                k.tt("dve", Wm[:], b1[:, 0:128], mT(q), ALU.mult, [b1, cst], [Wm])
            k.tt("dve", Wn[:], b1[:, 128:256], mTT(q), ALU.mult, [b1, cst], [Wn])
            yield
            if not lastq:
                k.mm(b1[:, 256:384], TT_[c_][:], Wm[:], True, True, [TT_[c_], Wm], [b1])
            k.mm(b1[:, 384:512], T_[c_][:], Wn[:], True, True, [T_[c_], Wn], [b1])
            yield
            if not lastq:
                k.tt("dve", T_[1 - c_][:], T_[c_][:], b1[:, 256:384], ALU.add, [T_[c_], b1], [T_[1 - c_]])
            if not lastq:
                k.tt("dve", TT_[1 - c_][:], TT_[c_][:], b1[:, 384:512], ALU.add, [TT_[c_], b1], [TT_[1 - c_]])
            else:
                k.tt("dve", TTf[:], TT_[c_][:], b1[:, 384:512], ALU.add, [TT_[c_], b1], [TTf])
            yield
            c_ = 1 - c_
        k.mm(b1[:, 0:128], TTf[:], Y0[:], True, True, [TTf, Y0], [b1])
        yield
        k.cp("act", X_[:], b1[:, 0:128], [b1], [X_])
        yield
        P1, P2 = X_[:, 0:64], X_[:, 64:128]
        Bh_, Kh_, V_ = TM[:, 0:64], TM[:, 64:128], TM[:, 128:192]
        k.mm(b2[ps_, 0:128], P1, XBK[:, 128:256], True, True, [X_, XBK], [b2])
        k.mm(b2[ps_, 128:192], P1, Bh_, True, True, [X_, TM], [b2])
        yield
        k.tt("dve", Q1T[ps_, :], b2[ps_, 0:128], Rt, ALU.add, [b2, AR], [Q1T])
        k.stt(GmT[ps_, :], ident[ps_, pb:pb + 64], WC[ps_, j:j + 1], b2[ps_, 128:192], ALU.mult, ALU.add,
              [cst, WC, b2], [GmT])
        yield
        Ho, Hn = Hs[cur], Hs[1 - cur]
        for _ in range(delay):
            yield
        k.mm(b2[ps_, 192:256], Bh_, P2, True, False, [TM, X_], [b2])
        k.mm(b2[ps_, 192:256], Kh_, V_, False, False, [TM], [b2])
        k.mm(b2[ps_, 192:256], GmT[ps_, :], Ho[ps_, :], False, True, [GmT, Ho], [b2])
        k.mm(b2[ps_, 256:384], P2, XBK[:, 128:256], True, False, [X_, XBK], [b2])
        k.mm(b2[ps_, 256:384], V_, XBK[:, 384:512], False, False, [TM, XBK], [b2])
        k.mm(b2[ps_, 256:384], Ho[ps_, :], Q1T[ps_, :], False, True, [Ho, Q1T], [b2])
        yield
        k.cp("act", Hn[ps_, :], b2[ps_, 192:256], [b2], [Hn])
        if d == 0:
            k.cp("dve", YT[ps_, tk_], b2[ps_, 256:384], [b2], [ytr])
        else:
            k.tt("dve", YT[ps_, tk_], YT[ps_, tk_], b2[ps_, 256:384], ALU.add, [ytr, b2], [ytr])
        yield

    cur_d = [0]
    import os
    STAG = int(os.environ.get("RW_STAG", "1"))
    for hp in range(4):
        load_shift(hp, rT)
        load_shift(4 + hp, kT)
        load_shift(8 + hp, vT)
        k.op("dve", lambda e: e.memset(YT[:, 0:1], 0.0), [], [YT] + YTr)
        for (t0, tl) in RBLK:
            g1, g2 = G["g1"], G["g2"]
            k.ts("dve", kkT[:, t0:t0 + tl], kT[:, t0:t0 + tl], P(l, "kk", hp), None, ALU.mult, ALU.bypass, [kT, pp], [kkT])
            k.act(g1[:, 0:tl], kkT[:, t0:t0 + tl], AF.Square, [kkT], [g1])
            k.mm(pz[:, 0:tl], C("blk64"), g1[:, 0:tl], True, True, [cst, g1], [pz])
            k.act(g2[:, 0:tl], pz[:, 0:tl], AF.Sqrt, [pz], [g2])
            k.ts("dve", g2[:, 0:tl], g2[:, 0:tl], 1e-12, None, ALU.max, ALU.bypass, [g2], [g2])
            k.op("dve", lambda e, tl=tl: e.reciprocal(out=g2[:, 0:tl], in_=g2[:, 0:tl]), [g2], [g2])
            k.tt("dve", kkT[:, t0:t0 + tl], kkT[:, t0:t0 + tl], g2[:, 0:tl], ALU.mult, [kkT, g2], [kkT])
        for d in range(2):
            blocks = list(RBLK) if d == 0 else [RBLK[0]] + list(reversed(RBLK[1:]))
            cur = 0
            cur_d[0] = d
            k.dma("sp", w2f[:], w2p[l, d, :, hp * 128:(hp + 1) * 128], writes=[w2f])
            k.dma("sp", a2f[:], a2p[l, d, :, hp * 128:(hp + 1) * 128], writes=[a2f])
            k.cp("dve", w2t[:], w2f[:], [w2f], [w2t])
            k.cp("dve", a2t[:], a2f[:], [a2f], [a2t])
            for hh_ in range(2):
                k.op("dve", lambda e, hh_=hh_: e.memset(Hss[hh_][0][:], 0.0), [], [Hss[hh_][0]])
            for (t0, tl) in blocks:
                nn = tl // 128
                sl = slice(t0, t0 + tl)
                w = slice(0, tl)
                k.mm(pz[:, w], w2t[:], wlT[:, sl], True, True, [w2t, wlT], [pz])
                k.mm(pa[:, w], a2t[:], alT[:, sl], True, True, [a2t, alT], [pa])
                k.act(G["sgm"][:, w], pz[:, w], AF.Sigmoid, [pz, pp], [G["sgm"]], bias=P(l, "w0", d * 4 + hp))
                k.act(G["ar"][:, w], pa[:, w], AF.Sigmoid, [pa, pp], [G["ar"]], bias=P(l, "a0", d * 4 + hp))
                k.ts("dve", G["ld"][:, w], G["sgm"][:, w], -0.6065306597126334, None, ALU.mult, ALU.bypass,
                     [G["sgm"]], [G["ld"]])
                o, c0_, rm = CS["rm"][0], None, None
                k.op("dve", lambda e, w=w: e.tensor_tensor_scan(out=G["cI"][:, w], data0=env["cst"][:, o:o + w.stop],
                                                               data1=G["ld"][:, w], initial=0.0, op0=ALU.mult,
                                                               op1=ALU.add), [cst, G["ld"]], [G["cI"]])
                cI3 = _v3(G["cI"][:, w])
                if d == 1:
                    k.tt("dve", G["g1"][:, w], G["ld"][:, w], G["cI"][:, w], ALU.subtract, [G["ld"], G["cI"]], [G["g1"]])
                    k.tt("dve", _v3(G["g2"][:, w]), _v3(G["g1"][:, w]), cI3[:, :, 127:128].broadcast_to([128, nn, 128]),
                         ALU.add, [G["g1"], G["cI"]], [G["g2"]])
                    k.cp("dve", G["cI"][:, w], G["g2"][:, w], [G["g2"]], [G["cI"]])
                    tot = cI3[:, :, 0:1]
                else:
                    tot = cI3[:, :, 127:128]
                k.tt("dve", G["cX"][:, w], G["cI"][:, w], G["ld"][:, w], ALU.subtract, [G["cI"], G["ld"]], [G["cX"]])
                k.act(WC[:, 0:nn].unsqueeze(2), tot, AF.Exp, [G["cI"]], [WC])
                k.act(G["E1"][:, w], G["cI"][:, w], AF.Exp, [G["cI"]], [G["E1"]])
                k.act(G["E2"][:, w], G["cI"][:, w], AF.Exp, [G["cI"]], [G["E2"]], scale=-1.0)
                k.act(G["E3"][:, w], G["cX"][:, w], AF.Exp, [G["cX"]], [G["E3"]])
                k.tt("dve", _v3(G["dl"][:, w]), tot.broadcast_to([128, nn, 128]), cI3, ALU.subtract, [G["cI"]], [G["dl"]])
                k.act(G["E4"][:, w], G["dl"][:, w], AF.Exp, [G["dl"]], [G["E4"]])
                k.ts("dve", G["g3"][:, w], G["ar"][:, w], -1.0, P(l, "ka", hp), ALU.add, ALU.mult, [G["ar"], pp], [G["g3"]])
                k.stt(G["ke"][:, w], G["g3"][:, w], 1.0, kT[:, sl], ALU.add, ALU.mult, [G["g3"], kT], [G["ke"]])
                ARv = AR[:, 0:nn * 256].rearrange("p (n a c) -> p n a c", a=2, c=128)
                k.stt(ARv[:, :, 0, :], _v3(kkT[:, sl]), -1.0, _v3(G["E3"][:, w]), ALU.mult, ALU.mult, [kkT, G["E3"]], [AR])
                k.tt("dve", ARv[:, :, 1, :], _v3(rT[:, sl]), _v3(G["E1"][:, w]), ALU.mult, [rT, G["E1"]], [AR])
                k.tt("pool", G["b"][:, w], kkT[:, sl], G["ar"][:, w], ALU.mult, [kkT, G["ar"]], [G["b"]])
                k.tt("dve", G["Bt"][:, w], G["b"][:, w], G["E2"][:, w], ALU.mult, [G["b"], G["E2"]], [G["Bt"]])
                k.tt("pool", G["Bh"][:, w], G["b"][:, w], G["E4"][:, w], ALU.mult, [G["b"], G["E4"]], [G["Bh"]])
                k.tt("dve", G["Kt"][:, w], G["ke"][:, w], G["E2"][:, w], ALU.mult, [G["ke"], G["E2"]], [G["Kt"]])
                k.tt("pool", G["Kh"][:, w], G["ke"][:, w], G["E4"][:, w], ALU.mult, [G["ke"], G["E4"]], [G["Kh"]])
                jorder = list(range(nn)) if d == 0 else list(reversed(range(nn)))
                todo = []
                for ji, j in enumerate(jorder):
                    for hh in range(2):
                        todo.append((len(todo), hh, j, cur))
                    cur = 1 - cur
                active = []
                rnd = 0
                nxt = 0
                last_start = -10
                while nxt < len(todo) or active:
                    if nxt < len(todo) and len(active) < NJ and rnd - last_start >= STAG:
                        idx, hh, j, cu = todo[nxt]
                        active.append(chunk_head(J[idx % NJ], hh, j, t0, cu, 0))
                        nxt += 1
                        last_start = rnd
                    for g_ in list(active):
                        try:
                            next(g_)
                        except StopIteration:
                            active.remove(g_)
                    rnd += 1
        for bi, (t0, tl) in enumerate(RBLK):
            if bi == 0 and not need_ctx:
                continue
            sl = slice(t0, t0 + tl)
            w = slice(0, tl)
            y = YT
            g_ = G["g4"]
            k.dma("sp", g_[:, w], PT[(46 + hp) * 128:(47 + hp) * 128, sl], writes=[g_])
            k.act(G["g1"][:, w], YT[:, sl], AF.Square, [YT] + YTr, [G["g1"]])
            k.mm(pz[:, w], C("blk64m"), YT[:, sl], True, True, [cst, YT] + YTr, [pz])
            k.mm(pa[:, w], C("blk64m"), G["g1"][:, w], True, True, [cst, G["g1"]], [pa])
            k.tt("dve", G["g2"][:, w], YT[:, sl], pz[:, w], ALU.subtract, [YT, pz] + YTr, [G["g2"]])
            k.act(G["g3"][:, w], pz[:, w], AF.Square, [pz], [G["g3"]])
            k.stt(G["g3"][:, w], G["g3"][:, w], -1.0, pa[:, w], ALU.mult, ALU.add, [G["g3"], pa], [G["g3"]])
            k.ts("dve", G["g3"][:, w], G["g3"][:, w], 0.0, 64e-5, ALU.max, ALU.add, [G["g3"]], [G["g3"]])
            k.act(G["g3"][:, w], G["g3"][:, w], AF.Sqrt, [G["g3"]], [G["g3"]])
            k.op("dve", lambda e, w=w: e.reciprocal(out=G["g3"][:, w], in_=G["g3"][:, w]), [G["g3"]], [G["g3"]])
            k.stt(G["g2"][:, w], G["g2"][:, w], P(l, "rwg", hp), G["g3"][:, w], ALU.mult, ALU.mult,
                  [G["g2"], G["g3"], pp], [G["g2"]])
            k.stt(G["g1"][:, w], rT[:, sl], P(l, "rk", hp), kT[:, sl], ALU.mult, ALU.mult, [rT, kT, pp], [G["g1"]])
            k.mm(pz[:, w], C("blk64"), G["g1"][:, w], True, True, [cst, G["g1"]], [pz])
            k.tt("dve", G["g1"][:, w], pz[:, w], vT[:, sl], ALU.mult, [pz, vT], [G["g1"]])
            k.tt("pool", G["g2"][:, w], G["g2"][:, w], G["g1"][:, w], ALU.add, [G["g2"], G["g1"]], [G["g2"]])
            m_ = mo[bi % 2]
            k.tt("pool", m_[:, w], G["g2"][:, w], g_[:, w], ALU.mult, [G["g2"], g_], [m_])
            k.dma("pool", MIXT[1536 + hp * 128:1536 + (hp + 1) * 128, sl], m_[:, w], reads=[m_])
    k.pop()


def half_start(half):
    return 0 if half == 0 else 2304


def half_len(half):
    return 2304 if half == 0 else T - 2304


def _in_cols():
    idx = list(range(0, 2048))
    idx += list(range(2048, 2560))
    idx += list(range(2560, 2816))
    kr = list(range(2816, 2880))
    idx += kr + kr
    krp = [2816 + (i ^ 16) for i in range(64)]
    idx += krp + krp
    idx += list(range(2880, 3904))
    idx += list(range(3904, 5696))
    idx += list(range(5696, 6208))
    assert len(idx) == NCOLS
    return np.asarray(idx)


def _uq_cols():
    idx = []
    for h in range(8):
        b = h * 192
        idx += list(range(b, b + 128))
        idx += list(range(b + 128, b + 192))
        idx += [b + 128 + (i ^ 16) for i in range(64)]
    return np.asarray(idx)


def _rope_tables():
    inv = (10000.0 ** (-np.arange(16, dtype=np.float32) / 16)).astype(np.float32)
    t = np.arange(NLAT)
    row = (t // 64).astype(np.float32)
    col = (t % 64).astype(np.float32)
    cos = np.ones((64, T), np.float32)
    sin = np.zeros((64, T), np.float32)
    for i in range(64):
        pos = row if i < 32 else col
        ang = (pos * inv[i % 16]).astype(np.float32)
        sgn = -1.0 if (i % 32) < 16 else 1.0
        cos[i, NCTX:] = np.cos(ang)
        sin[i, NCTX:] = sgn * np.sin(ang)
    tq = np.concatenate([cos, sin], 0)
    return tq, np.concatenate([cos, cos], 0), np.concatenate([sin, sin], 0)


def _consts():
    c = np.zeros((128, NCS), np.float32)
    i = np.arange(128)
    jj, ii = np.meshgrid(i, i, indexing="ij")

    def put(n, a):
        o, w = CS[n]
        c[:, o:o + w] = a
    put("ident", np.eye(128))
    put("ones", np.ones((128, 128)))
    put("onesm", np.full((128, 128), 1.0 / 128))
    put("relf", np.maximum(ii - jj, 0))
    put("relb", np.maximum(jj - ii, 0))
    put("mf", (ii >= jj).astype(np.float32))
    put("mb", (jj >= ii).astype(np.float32))
    put("ip1", np.broadcast_to(i[None, :] + 1.0, (128, 128)))
    put("cmi", np.broadcast_to(128.0 - i[None, :], (128, 128)))
    put("blk64", np.kron(np.eye(2), np.ones((64, 64))))
    put("blk64m", np.kron(np.eye(2), np.ones((64, 64))) / 64.0)
    put("ones2k", np.full((128, 128), 1.0 / 2048))
    put("sf", (ii > jj).astype(np.float32))
    put("mf2", (ii >= jj).astype(np.float32))
    put("sb", (jj > ii).astype(np.float32))
    put("mb2", (jj >= ii).astype(np.float32))
    rm = np.ones((128, 512), np.float32)
    rm[:, ::128] = 0.0
    put("rm", rm)
    lm = np.zeros((128, 7 * 128), np.float32)
    lmt = np.zeros((128, 7 * 128), np.float32)
    for kk in range(7):
        mk = (((jj >> (kk + 1)) == (ii >> (kk + 1))) & (((jj >> kk) & 1) == 1) & (((ii >> kk) & 1) == 0)).astype(np.float32)
        lm[:, kk * 128:(kk + 1) * 128] = mk
        lmt[:, kk * 128:(kk + 1) * 128] = mk.T
    put("LM", lm)
    put("LMT", lmt)
    put("colj", i[:, None].astype(np.float32))
    put("colcj", (127.0 - i)[:, None])
    return c


def _chunks(v, n):
    return np.ascontiguousarray(np.asarray(v, np.float32).reshape(n, 128).T)


def prep_inputs(inp, b):
    f = lambda a: np.ascontiguousarray(a, dtype=np.float32)
    m = {}
    xt = np.concatenate([inp["ctx"][b], inp["x"][b]], 0)
    m["xT"] = f(xt.T)
    m["cc"] = f(np.stack([_chunks(inp["c"][b], 16), _chunks(inp["c_ctx"], 16)], -1).reshape(128, 32))
    m["wada"] = f(inp["w_ada"])
    m["bada"] = f(np.concatenate([_chunks(inp["b_ada"][l], 48) for l in range(2)], 1))
    m["win"] = f(inp["w_in"][:, :, _in_cols()])
    m["wuq"] = f(inp["mla_w_uq"][:, :, _uq_cols()])
    m["wukv"] = f(inp["mla_w_ukv"])
    m["wout"] = f(inp["w_out"])
    pp = np.zeros((128, 2 * NPP), np.float32)
    for l in range(2):
        def put(n, a):
            o, w = PP[n]
            pp[:, l * NPP + o: l * NPP + o + w] = a
        put("gq", _chunks(inp["mla_q_norm_g"][l], 4))
        put("gkv", _chunks(inp["mla_kv_norm_g"][l], 2))
        put("retg", _chunks(inp["ret_gn_g"][l], 4))
        put("lng", _chunks(inp["ln_g"][l], 16))
        put("lnb", _chunks(inp["ln_b"][l], 16))
        put("mu0", _chunks(inp["rwkv_shift_mu"][l, 0], 14))
        put("mu1", _chunks(inp["rwkv_shift_mu"][l, 1], 14))
        put("w0", np.concatenate([_chunks(inp["rwkv_w0"][l, d], 4) for d in range(2)], 1))
        put("a0", np.concatenate([_chunks(inp["rwkv_a0"][l, d], 4) for d in range(2)], 1))
        put("kk", _chunks(inp["rwkv_k_k"][l], 4))
        put("ka", _chunks(inp["rwkv_k_a"][l], 4))
        put("rk", _chunks(inp["rwkv_r_k"][l].reshape(-1), 4))
        put("rwg", _chunks(inp["rwkv_gn_g"][l], 4))
    m["pp"] = pp
    w2p = np.zeros((2, 2, 128, 512), np.float32)
    a2p = np.zeros((2, 2, 128, 512), np.float32)
    for l in range(2):
        for d in range(2):
            w2p[l, d, d * 64:(d + 1) * 64] = inp["rwkv_w2"][l, d]
            a2p[l, d, d * 64:(d + 1) * 64] = inp["rwkv_a2"][l, d]
    m["w2p"], m["a2p"] = w2p, a2p
    m["rdl"] = f(np.broadcast_to(inp["ret_decay_logit"].reshape(1, 16), (128, 16)))
    tq, tkc, tks = _rope_tables()
    m["tabQ"], m["tabKc"], m["tabKs"] = f(tq), f(tkc), f(tks)
    m["cst"] = _consts()
    return m


def kernel(**inputs):
    inp = {k_: np.asarray(v) for k_, v in inputs.items()}
    nc = build_program()
    in_maps = [prep_inputs(inp, c % 4) for c in range(4)]
    in_maps = in_maps + in_maps
    res = run_bass_kernel_spmd(nc, in_maps, core_ids=list(range(8)))
    out = np.stack([np.ascontiguousarray(res.results[b]["yT"].T) for b in range(4)], 0)
    return out.astype(np.float32)
```

```python
import numpy as np
import ml_dtypes
from contextlib import ExitStack
import concourse.bass as bass
import concourse.mybir as mybir
from concourse.bass_utils import run_bass_kernel_spmd

F32 = mybir.dt.float32
BF16 = mybir.dt.bfloat16
AF = mybir.ActivationFunctionType
ALU = mybir.AluOpType

D = 2048
NCTX = 256
NLAT = 4096
T = NCTX + NLAT
NCH = T // 128
NCOLS = 6400
NCC = NCOLS // 128
ALPHA = 4 ** 0.25
MLA_SCALE = 192 ** -0.5
TBLK = [(0, 256)] + [(256 + i * 512, 512) for i in range(8)]


class Res:
    __slots__ = ("lw", "rd", "dsem", "dval", "name", "psum")

    def __init__(self, name="", psum=False):
        self.psum = psum
        self.lw = None
        self.rd = {}
        self.dsem = None
        self.dval = 0
        self.name = name


class Tile:
    def __init__(self, t, name):
        self.t = t
        self.r = Res(name)

    def __getitem__(self, idx):
        return self.t[idx]


class KB:
    ENG = ("pe", "act", "dve", "pool", "sp")

    def __init__(self, nc):
        self.nc = nc
        self.eng = dict(pe=nc.tensor, act=nc.scalar, dve=nc.vector, pool=nc.gpsimd, sp=nc.sync)
        self.root = ExitStack()
        self.stacks = [self.root]
        self.sem = {e: self.root.enter_context(nc.semaphore("s_" + e)) for e in self.ENG}
        self.cnt = {e: 0 for e in self.ENG}
        self.waited = {e: {} for e in self.ENG}
        self.dma_owners = [[]]
        self.nid = 0
        self.sem_pool = []
        self.okey = 0

    def push(self):
        es = ExitStack()
        self.stacks.append(es)
        self.dma_owners.append([])

    def pop(self):
        self.barrier()
        self.dma_owners.pop()
        self.stacks.pop().close()

    def _name(self, n):
        self.nid += 1
        return f"{n}_{self.nid}"

    def sb(self, name, shape, dtype):
        t = self.stacks[-1].enter_context(self.nc.sbuf_tensor(self._name(name), list(shape), dtype))
        return Tile(t, name)

    def ps(self, name, shape=(128, 512), dtype=F32):
        t = self.stacks[-1].enter_context(self.nc.psum_tensor(self._name(name), list(shape), dtype))
        tl = Tile(t, name)
        tl.r.psum = True
        return tl

    def _wait(self, e, ev):
        if ev is None:
            return
        key, sem, val = ev
        if key == "pe" and e == "pe":
            return
        if self.waited[e].get(key, 0) >= val:
            return
        self.eng[e].wait_ge(sem, val)
        self.waited[e][key] = val

    def _deps(self, e, reads, writes):
        for r in reads:
            self._wait(e, r.lw)
            if r.psum:
                for key, ev in r.rd.items():
                    if key != e:
                        self._wait(e, ev)
        for r in writes:
            self._wait(e, r.lw)
            for ev in r.rd.values():
                self._wait(e, ev)

    def _mark(self, ev, reads, writes):
        for r in reads:
            r.rd[ev[0]] = ev
        for r in writes:
            r.lw = ev
            r.rd = {}

    def op(self, e, fn, reads=(), writes=()):
        reads = [x.r if isinstance(x, Tile) else x for x in reads]
        writes = [x.r if isinstance(x, Tile) else x for x in writes]
        self._deps(e, reads, writes)
        inst = fn(self.eng[e])
        self.cnt[e] += 1
        inst.then_inc(self.sem[e], 1)
        self._mark((e, self.sem[e], self.cnt[e]), reads, writes)

    def dma(self, q, out, in_, reads=(), writes=(), **kw):
        reads = [x.r if isinstance(x, Tile) else x for x in reads]
        writes = [x.r if isinstance(x, Tile) else x for x in writes]
        owner = writes[0] if writes else reads[0]
        self._deps(q, reads, writes)
        if owner.dsem is None:
            if self.sem_pool:
                owner.dsem, owner.dval = self.sem_pool.pop()
            else:
                owner.dsem = self.root.enter_context(self.nc.semaphore(self._name("d")))
                owner.dval = 0
            self.okey += 1
            owner.name = ("dma", self.okey)
            self.dma_owners[-1].append(owner)
        inst = self.eng[q].dma_start(out=out, in_=in_, **kw)
        owner.dval += 16
        inst.then_inc(owner.dsem, 16)
        self._mark((owner.name, owner.dsem, owner.dval), reads, writes)

    def barrier(self):
        evs = [(e, self.sem[e], self.cnt[e]) for e in self.ENG if self.cnt[e] > 0]
        for lst in self.dma_owners:
            for o in lst:
                evs.append((o.name, o.dsem, o.dval))
        for e in self.ENG:
            for ev in evs:
                if ev[0] == e:
                    continue
                if self.waited[e].get(ev[0], 0) >= ev[2]:
                    continue
                self.eng[e].wait_ge(ev[1], ev[2])
                self.waited[e][ev[0]] = ev[2]
        for o in self.dma_owners[-1]:
            for e in self.ENG:
                self.waited[e].pop(o.name, None)
            self.sem_pool.append((o.dsem, o.dval))
            o.dsem = None

    def mm(self, out, lhsT, rhs, start, stop, reads, writes):
        self.op("pe", lambda g: g.matmul(out, lhsT=lhsT, rhs=rhs, start=start, stop=stop), reads, writes)

    def tr(self, out, in_, ident, reads, writes):
        self.op("pe", lambda g: g.transpose(out, in_, ident), reads, writes)

    def act(self, out, in_, func, reads, writes, scale=1.0, bias=0.0, e="act"):
        self.op(e, lambda g: g.activation(out=out, in_=in_, func=func, bias=bias, scale=scale), reads, writes)

    def ts(self, e, out, in0, s1, s2, op0, op1, reads, writes):
        self.op(e, lambda g: g.tensor_scalar(out=out, in0=in0, scalar1=s1, scalar2=s2, op0=op0, op1=op1), reads, writes)

    def tt(self, e, out, in0, in1, op, reads, writes):
        self.op(e, lambda g: g.tensor_tensor(out=out, in0=in0, in1=in1, op=op), reads, writes)

    def stt(self, out, in0, scalar, in1, op0, op1, reads, writes):
        self.op("dve", lambda g: g.scalar_tensor_tensor(out=out, in0=in0, scalar=scalar, in1=in1, op0=op0, op1=op1),
                reads, writes)

    def cp(self, e, out, in_, reads, writes):
        if e == "act":
            self.op(e, lambda g: g.copy(out=out, in_=in_), reads, writes)
        else:
            self.op(e, lambda g: g.tensor_copy(out=out, in_=in_), reads, writes)


PP = {}
_o = 0
for _n, _w in [("gq", 4), ("gkv", 2), ("retg", 4), ("lng", 16), ("lnb", 16), ("mu0", 14), ("mu1", 14),
               ("w0", 8), ("a0", 8), ("kk", 4), ("ka", 4), ("rk", 4), ("rwg", 4)]:
    PP[_n] = (_o, _w)
    _o += _w
NPP = _o

CS = {}
_o = 0
for _n, _w in [("ident", 128), ("ones", 128), ("onesm", 128), ("relf", 128), ("relb", 128), ("mf", 128), ("mb", 128),
               ("ip1", 128), ("cmi", 128), ("blk64", 128), ("blk64m", 128), ("ones2k", 128),
               ("sf", 128), ("mf2", 128), ("sb", 128), ("mb2", 128), ("rm", 512), ("LM", 7 * 128), ("LMT", 7 * 128),
               ("colj", 1), ("colcj", 1)]:
    CS[_n] = (_o, _w)
    _o += _w
NCS = _o


def build_program(n_layers=2, dbg=()):
    nc = bass.Bass("TRN2", target_bir_lowering=False)
    k = KB(nc)

    def din(name, shape, dt=F32):
        return nc.dram_tensor(name, list(shape), dt, kind="ExternalInput").ap()

    xT = din("xT", [D, T])
    cc = din("cc", [128, 32])
    wada = din("wada", [2, D, 6144])
    bada = din("bada", [128, 96])
    win = din("win", [2, D, NCOLS])
    wuq = din("wuq", [2, 512, 2048])
    wukv = din("wukv", [2, 256, 2048])
    wout = din("wout", [2, D, D])
    ppd = din("pp", [128, 2 * NPP])
    rdl = din("rdl", [128, 16])
    w2p = din("w2p", [2, 2, 128, 512])
    a2p = din("a2p", [2, 2, 128, 512])
    tabQ = din("tabQ", [128, T])
    tabKc = din("tabKc", [128, T])
    tabKs = din("tabKs", [128, T])
    cstd = din("cst", [128, NCS])
    yT = nc.dram_tensor("yT", [D, NLAT], F32, kind="ExternalOutput").ap()
    PT = nc.dram_tensor("PT", [NCOLS, T], F32, kind="Internal").ap()
    MIXT = nc.dram_tensor("MIXT", [D, T], BF16, kind="Internal").ap()
    XN = nc.dram_tensor("XN", [D, T], F32, kind="Internal").ap()
    dbg_out = {}
    if "mod" in dbg:
        dbg_out["mod"] = nc.dram_tensor("dbg_mod", [128, 192], F32, kind="ExternalOutput").ap()
    if "PT" in dbg:
        dbg_out["PT"] = nc.dram_tensor("dbg_PT", [NCOLS, T], F32, kind="ExternalOutput").ap()
        PT = dbg_out["PT"]
    if "PTin" in dbg:
        PT = nc.dram_tensor("PTin", [NCOLS, T], F32, kind="ExternalInput").ap()
    if "MIXT" in dbg:
        dbg_out["MIXT"] = nc.dram_tensor("dbg_MIXT", [D, T], F32, kind="ExternalOutput").ap()
        MIXT = dbg_out["MIXT"]
    if "XN" in dbg:
        dbg_out["XN"] = nc.dram_tensor("dbg_XN", [D, T], F32, kind="ExternalOutput").ap()
        XN = dbg_out["XN"]

    mod = k.sb("mod", [128, 192], F32)
    pp = k.sb("pp", [128, 2 * NPP], F32)
    cst = k.sb("cst", [128, NCS], F32)
    k.dma("sp", pp[:], ppd, writes=[pp])
    k.dma("sp", cst[:], cstd, writes=[cst])

    def C(name):
        o, w = CS[name]
        return cst[:, o:o + w]

    def P(l, name, j=0):
        o, w = PP[name]
        return pp[:, l * NPP + o + j: l * NPP + o + j + 1]

    def modc(l, j, r):
        return mod[:, l * 96 + j * 2 + r: l * 96 + j * 2 + r + 1]

    k.push()
    if "PTin" in dbg:
        n_layers_mod = 0
    else:
        n_layers_mod = n_layers
    cct = k.sb("cct", [128, 32], F32)
    sc = k.sb("sc", [128, 32], F32)
    bad = k.sb("bad", [128, 96], F32)
    k.dma("sp", cct[:], cc, writes=[cct])
    k.dma("sp", bad[:], bada, writes=[bad])
    k.act(sc[:], cct[:], AF.Silu, [cct], [sc])
    wa = [k.sb("wa", [128, 16, 512], F32) for _ in range(2)]
    pm = [k.ps("pm") for _ in range(2)]
    it = 0
    for l in range(n_layers_mod):
        for mc in range(12):
            w = wa[it % 2]
            src = wada[l, :, mc * 512:(mc + 1) * 512].rearrange("(kc p) c -> p kc c", p=128)
            k.dma("sp", w[:, 0:8, :], src[:, 0:8, :], writes=[w])
            k.dma("sp", w[:, 8:16, :], src[:, 8:16, :], writes=[w])
            for j in range(4):
                p_ = pm[(it * 4 + j) % 2]
                for kc in range(16):
                    k.mm(p_[:, 0:2], w[:, kc, j * 128:(j + 1) * 128], sc[:, kc * 2:kc * 2 + 2], kc == 0, kc == 15,
                         [w, sc], [p_])
                jj = mc * 4 + j
                k.ts("dve", mod[:, l * 96 + jj * 2: l * 96 + jj * 2 + 2], p_[:, 0:2],
                     bad[:, l * 48 + jj: l * 48 + jj + 1], None, ALU.add, ALU.bypass, [p_, bad], [mod])
            it += 1
        k.ts("dve", mod[:, l * 96 + 32: l * 96 + 64], mod[:, l * 96 + 32: l * 96 + 64], 1.0, None, ALU.add, ALU.bypass,
             [mod], [mod])
    if "mod" in dbg:
        k.dma("sp", dbg_out["mod"], mod[:], reads=[mod])
    k.pop()

    for l in range(n_layers):
        xsrc = xT if l == 0 else XN
        last = (l == n_layers - 1)
        if "PTin" not in dbg:
            phase_inproj(k, l, xsrc, win, PT, modc)
        if "PT" in dbg:
            break
        env = dict(C=C, P=P, cst=cst, pp=pp, mod=mod, modc=modc)
        if "nomla" not in dbg:
            phase_mla(k, l, PT, MIXT, wuq, wukv, tabQ, tabKc, tabKs, env, not last)
        if "noret" not in dbg:
            phase_ret(k, l, PT, MIXT, rdl, env, not last)
        if "norw" not in dbg:
            phase_rwkv(k, l, PT, MIXT, w2p, a2p, env, not last)
        if "MIXT" in dbg:
            break
        phase_out(k, l, xsrc, MIXT, wout, yT if last else XN, env, last)

    k.barrier()
    k.root.close()
    return nc


def phase_inproj(k, l, xsrc, win, PT, modc):
    k.push()
    HT = k.sb("HT", [128, 16, T], BF16)
    k.push()
    xs = [k.sb("xs", [128, 2176], F32) for _ in range(2)]
    it = 0
    for kc in range(16):
        for half in range(2):
            x_ = xs[it % 2]
            it += 1
            t0 = half * 2176
            k.dma("sp", x_[:], xsrc[kc * 128:(kc + 1) * 128, t0:t0 + 2176], writes=[x_])
            def affine(out_, in_, r_):
                if it % 2:
                    k.ts("dve", out_, in_, modc(l, 16 + kc, r_), modc(l, kc, r_), ALU.mult, ALU.add, [x_], [HT])
                else:
                    k.act(out_, in_, AF.Identity, [x_], [HT], scale=modc(l, 16 + kc, r_), bias=modc(l, kc, r_))
            if half == 0:
                affine(HT[:, kc, 0:NCTX], x_[:, 0:NCTX], 1)
                affine(HT[:, kc, NCTX:2176], x_[:, NCTX:2176], 0)
            else:
                affine(HT[:, kc, 2176:T], x_[:], 0)
    k.pop()
    wf = [k.sb("wf", [128, 16, 128], F32) for _ in range(2)]
    wb = [k.sb("wb", [128, 16, 128], BF16) for _ in range(2)]
    og = [k.sb("og", [128, 2304], F32) for _ in range(2)]
    pp_ = [k.ps("pi") for _ in range(4)]
    GATE = set(range(12, 16)) | set(range(24, 32)) | set(range(46, 50))
    pi = 0
    oi = 0
    for c in range(NCC):
        w_f, w_b = wf[c % 2], wb[c % 2]
        k.dma("sp", w_f[:], win[l, :, c * 128:(c + 1) * 128].rearrange("(kc p) c -> p kc c", p=128), writes=[w_f])
        k.cp("pool", w_b[:], w_f[:], [w_f], [w_b])
        for half in range(2):
            o_ = og[oi % 2]
            oi += 1
            blocks = [b for b in TBLK if (b[0] < 2176) == (half == 0)]
            for (t0, tl) in blocks:
                p_ = pp_[pi % 4]
                pi += 1
                for kc in range(16):
                    k.mm(p_[:, 0:tl], w_b[:, kc, :], HT[:, kc, t0:t0 + tl], kc == 0, kc == 15, [w_b, HT], [p_])
                so = t0 - half_start(half)
                fn = AF.Silu if c in GATE else AF.Copy
                if fn == AF.Copy and (pi % 2):
                    k.cp("dve", o_[:, so:so + tl], p_[:, 0:tl], [p_], [o_])
                else:
                    k.act(o_[:, so:so + tl], p_[:, 0:tl], fn, [p_], [o_])
            hs = half_start(half)
            hl = half_len(half)
            k.dma("pool", PT[c * 128:(c + 1) * 128, hs:hs + hl], o_[:, 0:hl], reads=[o_])
    k.pop()


def _v3(ap, c=128):
    return ap.rearrange("p (n c) -> p n c", c=c)


def group_norm_fm(k, env, pO, tl, lhs_mean, eps, gcol, tmp, e2="pool"):
    C, cst = env["C"], env["cst"]
    y, ysq, d, m2, pm, pq = tmp
    k.cp("act", y[:, 0:tl], pO[:, 0:tl], [pO], [y])
    k.act(ysq[:, 0:tl], pO[:, 0:tl], AF.Square, [pO], [ysq])
    k.mm(pm[:, 0:tl], lhs_mean, y[:, 0:tl], True, True, [cst, y], [pm])
    k.mm(pq[:, 0:tl], lhs_mean, ysq[:, 0:tl], True, True, [cst, ysq], [pq])
    k.tt("dve", d[:, 0:tl], y[:, 0:tl], pm[:, 0:tl], ALU.subtract, [y, pm], [d])
    k.act(m2[:, 0:tl], pm[:, 0:tl], AF.Square, [pm], [m2])
    k.stt(m2[:, 0:tl], m2[:, 0:tl], -1.0, pq[:, 0:tl], ALU.mult, ALU.add, [m2, pq], [m2])
    k.ts("dve", m2[:, 0:tl], m2[:, 0:tl], 0.0, eps, ALU.max, ALU.add, [m2], [m2])
    k.act(m2[:, 0:tl], m2[:, 0:tl], AF.Sqrt, [m2], [m2])
    k.op("dve", lambda g: g.reciprocal(out=m2[:, 0:tl], in_=m2[:, 0:tl]), [m2], [m2])
    k.stt(d[:, 0:tl], d[:, 0:tl], gcol, m2[:, 0:tl], ALU.mult, ALU.mult, [d, m2, env["pp"]], [d])
    return d


def phase_mla(k, l, PT, MIXT, wuq, wukv, tabQ, tabKc, tabKs, env, need_ctx):
    C, P, cst, pp = env["C"], env["P"], env["cst"], env["pp"]
    k.push()
    cqn = k.sb("cqn", [128, 4, T], BF16)
    ckvn = k.sb("ckvn", [128, 2, T], BF16)
    KR = k.sb("KR", [128, T], BF16)
    tq = k.sb("tq", [128, T], F32)
    k.dma("sp", tq[:], tabQ, writes=[tq])
    wq_b = k.sb("wq_b", [128, 4, 2048], BF16)
    wkv_b = k.sb("wkv_b", [128, 2, 2048], BF16)
    onesb = k.sb("onesb", [128, 128], BF16)
    k.cp("dve", onesb[:], C("ones"), [cst], [onesb])
    k.push()
    wst = [k.sb("wst", [128, 2048], F32) for _ in range(2)]
    for i in range(6):
        w = wst[i % 2]
        src = wuq[l, i * 128:(i + 1) * 128, :] if i < 4 else wukv[l, (i - 4) * 128:(i - 3) * 128, :]
        k.dma("sp", w[:], src, writes=[w])
        if i < 4:
            k.cp("pool", wq_b[:, i, :], w[:], [w], [wq_b])
        else:
            k.cp("pool", wkv_b[:, i - 4, :], w[:], [w], [wkv_b])
    xin = [k.sb("xin", [128, 8, 512], F32) for _ in range(2)]
    tk = [k.sb("tk", [128, 2, 512], F32) for _ in range(2)]
    sq = k.sb("sq", [128, 6, 512], F32)
    rs = k.sb("rs", [128, 2, 512], F32)
    t1 = k.sb("t1", [128, 512], F32)
    t2 = k.sb("t2", [128, 512], F32)
    psq = [k.ps("psq") for _ in range(2)]
    for bi, (t0, tl) in enumerate(TBLK):
        x_, tk_ = xin[bi % 2], tk[bi % 2]
        k.dma("sp", x_[:, :, 0:tl], PT[16 * 128:24 * 128, t0:t0 + tl].rearrange("(c p) t -> p c t", p=128), writes=[x_])
        k.dma("sp", tk_[:, 0, 0:tl], tabKc[:, t0:t0 + tl], writes=[tk_])
        k.dma("sp", tk_[:, 1, 0:tl], tabKs[:, t0:t0 + tl], writes=[tk_])
        k.act(sq[:, :, 0:tl], x_[:, 0:6, 0:tl], AF.Square, [x_], [sq])
        for g, (c0, n) in enumerate([(0, 4), (4, 2)]):
            for j in range(n):
                k.mm(psq[g][:, 0:tl], C("ones"), sq[:, c0 + j, 0:tl], j == 0, j == n - 1, [cst, sq], [psq[g]])
            k.ts("dve", rs[:, g, 0:tl], psq[g][:, 0:tl], 1.0 / (n * 128), 1e-6, ALU.mult, ALU.add, [psq[g]], [rs])
            k.act(rs[:, g, 0:tl], rs[:, g, 0:tl], AF.Sqrt, [rs], [rs])
            k.op("dve", lambda e, g=g: e.reciprocal(out=rs[:, g, 0:tl], in_=rs[:, g, 0:tl]), [rs], [rs])
            for j in range(n):
                if g == 0:
                    k.stt(cqn[:, j, t0:t0 + tl], x_[:, j, 0:tl], P(l, "gq", j), rs[:, 0, 0:tl], ALU.mult, ALU.mult,
                          [x_, rs, pp], [cqn])
                else:
                    k.stt(ckvn[:, j, t0:t0 + tl], x_[:, 4 + j, 0:tl], P(l, "gkv", j), rs[:, 1, 0:tl], ALU.mult, ALU.mult,
                          [x_, rs, pp], [ckvn])
        k.tt("pool", t1[:, 0:tl], x_[:, 6, 0:tl], tk_[:, 0, 0:tl], ALU.mult, [x_, tk_], [t1])
        k.tt("dve", t2[:, 0:tl], x_[:, 7, 0:tl], tk_[:, 1, 0:tl], ALU.mult, [x_, tk_], [t2])
        k.tt("dve", KR[:, t0:t0 + tl], t1[:, 0:tl], t2[:, 0:tl], ALU.add, [t1, t2], [KR])
    k.pop()
    KN = k.sb("KN", [128, T], BF16)
    V = k.sb("V", [128, NCH, 128], BF16)
    QN = k.sb("QN", [128, T], BF16)
    QR = k.sb("QR", [128, T], BF16)
    pS = [k.ps("pS") for _ in range(3)]
    pO = k.ps("pO")
    pD = k.ps("pD")
    pA = [k.ps("pA") for _ in range(2)]
    Pt = [k.sb("Pt", [128, 512], BF16) for _ in range(4)]
    gt = [k.sb("gt", [128, 512], F32) for _ in range(2)]
    rd = k.sb("rd", [128, 512], F32)
    ot = k.sb("ot", [128, 512], F32)
    mo = [k.sb("mo", [128, 512], MIXT.dtype) for _ in range(2)]
    ai = 0
    for h in range(8):
        for (t0, tl) in TBLK:
            p_ = pA[ai % 2]
            ai += 1
            for kc in range(2):
                k.mm(p_[:, 0:tl], wkv_b[:, kc, h * 256:h * 256 + 128], ckvn[:, kc, t0:t0 + tl], kc == 0, kc == 1,
                     [wkv_b, ckvn], [p_])
            k.cp("dve", KN[:, t0:t0 + tl], p_[:, 0:tl], [p_], [KN])
            p_ = pA[ai % 2]
            ai += 1
            for kc in range(4):
                k.mm(p_[:, 0:tl], wq_b[:, kc, h * 256:h * 256 + 128], cqn[:, kc, t0:t0 + tl], kc == 0, kc == 3,
                     [wq_b, cqn], [p_])
            k.cp("act", QN[:, t0:t0 + tl], p_[:, 0:tl], [p_], [QN])
            p_ = pA[ai % 2]
            ai += 1
            for kc in range(4):
                k.mm(p_[:, 0:tl], wq_b[:, kc, h * 256 + 128:h * 256 + 256], cqn[:, kc, t0:t0 + tl], kc == 0, kc == 3,
                     [wq_b, cqn], [p_])
            k.tt("dve", QR[:, t0:t0 + tl], p_[:, 0:tl], tq[:, t0:t0 + tl], ALU.mult, [p_, tq], [QR])
        for n4 in range(0, NCH, 4):
            p_ = pA[ai % 2]
            ai += 1
            nn = min(4, NCH - n4)
            for j in range(nn):
                n = n4 + j
                for kc in range(2):
                    k.mm(p_[:, j * 128:(j + 1) * 128], ckvn[:, kc, n * 128:(n + 1) * 128],
                         wkv_b[:, kc, h * 256 + 128:h * 256 + 256], kc == 0, kc == 1, [ckvn, wkv_b], [p_])
            k.cp("act", V[:, n4:n4 + nn, :], _v3(p_[:, 0:nn * 128]), [p_], [V])
        for bi, (t0, tl) in enumerate(TBLK):
            if bi == 0 and not need_ctx:
                continue
            kbs = list(range(0, 2)) if bi == 0 else list(range(NCH))
            g_ = gt[bi % 2]
            k.dma("sp", g_[:, 0:tl], PT[(24 + h) * 128:(25 + h) * 128, t0:t0 + tl], writes=[g_])
            nk = len(kbs)

            def s_step(ii):
                kb = kbs[ii]
                s_ = pS[ii % 3]
                k.mm(s_[:, 0:tl], KN[:, kb * 128:(kb + 1) * 128], QN[:, t0:t0 + tl], True, False, [KN, QN], [s_])
                k.mm(s_[:, 0:tl], KR[:, kb * 128:(kb + 1) * 128], QR[:, t0:t0 + tl], False, True, [KR, QR], [s_])
                p_ = Pt[ii % 4]
                k.act(p_[:, 0:tl], s_[:, 0:tl], AF.Exp, [s_], [p_], scale=MLA_SCALE)

            def od_step(ii):
                kb = kbs[ii]
                p_ = Pt[ii % 4]
                k.mm(pO[:, 0:tl], V[:, kb, :], p_[:, 0:tl], ii == 0, ii == nk - 1, [V, p_], [pO])
                k.mm(pD[:, 0:tl], onesb[:], p_[:, 0:tl], ii == 0, ii == nk - 1, [onesb, p_], [pD])

            for ii in range(min(2, nk)):
                s_step(ii)
            for ii in range(nk):
                if ii + 2 < nk:
                    s_step(ii + 2)
                od_step(ii)
            k.op("dve", lambda e, tl=tl: e.reciprocal(out=rd[:, 0:tl], in_=pD[:, 0:tl]), [pD], [rd])
            k.tt("dve", ot[:, 0:tl], pO[:, 0:tl], rd[:, 0:tl], ALU.mult, [pO, rd], [ot])
            m_ = mo[bi % 2]
            k.tt("pool", m_[:, 0:tl], ot[:, 0:tl], g_[:, 0:tl], ALU.mult, [ot, g_], [m_])
            k.dma("pool", MIXT[512 + h * 128:512 + (h + 1) * 128, t0:t0 + tl], m_[:, 0:tl], reads=[m_])
    k.pop()


def phase_ret(k, l, PT, MIXT, rdl, env, need_ctx):
    C, P, cst, pp = env["C"], env["P"], env["cst"], env["pp"]
    k.push()
    lg = k.sb("lg", [128, 8], F32)
    rdt = k.sb("rdt", [128, 16], F32)
    k.dma("sp", rdt[:], rdl, writes=[rdt])
    k.act(lg[:], rdt[:, l * 8:(l + 1) * 8], AF.Exp, [rdt], [lg], scale=-1.0)
    k.ts("dve", lg[:], lg[:], 1.0, None, ALU.add, ALU.bypass, [lg], [lg])
    k.act(lg[:], lg[:], AF.Ln, [lg], [lg])
    k.ts("dve", lg[:], lg[:], -1.0, None, ALU.mult, ALU.bypass, [lg], [lg])
    mask = k.sb("mask", [128, 128], F32)
    mt = k.sb("mt", [128, 128], F32)
    xi = [k.sb("xi", [128, 128], F32) for _ in range(2)]
    zeta = k.sb("zeta", [128, 2], F32)
    gC = k.sb("gC", [128, 2], F32)
    qT, kT, vT, sg = [k.sb(n, [128, T], F32) for n in ("qT", "kT", "vT", "sg")]
    qb, kb_, qxf, qxb = [k.sb(n, [128, T], BF16) for n in ("qb", "kb", "qxf", "qxb")]
    knf, knb, vn, SPf, SPb = [k.sb(n, [128, NCH, 128], BF16) for n in ("knf", "knb", "vn", "SPf", "SPb")]
    st = [k.sb("st", [128, 128], F32) for _ in range(2)]
    SM = k.sb("SM", [128, 4, 128], BF16)
    tmp = [k.sb("gn", [128, 512], F32) for _ in range(4)]
    mo = [k.sb("mo", [128, 512], MIXT.dtype) for _ in range(2)]
    pT = [k.ps("pT") for _ in range(2)]
    pU = [k.ps("pU") for _ in range(2)]
    pS = k.ps("pS")
    pO = k.ps("pO")
    pm, pq = k.ps("pm"), k.ps("pq")
    sc = 128 ** -0.5
    ui = 0
    import os
    STOP = int(os.environ.get("RET_STOP", "99"))
    for h in range(4):
        lgf, lgb = lg[:, h:h + 1], lg[:, 4 + h:5 + h]
        k.act(mask[:], C("relf"), AF.Exp, [cst, lg], [mask], scale=lgf)
        k.tt("dve", mask[:], mask[:], C("mf"), ALU.mult, [mask, cst], [mask])
        k.act(mt[:], C("relb"), AF.Exp, [cst, lg], [mt], scale=lgb)
        k.tt("dve", mt[:], mt[:], C("mb"), ALU.mult, [mt, cst], [mt])
        k.tt("dve", mask[:], mask[:], mt[:], ALU.add, [mask, mt], [mask])
        k.ts("dve", mask[:], mask[:], sc, None, ALU.mult, ALU.bypass, [mask], [mask])
        k.act(xi[0][:], C("ip1"), AF.Exp, [cst, lg], [xi[0]], scale=lgf)
        k.act(xi[1][:], C("cmi"), AF.Exp, [cst, lg], [xi[1]], scale=lgb)
        k.act(zeta[:, 0:1], C("colcj"), AF.Exp, [cst, lg], [zeta], scale=lgf)
        k.act(zeta[:, 1:2], C("colj"), AF.Exp, [cst, lg], [zeta], scale=lgb)
        k.ts("dve", zeta[:], zeta[:], sc, None, ALU.mult, ALU.bypass, [zeta], [zeta])
        k.act(gC[:, 0:1], lgf, AF.Exp, [lg], [gC], scale=128.0)
        k.act(gC[:, 1:2], lgb, AF.Exp, [lg], [gC], scale=128.0)
        if STOP <= 1:
            continue
        for j, tile in enumerate([qT, kT, vT, sg]):
            r0 = (j * 4 + h) * 128
            k.dma("sp", tile[:, 0:2176], PT[r0:r0 + 128, 0:2176], writes=[tile])
            k.dma("sp", tile[:, 2176:T], PT[r0:r0 + 128, 2176:T], writes=[tile])
        k.cp("act", qb[:], qT[:], [qT], [qb])
        k.cp("dve", kb_[:], kT[:], [kT], [kb_])
        k.tt("dve", _v3(qxf[:]), _v3(qT[:]), xi[0][:].unsqueeze(1).broadcast_to([128, NCH, 128]), ALU.mult,
             [qT, xi[0]], [qxf])
        k.tt("dve", _v3(qxb[:]), _v3(qT[:]), xi[1][:].unsqueeze(1).broadcast_to([128, NCH, 128]), ALU.mult,
             [qT, xi[1]], [qxb])
        if STOP <= 2:
            continue
        for n4 in range(0, NCH, 4):
            nn = min(4, NCH - n4)
            pk, pv = pT
            for j in range(nn):
                n = n4 + j
                k.tr(pk[:, j * 128:(j + 1) * 128], kT[:, n * 128:(n + 1) * 128], C("ident"), [kT, cst], [pk])
                k.tr(pv[:, j * 128:(j + 1) * 128], vT[:, n * 128:(n + 1) * 128], C("ident"), [vT, cst], [pv])
            k.ts("dve", knf[:, n4:n4 + nn, :], _v3(pk[:, 0:nn * 128]), zeta[:, 0:1], None, ALU.mult, ALU.bypass,
                 [pk, zeta], [knf])
            k.ts("dve", knb[:, n4:n4 + nn, :], _v3(pk[:, 0:nn * 128]), zeta[:, 1:2], None, ALU.mult, ALU.bypass,
                 [pk, zeta], [knb])
            k.cp("act", vn[:, n4:n4 + nn, :], _v3(pv[:, 0:nn * 128]), [pv], [vn])
        if STOP <= 3:
            continue
        for d, order in enumerate([list(range(NCH)), [1, 0] + list(range(NCH - 1, 1, -1))]):
            kn = knf if d == 0 else knb
            SP = SPf if d == 0 else SPb
            s_ = st[d]
            k.op("dve", lambda e, s_=s_: e.memset(s_[:], 0.0), [], [s_])
            for n in order:
                k.cp("act", SP[:, n, :], s_[:], [s_], [SP])
                pu = pU[ui % 2]
                ui += 1
                k.mm(pu[:, 0:128], kn[:, n, :], vn[:, n, :], True, True, [kn, vn], [pu])
                k.stt(s_[:], s_[:], gC[:, d:d + 1], pu[:, 0:128], ALU.mult, ALU.add, [s_, gC, pu], [s_])
        if STOP <= 4:
            continue
        for bi, (t0, tl) in enumerate(TBLK):
            if bi == 0 and not need_ctx:
                continue
            nn, n0 = tl // 128, t0 // 128
            for j in range(nn):
                n = n0 + j
                k.mm(pS[:, j * 128:(j + 1) * 128], kb_[:, n * 128:(n + 1) * 128], qb[:, n * 128:(n + 1) * 128], True, True,
                     [kb_, qb], [pS])
            k.tt("dve", SM[:, 0:nn, :], _v3(pS[:, 0:tl]), mask[:].unsqueeze(1).broadcast_to([128, nn, 128]), ALU.mult,
                 [pS, mask], [SM])
            for j in range(nn):
                n = n0 + j
                cs = slice(j * 128, (j + 1) * 128)
                k.mm(pO[:, cs], vn[:, n, :], SM[:, j, :], True, False, [vn, SM], [pO])
                k.mm(pO[:, cs], SPf[:, n, :], qxf[:, n * 128:(n + 1) * 128], False, False, [SPf, qxf], [pO])
                k.mm(pO[:, cs], SPb[:, n, :], qxb[:, n * 128:(n + 1) * 128], False, True, [SPb, qxb], [pO])
            d_ = group_norm_fm(k, env, pO, tl, C("onesm"), 1e-5, P(l, "retg", h), tmp + [pm, pq])
            m_ = mo[bi % 2]
            k.tt("pool", m_[:, 0:tl], d_[:, 0:tl], sg[:, t0:t0 + tl], ALU.mult, [d_, sg], [m_])
            k.dma("pool", MIXT[h * 128:(h + 1) * 128, t0:t0 + tl], m_[:, 0:tl], reads=[m_])
    k.pop()


def phase_out(k, l, xsrc, MIXT, wout, dst, env, last):
    C, P, cst, pp, mod, modc = env["C"], env["P"], env["cst"], env["pp"], env["mod"], env["modc"]
    k.push()
    wo = k.sb("wo", [128, 16, 2048], BF16)
    k.push()
    wst = [k.sb("wst", [128, 2048], F32) for _ in range(2)]
    for kc in range(16):
        w = wst[kc % 2]
        k.dma("sp", w[:], wout[l, kc * 128:(kc + 1) * 128, :], writes=[w])
        k.cp("pool" if kc % 2 else "act", wo[:, kc, :], w[:], [w], [wo])
    k.pop()
    mx = [k.sb("mx", [128, 16, 512], BF16) for _ in range(2)]
    xz = [k.sb("xz", [128, 16, 512], F32) for _ in range(2)]
    tt_ = [k.sb("tt", [128, 512], F32) for _ in range(2)]
    sq_ = [k.sb("sq", [128, 512], BF16) for _ in range(2)]
    zb_ = [k.sb("zb", [128, 512], BF16) for _ in range(2)]
    o2kb = k.sb("o2kb", [128, 128], BF16)
    k.cp("dve", o2kb[:], C("ones2k"), [cst], [o2kb])
    mean, rs, dd = [k.sb(n, [128, 512], F32) for n in ("mean", "rs", "dd")]
    dd2 = k.sb("dd2", [128, 512], F32)
    pz = [k.ps("pz") for _ in range(3)]
    pm_, pq_ = k.ps("pm"), k.ps("pq")
    blocks = TBLK[1:] if last else TBLK
    for bi, (t0, tl) in enumerate(blocks):
        m_, x_ = mx[bi % 2], xz[bi % 2]
        k.dma("sp", m_[:, :, 0:tl], MIXT[:, t0:t0 + tl].rearrange("(kc p) t -> p kc t", p=128), writes=[m_])
        k.dma("sp", x_[:, :, 0:tl], xsrc[:, t0:t0 + tl].rearrange("(kc p) t -> p kc t", p=128), writes=[x_])
        r = 1 if t0 < NCTX else 0
        for dc in range(16):
            p_ = pz[dc % 3]
            for kc in range(16):
                k.mm(p_[:, 0:tl], wo[:, kc, dc * 128:(dc + 1) * 128], m_[:, kc, 0:tl], kc == 0, kc == 15, [wo, m_], [p_])
            t_ = tt_[dc % 2]
            s_ = sq_[dc % 2]
            k.act(t_[:, 0:tl], p_[:, 0:tl], AF.Identity, [p_, mod], [t_], scale=modc(l, 32 + dc, r))
            k.stt(x_[:, dc, 0:tl], x_[:, dc, 0:tl], ALPHA, t_[:, 0:tl], ALU.mult, ALU.add, [x_, t_], [x_])
            k.act(s_[:, 0:tl], x_[:, dc, 0:tl], AF.Square, [x_], [s_])
            zb = zb_[dc % 2]
            k.cp("dve", zb[:, 0:tl], x_[:, dc, 0:tl], [x_], [zb])
            k.mm(pm_[:, 0:tl], o2kb[:], zb[:, 0:tl], dc == 0, dc == 15, [o2kb, zb], [pm_])
            k.mm(pq_[:, 0:tl], o2kb[:], s_[:, 0:tl], dc == 0, dc == 15, [o2kb, s_], [pq_])
        k.cp("act", mean[:, 0:tl], pm_[:, 0:tl], [pm_], [mean])
        k.act(rs[:, 0:tl], pm_[:, 0:tl], AF.Square, [pm_], [rs])
        k.stt(rs[:, 0:tl], rs[:, 0:tl], -1.0, pq_[:, 0:tl], ALU.mult, ALU.add, [rs, pq_], [rs])
        k.ts("dve", rs[:, 0:tl], rs[:, 0:tl], 0.0, 1e-5, ALU.max, ALU.add, [rs], [rs])
        k.act(rs[:, 0:tl], rs[:, 0:tl], AF.Sqrt, [rs], [rs])
        k.op("dve", lambda e, tl=tl: e.reciprocal(out=rs[:, 0:tl], in_=rs[:, 0:tl]), [rs], [rs])
        for dc in range(16):
            da = dd if dc % 2 == 0 else dd2
            k.tt("dve", da[:, 0:tl], x_[:, dc, 0:tl], mean[:, 0:tl], ALU.subtract, [x_, mean], [da])
            k.tt("dve", da[:, 0:tl], da[:, 0:tl], rs[:, 0:tl], ALU.mult, [da, rs], [da])
            k.act(x_[:, dc, 0:tl], da[:, 0:tl], AF.Identity, [da, pp], [x_], scale=P(l, "lng", dc), bias=P(l, "lnb", dc))
        o0 = t0 - NCTX if last else t0
        k.dma("pool", dst[:, o0:o0 + tl].rearrange("(kc p) t -> p kc t", p=128), x_[:, :, 0:tl], reads=[x_])
    k.pop()


RBLK = list(TBLK)


def phase_rwkv(k, l, PT, MIXT, w2p, a2p, env, need_ctx):
    C, P, cst, pp = env["C"], env["P"], env["cst"], env["pp"]
    k.push()
    rT, kT, vT, kkT, YT = [k.sb(n, [128, T], F32) for n in ("rT", "kT", "vT", "kkT", "YT")]
    wlT, alT = [k.sb(n, [128, T], BF16) for n in ("wlT", "alT")]
    w2t = k.sb("w2t", [128, 128], BF16)
    a2t = k.sb("a2t", [128, 128], BF16)
    w2f = k.sb("w2f", [128, 128], F32)
    a2f = k.sb("a2f", [128, 128], F32)
    muc = k.sb("muc", [128, 14], F32)
    o0, o1 = PP["mu0"][0] + l * NPP, PP["mu1"][0] + l * NPP
    k.tt("dve", muc[:], pp[:, o0:o0 + 14], pp[:, o1:o1 + 14], ALU.add, [pp], [muc])
    k.ts("dve", muc[:], muc[:], -1.0, 1.0, ALU.mult, ALU.add, [muc], [muc])
    M4 = [k.sb("M4", [128, 512], F32) for _ in range(2)]
    MS = [None, None]
    for d, (a, b) in enumerate((("sf", "mf2"), ("sb", "mb2"))):
        for q in range(2):
            k.cp("dve", M4[d][:, q * 256:q * 256 + 128], C(a), [cst], [M4[d]])
            k.cp("dve", M4[d][:, q * 256 + 128:q * 256 + 256], C(b), [cst], [M4[d]])
    MS[0], MS[1] = C("sb"), C("sf")
    raw = YT

    def load_shift(c, dst):
        r0 = (32 + c) * 128
        k.dma("sp", raw[:, 0:2176], PT[r0:r0 + 128, 0:2176], writes=[raw])
        k.dma("sp", raw[:, 2176:T], PT[r0:r0 + 128, 2176:T], writes=[raw])
        k.act(dst[:], raw[:], AF.Identity, [raw, muc], [dst], scale=muc[:, c:c + 1])
        for (a, b) in ((0, NCTX), (NCTX, T)):
            k.stt(dst[:, a + 1:b], raw[:, a:b - 1], P(l, "mu0", c), dst[:, a + 1:b], ALU.mult, ALU.add, [raw, dst, pp], [dst])
            k.stt(dst[:, a:b - 1], raw[:, a + 1:b], P(l, "mu1", c), dst[:, a:b - 1], ALU.mult, ALU.add, [raw, dst, pp], [dst])

    load_shift(12, wlT)
    k.act(wlT[:], wlT[:], AF.Tanh, [wlT], [wlT])
    load_shift(13, alT)
    G = {n: k.sb(n, [128, 512], F32) for n in ("sgm", "ar", "ld", "cI", "cX", "E1", "E2", "E3", "E4", "dl", "ke", "b",
                                               "Bt", "Bh", "Kt", "Kh", "g1", "g2", "g3", "g4")}
    AR = k.sb("AR", [128, 1024], F32)
    WC = k.sb("WC", [128, 4], F32)
    NJ = 4
    J = []
    for _ in range(NJ):
        J.append(dict(
            XBK=k.sb("XBK", [128, 512], F32), A_=k.sb("A_", [128, 128], BF16), TM=k.sb("TM", [128, 192], F32),
            ATb=k.sb("ATb", [128, 128], BF16), TTf=k.sb("TTf", [128, 128], F32),
            Y0=k.sb("Y0", [128, 128], F32), T_=[k.sb("T_", [128, 128], BF16) for _ in range(2)],
            TT_=[k.sb("TT_", [128, 128], BF16) for _ in range(2)], Wm=k.sb("Wm", [128, 128], BF16),
            Wn=k.sb("Wn", [128, 128], BF16), X_=k.sb("X_", [128, 128], F32), Q1T=k.sb("Q1T", [128, 128], F32),
            GmT=k.sb("GmT", [128, 64], F32), b1=k.ps("b1"), b2=k.ps("b2")))
    Hss = [[k.sb("Hs", [128, 64], F32) for _ in range(2)] for _ in range(2)]
    YTr = [Res("yt0"), Res("yt1")]
    mo = [k.sb("mo", [128, 512], MIXT.dtype) for _ in range(2)]
    pz, pa = J[0]["b1"], J[1]["b1"]
    ident = C("ident")
    oLM, oLMT = CS["LM"][0], CS["LMT"][0]

    def chunk_head(job, hh, j, t0, cur, delay):
        XBK, A_, TM, Y0, T_, TT_, Wm, Wn, X_, Q1T, GmT, b1, b2 = (job[n] for n in (
            "XBK", "A_", "TM", "Y0", "T_", "TT_", "Wm", "Wn", "X_", "Q1T", "GmT", "b1", "b2"))
        ytr = YTr[hh]
        Hs = Hss[hh]
        ATb, TTf = job["ATb"], job["TTf"]
        d = cur_d[0]
        mT = (lambda q: cst[:, oLM + q * 128:oLM + (q + 1) * 128]) if d == 0 else \
             (lambda q: cst[:, oLMT + q * 128:oLMT + (q + 1) * 128])
        mTT = (lambda q: cst[:, oLMT + q * 128:oLMT + (q + 1) * 128]) if d == 0 else \
              (lambda q: cst[:, oLM + q * 128:oLM + (q + 1) * 128])
        cs = slice(j * 128, (j + 1) * 128)
        tk_ = slice(t0 + j * 128, t0 + (j + 1) * 128)
        pb = hh * 64
        ps_ = slice(pb, pb + 64)
        At = AR[ps_, j * 256:j * 256 + 128]
        Rt = AR[ps_, j * 256 + 128:j * 256 + 256]
        ARj = AR[ps_, j * 256:(j + 1) * 256]
        idn = ident[ps_, pb:pb + 64]
        k.mm(b1[:, 0:256], G["Bt"][ps_, cs], ARj, True, True, [G["Bt"], AR], [b1])
        k.mm(b1[:, 256:512], G["Kt"][ps_, cs], ARj, True, True, [G["Kt"], AR], [b1])
        k.mm(b2[:, 0:128], At, G["Bt"][ps_, cs], True, True, [AR, G["Bt"]], [b2])
        k.tr(b2[:, 128:192], G["Bh"][ps_, cs], idn, [G["Bh"], cst], [b2])
        k.tr(b2[:, 192:256], G["Kh"][ps_, cs], idn, [G["Kh"], cst], [b2])
        k.tr(b2[:, 256:320], vT[ps_, tk_], idn, [vT, cst], [b2])
        k.tr(b2[:, 320:384], At, idn, [AR, cst], [b2])
        yield
        k.tt("dve", XBK[:], b1[:], M4[d][:], ALU.mult, [b1, M4[d]], [XBK])
        k.cp("act", TM[:], b2[:, 128:320], [b2], [TM])
        yield
        k.tt("dve", A_[:], b2[:, 0:128], MS[d], ALU.mult, [b2, cst], [A_])
        k.mm(b2[:, 384:448], XBK[:, 256:384], TM[:, 128:192], True, True, [XBK, TM], [b2])
        yield
        k.tt("pool", Wm[:], A_[:], mT(0), ALU.mult, [A_, cst], [Wm])
        k.tt("pool", Wn[:], XBK[:, 0:128], mTT(0), ALU.mult, [XBK, cst], [Wn])
        k.cp("pool", ATb[:], XBK[:, 0:128], [XBK], [ATb])
        k.cp("act", Y0[:], b2[:, 320:448], [b2], [Y0])
        yield
        k.tt("pool", T_[0][:], Wm[:], ident, ALU.add, [Wm, cst], [T_[0]])
        k.tt("pool", TT_[0][:], Wn[:], ident, ALU.add, [Wn, cst], [TT_[0]])
        yield
        c_ = 0
        for q in range(1, 7):
            lastq = (q == 6)
            if not lastq:
                k.mm(b1[:, 0:128], ATb[:], T_[c_][:], True, True, [ATb, T_[c_]], [b1])
            k.mm(b1[:, 128:256], A_[:], TT_[c_][:], True, True, [A_, TT_[c_]], [b1])
            yield
            if not lastq:
                k.tt("dve", Wm[:], b1[:, 0:128], mT(q), ALU.mult, [b1, cst], [Wm])
            k.tt("dve", Wn[:], b1[:, 128:256], mTT(q), ALU.mult, [b1, cst], [Wn])
            yield
            if not lastq:
                k.mm(b1[:, 256:384], TT_[c_][:], Wm[:], True, True, [TT_[c_], Wm], [b1])
            k.mm(b1[:, 384:512], T_[c_][:], Wn[:], True, True, [T_[c_], Wn], [b1])
            yield
            if not lastq:
                k.tt("dve", T_[1 - c_][:], T_[c_][:], b1[:, 256:384], ALU.add, [T_[c_], b1], [T_[1 - c_]])
            if not lastq:
                k.tt("dve", TT_[1 - c_][:], TT_[c_][:], b1[:, 384:512], ALU.add, [TT_[c_], b1], [TT_[1 - c_]])
            else:
                k.tt("dve", TTf[:], TT_[c_][:], b1[:, 384:512], ALU.add, [TT_[c_], b1], [TTf])
            yield
            c_ = 1 - c_
        k.mm(b1[:, 0:128], TTf[:], Y0[:], True, True, [TTf, Y0], [b1])
        yield
        k.cp("act", X_[:], b1[:, 0:128], [b1], [X_])
        yield
        P1, P2 = X_[:, 0:64], X_[:, 64:128]
        Bh_, Kh_, V_ = TM[:, 0:64], TM[:, 64:128], TM[:, 128:192]
        k.mm(b2[ps_, 0:128], P1, XBK[:, 128:256], True, True, [X_, XBK], [b2])
        k.mm(b2[ps_, 128:192], P1, Bh_, True, True, [X_, TM], [b2])
        yield
        k.tt("dve", Q1T[ps_, :], b2[ps_, 0:128], Rt, ALU.add, [b2, AR], [Q1T])
        k.stt(GmT[ps_, :], ident[ps_, pb:pb + 64], WC[ps_, j:j + 1], b2[ps_, 128:192], ALU.mult, ALU.add,
              [cst, WC, b2], [GmT])
        yield
        Ho, Hn = Hs[cur], Hs[1 - cur]
        for _ in range(delay):
            yield
        k.mm(b2[ps_, 192:256], Bh_, P2, True, False, [TM, X_], [b2])
        k.mm(b2[ps_, 192:256], Kh_, V_, False, False, [TM], [b2])
        k.mm(b2[ps_, 192:256], GmT[ps_, :], Ho[ps_, :], False, True, [GmT, Ho], [b2])
        k.mm(b2[ps_, 256:384], P2, XBK[:, 128:256], True, False, [X_, XBK], [b2])
        k.mm(b2[ps_, 256:384], V_, XBK[:, 384:512], False, False, [TM, XBK], [b2])
        k.mm(b2[ps_, 256:384], Ho[ps_, :], Q1T[ps_, :], False, True, [Ho, Q1T], [b2])
        yield
        k.cp("act", Hn[ps_, :], b2[ps_, 192:256], [b2], [Hn])
        if d == 0:
            k.cp("dve", YT[ps_, tk_], b2[ps_, 256:384], [b2], [ytr])
        else:
            k.tt("dve", YT[ps_, tk_], YT[ps_, tk_], b2[ps_, 256:384], ALU.add, [ytr, b2], [ytr])
        yield

    cur_d = [0]
    import os
    STAG = int(os.environ.get("RW_STAG", "1"))
    for hp in range(4):
        load_shift(hp, rT)
        load_shift(4 + hp, kT)
        load_shift(8 + hp, vT)
        k.op("dve", lambda e: e.memset(YT[:, 0:1], 0.0), [], [YT] + YTr)
        for (t0, tl) in RBLK:
            g1, g2 = G["g1"], G["g2"]
            k.ts("dve", kkT[:, t0:t0 + tl], kT[:, t0:t0 + tl], P(l, "kk", hp), None, ALU.mult, ALU.bypass, [kT, pp], [kkT])
            k.act(g1[:, 0:tl], kkT[:, t0:t0 + tl], AF.Square, [kkT], [g1])
            k.mm(pz[:, 0:tl], C("blk64"), g1[:, 0:tl], True, True, [cst, g1], [pz])
            k.act(g2[:, 0:tl], pz[:, 0:tl], AF.Sqrt, [pz], [g2])
            k.ts("dve", g2[:, 0:tl], g2[:, 0:tl], 1e-12, None, ALU.max, ALU.bypass, [g2], [g2])
            k.op("dve", lambda e, tl=tl: e.reciprocal(out=g2[:, 0:tl], in_=g2[:, 0:tl]), [g2], [g2])
            k.tt("dve", kkT[:, t0:t0 + tl], kkT[:, t0:t0 + tl], g2[:, 0:tl], ALU.mult, [kkT, g2], [kkT])
        for d in range(2):
            blocks = list(RBLK) if d == 0 else [RBLK[0]] + list(reversed(RBLK[1:]))
            cur = 0
            cur_d[0] = d
            k.dma("sp", w2f[:], w2p[l, d, :, hp * 128:(hp + 1) * 128], writes=[w2f])
            k.dma("sp", a2f[:], a2p[l, d, :, hp * 128:(hp + 1) * 128], writes=[a2f])
            k.cp("dve", w2t[:], w2f[:], [w2f], [w2t])
            k.cp("dve", a2t[:], a2f[:], [a2f], [a2t])
            for hh_ in range(2):
                k.op("dve", lambda e, hh_=hh_: e.memset(Hss[hh_][0][:], 0.0), [], [Hss[hh_][0]])
            for (t0, tl) in blocks:
                nn = tl // 128
                sl = slice(t0, t0 + tl)
                w = slice(0, tl)
                k.mm(pz[:, w], w2t[:], wlT[:, sl], True, True, [w2t, wlT], [pz])
                k.mm(pa[:, w], a2t[:], alT[:, sl], True, True, [a2t, alT], [pa])
                k.act(G["sgm"][:, w], pz[:, w], AF.Sigmoid, [pz, pp], [G["sgm"]], bias=P(l, "w0", d * 4 + hp))
                k.act(G["ar"][:, w], pa[:, w], AF.Sigmoid, [pa, pp], [G["ar"]], bias=P(l, "a0", d * 4 + hp))
                k.ts("dve", G["ld"][:, w], G["sgm"][:, w], -0.6065306597126334, None, ALU.mult, ALU.bypass,
                     [G["sgm"]], [G["ld"]])
                o, c0_, rm = CS["rm"][0], None, None
                k.op("dve", lambda e, w=w: e.tensor_tensor_scan(out=G["cI"][:, w], data0=env["cst"][:, o:o + w.stop],
                                                               data1=G["ld"][:, w], initial=0.0, op0=ALU.mult,
                                                               op1=ALU.add), [cst, G["ld"]], [G["cI"]])
                cI3 = _v3(G["cI"][:, w])
                if d == 1:
                    k.tt("dve", G["g1"][:, w], G["ld"][:, w], G["cI"][:, w], ALU.subtract, [G["ld"], G["cI"]], [G["g1"]])
                    k.tt("dve", _v3(G["g2"][:, w]), _v3(G["g1"][:, w]), cI3[:, :, 127:128].broadcast_to([128, nn, 128]),
                         ALU.add, [G["g1"], G["cI"]], [G["g2"]])
                    k.cp("dve", G["cI"][:, w], G["g2"][:, w], [G["g2"]], [G["cI"]])
                    tot = cI3[:, :, 0:1]
                else:
                    tot = cI3[:, :, 127:128]
                k.tt("dve", G["cX"][:, w], G["cI"][:, w], G["ld"][:, w], ALU.subtract, [G["cI"], G["ld"]], [G["cX"]])
                k.act(WC[:, 0:nn].unsqueeze(2), tot, AF.Exp, [G["cI"]], [WC])
                k.act(G["E1"][:, w], G["cI"][:, w], AF.Exp, [G["cI"]], [G["E1"]])
                k.act(G["E2"][:, w], G["cI"][:, w], AF.Exp, [G["cI"]], [G["E2"]], scale=-1.0)
                k.act(G["E3"][:, w], G["cX"][:, w], AF.Exp, [G["cX"]], [G["E3"]])
                k.tt("dve", _v3(G["dl"][:, w]), tot.broadcast_to([128, nn, 128]), cI3, ALU.subtract, [G["cI"]], [G["dl"]])
                k.act(G["E4"][:, w], G["dl"][:, w], AF.Exp, [G["dl"]], [G["E4"]])
                k.ts("dve", G["g3"][:, w], G["ar"][:, w], -1.0, P(l, "ka", hp), ALU.add, ALU.mult, [G["ar"], pp], [G["g3"]])
                k.stt(G["ke"][:, w], G["g3"][:, w], 1.0, kT[:, sl], ALU.add, ALU.mult, [G["g3"], kT], [G["ke"]])
                ARv = AR[:, 0:nn * 256].rearrange("p (n a c) -> p n a c", a=2, c=128)
                k.stt(ARv[:, :, 0, :], _v3(kkT[:, sl]), -1.0, _v3(G["E3"][:, w]), ALU.mult, ALU.mult, [kkT, G["E3"]], [AR])
                k.tt("dve", ARv[:, :, 1, :], _v3(rT[:, sl]), _v3(G["E1"][:, w]), ALU.mult, [rT, G["E1"]], [AR])
                k.tt("pool", G["b"][:, w], kkT[:, sl], G["ar"][:, w], ALU.mult, [kkT, G["ar"]], [G["b"]])
                k.tt("dve", G["Bt"][:, w], G["b"][:, w], G["E2"][:, w], ALU.mult, [G["b"], G["E2"]], [G["Bt"]])
                k.tt("pool", G["Bh"][:, w], G["b"][:, w], G["E4"][:, w], ALU.mult, [G["b"], G["E4"]], [G["Bh"]])
                k.tt("dve", G["Kt"][:, w], G["ke"][:, w], G["E2"][:, w], ALU.mult, [G["ke"], G["E2"]], [G["Kt"]])
                k.tt("pool", G["Kh"][:, w], G["ke"][:, w], G["E4"][:, w], ALU.mult, [G["ke"], G["E4"]], [G["Kh"]])
                jorder = list(range(nn)) if d == 0 else list(reversed(range(nn)))
                todo = []
                for ji, j in enumerate(jorder):
                    for hh in range(2):
                        todo.append((len(todo), hh, j, cur))
                    cur = 1 - cur
                active = []
                rnd = 0
                nxt = 0
                last_start = -10
                while nxt < len(todo) or active:
                    if nxt < len(todo) and len(active) < NJ and rnd - last_start >= STAG:
                        idx, hh, j, cu = todo[nxt]
                        active.append(chunk_head(J[idx % NJ], hh, j, t0, cu, 0))
                        nxt += 1
                        last_start = rnd
                    for g_ in list(active):
                        try:
                            next(g_)
                        except StopIteration:
                            active.remove(g_)
                    rnd += 1
        for bi, (t0, tl) in enumerate(RBLK):
            if bi == 0 and not need_ctx:
                continue
            sl = slice(t0, t0 + tl)
            w = slice(0, tl)
            y = YT
            g_ = G["g4"]
            k.dma("sp", g_[:, w], PT[(46 + hp) * 128:(47 + hp) * 128, sl], writes=[g_])
            k.act(G["g1"][:, w], YT[:, sl], AF.Square, [YT] + YTr, [G["g1"]])
            k.mm(pz[:, w], C("blk64m"), YT[:, sl], True, True, [cst, YT] + YTr, [pz])
            k.mm(pa[:, w], C("blk64m"), G["g1"][:, w], True, True, [cst, G["g1"]], [pa])
            k.tt("dve", G["g2"][:, w], YT[:, sl], pz[:, w], ALU.subtract, [YT, pz] + YTr, [G["g2"]])
            k.act(G["g3"][:, w], pz[:, w], AF.Square, [pz], [G["g3"]])
            k.stt(G["g3"][:, w], G["g3"][:, w], -1.0, pa[:, w], ALU.mult, ALU.add, [G["g3"], pa], [G["g3"]])
            k.ts("dve", G["g3"][:, w], G["g3"][:, w], 0.0, 64e-5, ALU.max, ALU.add, [G["g3"]], [G["g3"]])
            k.act(G["g3"][:, w], G["g3"][:, w], AF.Sqrt, [G["g3"]], [G["g3"]])
            k.op("dve", lambda e, w=w: e.reciprocal(out=G["g3"][:, w], in_=G["g3"][:, w]), [G["g3"]], [G["g3"]])
            k.stt(G["g2"][:, w], G["g2"][:, w], P(l, "rwg", hp), G["g3"][:, w], ALU.mult, ALU.mult,
                  [G["g2"], G["g3"], pp], [G["g2"]])
            k.stt(G["g1"][:, w], rT[:, sl], P(l, "rk", hp), kT[:, sl], ALU.mult, ALU.mult, [rT, kT, pp], [G["g1"]])
            k.mm(pz[:, w], C("blk64"), G["g1"][:, w], True, True, [cst, G["g1"]], [pz])
            k.tt("dve", G["g1"][:, w], pz[:, w], vT[:, sl], ALU.mult, [pz, vT], [G["g1"]])
            k.tt("pool", G["g2"][:, w], G["g2"][:, w], G["g1"][:, w], ALU.add, [G["g2"], G["g1"]], [G["g2"]])
            m_ = mo[bi % 2]
            k.tt("pool", m_[:, w], G["g2"][:, w], g_[:, w], ALU.mult, [G["g2"], g_], [m_])
            k.dma("pool", MIXT[1536 + hp * 128:1536 + (hp + 1) * 128, sl], m_[:, w], reads=[m_])
    k.pop()


def half_start(half):
    return 0 if half == 0 else 2304


def half_len(half):
    return 2304 if half == 0 else T - 2304


def _in_cols():
    idx = list(range(0, 2048))
    idx += list(range(2048, 2560))
    idx += list(range(2560, 2816))
    kr = list(range(2816, 2880))
    idx += kr + kr
    krp = [2816 + (i ^ 16) for i in range(64)]
    idx += krp + krp
    idx += list(range(2880, 3904))
    idx += list(range(3904, 5696))
    idx += list(range(5696, 6208))
    assert len(idx) == NCOLS
    return np.asarray(idx)


def _uq_cols():
    idx = []
    for h in range(8):
        b = h * 192
        idx += list(range(b, b + 128))
        idx += list(range(b + 128, b + 192))
        idx += [b + 128 + (i ^ 16) for i in range(64)]
    return np.asarray(idx)


def _rope_tables():
    inv = (10000.0 ** (-np.arange(16, dtype=np.float32) / 16)).astype(np.float32)
    t = np.arange(NLAT)
    row = (t // 64).astype(np.float32)
    col = (t % 64).astype(np.float32)
    cos = np.ones((64, T), np.float32)
    sin = np.zeros((64, T), np.float32)
    for i in range(64):
        pos = row if i < 32 else col
        ang = (pos * inv[i % 16]).astype(np.float32)
        sgn = -1.0 if (i % 32) < 16 else 1.0
        cos[i, NCTX:] = np.cos(ang)
        sin[i, NCTX:] = sgn * np.sin(ang)
    tq = np.concatenate([cos, sin], 0)
    return tq, np.concatenate([cos, cos], 0), np.concatenate([sin, sin], 0)


def _consts():
    c = np.zeros((128, NCS), np.float32)
    i = np.arange(128)
    jj, ii = np.meshgrid(i, i, indexing="ij")

    def put(n, a):
        o, w = CS[n]
        c[:, o:o + w] = a
    put("ident", np.eye(128))
    put("ones", np.ones((128, 128)))
    put("onesm", np.full((128, 128), 1.0 / 128))
    put("relf", np.maximum(ii - jj, 0))
    put("relb", np.maximum(jj - ii, 0))
    put("mf", (ii >= jj).astype(np.float32))
    put("mb", (jj >= ii).astype(np.float32))
    put("ip1", np.broadcast_to(i[None, :] + 1.0, (128, 128)))
    put("cmi", np.broadcast_to(128.0 - i[None, :], (128, 128)))
    put("blk64", np.kron(np.eye(2), np.ones((64, 64))))
    put("blk64m", np.kron(np.eye(2), np.ones((64, 64))) / 64.0)
    put("ones2k", np.full((128, 128), 1.0 / 2048))
    put("sf", (ii > jj).astype(np.float32))
    put("mf2", (ii >= jj).astype(np.float32))
    put("sb", (jj > ii).astype(np.float32))
    put("mb2", (jj >= ii).astype(np.float32))
    rm = np.ones((128, 512), np.float32)
    rm[:, ::128] = 0.0
    put("rm", rm)
    lm = np.zeros((128, 7 * 128), np.float32)
    lmt = np.zeros((128, 7 * 128), np.float32)
    for kk in range(7):
        mk = (((jj >> (kk + 1)) == (ii >> (kk + 1))) & (((jj >> kk) & 1) == 1) & (((ii >> kk) & 1) == 0)).astype(np.float32)
        lm[:, kk * 128:(kk + 1) * 128] = mk
        lmt[:, kk * 128:(kk + 1) * 128] = mk.T
    put("LM", lm)
    put("LMT", lmt)
    put("colj", i[:, None].astype(np.float32))
    put("colcj", (127.0 - i)[:, None])
    return c


def _chunks(v, n):
    return np.ascontiguousarray(np.asarray(v, np.float32).reshape(n, 128).T)


def prep_inputs(inp, b):
    f = lambda a: np.ascontiguousarray(a, dtype=np.float32)
    m = {}
    xt = np.concatenate([inp["ctx"][b], inp["x"][b]], 0)
    m["xT"] = f(xt.T)
    m["cc"] = f(np.stack([_chunks(inp["c"][b], 16), _chunks(inp["c_ctx"], 16)], -1).reshape(128, 32))
    m["wada"] = f(inp["w_ada"])
    m["bada"] = f(np.concatenate([_chunks(inp["b_ada"][l], 48) for l in range(2)], 1))
    m["win"] = f(inp["w_in"][:, :, _in_cols()])
    m["wuq"] = f(inp["mla_w_uq"][:, :, _uq_cols()])
    m["wukv"] = f(inp["mla_w_ukv"])
    m["wout"] = f(inp["w_out"])
    pp = np.zeros((128, 2 * NPP), np.float32)
    for l in range(2):
        def put(n, a):
            o, w = PP[n]
            pp[:, l * NPP + o: l * NPP + o + w] = a
        put("gq", _chunks(inp["mla_q_norm_g"][l], 4))
        put("gkv", _chunks(inp["mla_kv_norm_g"][l], 2))
        put("retg", _chunks(inp["ret_gn_g"][l], 4))
        put("lng", _chunks(inp["ln_g"][l], 16))
        put("lnb", _chunks(inp["ln_b"][l], 16))
        put("mu0", _chunks(inp["rwkv_shift_mu"][l, 0], 14))
        put("mu1", _chunks(inp["rwkv_shift_mu"][l, 1], 14))
        put("w0", np.concatenate([_chunks(inp["rwkv_w0"][l, d], 4) for d in range(2)], 1))
        put("a0", np.concatenate([_chunks(inp["rwkv_a0"][l, d], 4) for d in range(2)], 1))
        put("kk", _chunks(inp["rwkv_k_k"][l], 4))
        put("ka", _chunks(inp["rwkv_k_a"][l], 4))
        put("rk", _chunks(inp["rwkv_r_k"][l].reshape(-1), 4))
        put("rwg", _chunks(inp["rwkv_gn_g"][l], 4))
    m["pp"] = pp
    w2p = np.zeros((2, 2, 128, 512), np.float32)
    a2p = np.zeros((2, 2, 128, 512), np.float32)
    for l in range(2):
        for d in range(2):
            w2p[l, d, d * 64:(d + 1) * 64] = inp["rwkv_w2"][l, d]
            a2p[l, d, d * 64:(d + 1) * 64] = inp["rwkv_a2"][l, d]
    m["w2p"], m["a2p"] = w2p, a2p
    m["rdl"] = f(np.broadcast_to(inp["ret_decay_logit"].reshape(1, 16), (128, 16)))
    tq, tkc, tks = _rope_tables()
    m["tabQ"], m["tabKc"], m["tabKs"] = f(tq), f(tkc), f(tks)
    m["cst"] = _consts()
    return m


def kernel(**inputs):
    inp = {k_: np.asarray(v) for k_, v in inputs.items()}
    nc = build_program()
    in_maps = [prep_inputs(inp, c % 4) for c in range(4)]
    in_maps = in_maps + in_maps
    res = run_bass_kernel_spmd(nc, in_maps, core_ids=list(range(8)))
    out = np.stack([np.ascontiguousarray(res.results[b]["yT"].T) for b in range(4)], 0)
    return out.astype(np.float32)
```
